# Optimizing a Trainium2 kernel written in Bass

```python
import math
import jax
import jax.numpy as jnp
from jax import lax
import numpy as np

D_MODEL = 1024
BATCH = 8
SEQ = 2048
DEPTH = 2
DEC_BATCH = 128
DEC_SEQ = 4
PAST_LEN = 16384
PAGE_SIZE = 128

HEAD_DIM = 64
MIX_WIDTH = D_MODEL
GROUP_WIDTH = MIX_WIDTH // 4
H_A = GROUP_WIDTH // HEAD_DIM
DK_A = HEAD_DIM
DV_A = HEAD_DIM
H_B = GROUP_WIDTH // HEAD_DIM
P_B = HEAD_DIM
N_B = 64
G_B = 2
H_C = GROUP_WIDTH // HEAD_DIM
DK_C = HEAD_DIM
DV_C = HEAD_DIM
H_D = GROUP_WIDTH // HEAD_DIM
DK_D = HEAD_DIM
DV_D = HEAD_DIM

CONV_K = 4
CHUNK = 64
D_FF = 4 * D_MODEL
EPS = 1e-6
NEG = -1e30
DEEPNORM_ALPHA = (2 * DEPTH) ** 0.25
DEEPNORM_BETA = (8 * DEPTH) ** -0.25

QK_A = H_A * DK_A
W_A = H_A * DV_A
CONV_A_CH = 2 * QK_A + W_A
W_B = H_B * P_B
CONV_B_CH = W_B + 2 * G_B * N_B
QK_C = H_C * DK_C
W_C = H_C * DV_C
QK_D = H_D * DK_D
W_D = H_D * DV_D
MIX_OUT = W_A + W_B + W_C + W_D
IN_SIZES = (CONV_A_CH, W_A, H_A, H_A,
            W_B, CONV_B_CH, H_B,
            QK_C, QK_C, W_C, W_C,
            QK_D, QK_D, W_D, W_D, H_D, H_D)
IN_DIM = sum(IN_SIZES)

kernel_name = 'hybrid_deltanet_ssd_hgrn2_mlstm_decode_step'


def _split(u, sizes):
    offs = np.cumsum(sizes)[:-1].tolist()
    return jnp.split(u, offs, axis=-1)


def _layernorm(x, g, b):
    x32 = x.astype(jnp.float32)
    mu = jnp.mean(x32, axis=-1, keepdims=True)
    xc = x32 - mu
    var = jnp.mean(xc * xc, axis=-1, keepdims=True)
    return (xc * lax.rsqrt(var + EPS) * g + b).astype(x.dtype)


def _rms_heads(x, g):
    y = x * lax.rsqrt(jnp.mean(x * x, axis=-1, keepdims=True) + EPS)
    return y.reshape(x.shape[:-2] + (-1,)) * g


def _l2norm(x):
    return x * lax.rsqrt(jnp.sum(x * x, axis=-1, keepdims=True) + EPS)


def _masked_exp(diff, mask):
    return jnp.where(mask, jnp.exp(jnp.where(mask, diff, 0.0)), 0.0)


def _causal_conv(x, buf, w, b):
    t = x.shape[1]
    xx = jnp.concatenate([buf.astype(x.dtype), x], axis=1)
    y = b + xx[:, 0:t] * w[0]
    for j in range(1, CONV_K):
        y = y + xx[:, j:j + t] * w[j]
    return y, xx[:, t:]


def _chunks(x, L):
    b, t, h = x.shape[:3]
    x = x.reshape((b, t // L, L, h) + x.shape[3:])
    return jnp.moveaxis(x, (1, 3), (0, 2))


def _unchunk(y):
    y = jnp.moveaxis(y, (0, 2), (1, 3))
    return y.reshape((y.shape[0], -1) + y.shape[3:])


def _masks(L):
    idx = jnp.arange(L)
    return idx[:, None] >= idx[None, :], idx[:, None] > idx[None, :]


def _gated_delta(q, k, v, log_a, beta, s0):
    L = math.gcd(q.shape[1], CHUNK)
    incl, strict = _masks(L)
    eye = jnp.eye(L, dtype=jnp.float32)
    dv = v.shape[-1]

    def step(s, inp):
        qc, kc, vc, gc, bc = inp
        g = jnp.cumsum(gc, axis=-1)
        decay = _masked_exp(g[..., :, None] - g[..., None, :], incl)
        kk = jnp.einsum('bhrd,bhjd->bhrj', kc, kc)
        m = jnp.where(strict, bc[..., :, None] * kk * decay, 0.0) + eye
        rhs = jnp.concatenate([bc[..., None] * vc, (bc * jnp.exp(g))[..., None] * kc], axis=-1)
        sol = lax.linalg.triangular_solve(m, rhs, left_side=True, lower=True, unit_diagonal=True)
        u = sol[..., :dv] - jnp.einsum('bhlk,bhkv->bhlv', sol[..., dv:], s)
        qk = jnp.einsum('bhrd,bhjd->bhrj', qc, kc) * decay
        o = (jnp.exp(g)[..., None] * jnp.einsum('bhlk,bhkv->bhlv', qc, s)
             + jnp.einsum('bhrj,bhjv->bhrv', qk, u))
        g_last = g[..., -1:]
        s_new = (jnp.exp(g_last)[..., None] * s
                 + jnp.einsum('bhjk,bhjv->bhkv', kc * jnp.exp(g_last - g)[..., None], u))
        return s_new, o

    xs = (_chunks(q, L), _chunks(k, L), _chunks(v, L), _chunks(log_a, L), _chunks(beta, L))
    s_fin, o = lax.scan(step, s0, xs)
    return _unchunk(o), s_fin


def _scalar_decay_attn(q, k, v, log_a, s0):
    L = math.gcd(q.shape[1], CHUNK)
    incl, _ = _masks(L)

    def step(s, inp):
        qc, kc, vc, gc = inp
        g = jnp.cumsum(gc, axis=-1)
        decay = _masked_exp(g[..., :, None] - g[..., None, :], incl)
        att = jnp.einsum('bhrd,bhjd->bhrj', qc, kc) * decay
        o = (jnp.exp(g)[..., None] * jnp.einsum('bhlk,bhkv->bhlv', qc, s)
             + jnp.einsum('bhrj,bhjv->bhrv', att, vc))
        g_last = g[..., -1:]
        s_new = (jnp.exp(g_last)[..., None] * s
                 + jnp.einsum('bhjk,bhjv->bhkv', kc * jnp.exp(g_last - g)[..., None], vc))
        return s_new, o

    xs = (_chunks(q, L), _chunks(k, L), _chunks(v, L), _chunks(log_a, L))
    s_fin, o = lax.scan(step, s0, xs)
    return _unchunk(o), s_fin


def _vector_decay_attn(q, k, v, log_f, s0):
    L = math.gcd(q.shape[1], CHUNK)
    incl, _ = _masks(L)
    mask3 = incl[:, :, None]

    def step(s, inp):
        qc, kc, vc, fc = inp
        g = jnp.cumsum(fc, axis=2)
        w = _masked_exp(g[:, :, :, None, :] - g[:, :, None, :, :], mask3)
        att = jnp.einsum('bhrc,bhjc,bhrjc->bhrj', qc, kc, w)
        o = (jnp.einsum('bhlc,bhcv->bhlv', qc * jnp.exp(g), s)
             + jnp.einsum('bhrj,bhjv->bhrv', att, vc))
        g_last = g[:, :, -1:, :]
        s_new = (jnp.exp(g[:, :, -1, :])[..., None] * s
                 + jnp.einsum('bhjc,bhjv->bhcv', kc * jnp.exp(g_last - g), vc))
        return s_new, o

    xs = (_chunks(q, L), _chunks(k, L), _chunks(v, L), _chunks(log_f, L))
    s_fin, o = lax.scan(step, s0, xs)
    return _unchunk(o), s_fin


def _mlstm(q, k, v, i_pre, log_f, c0, n0, m0):
    L = math.gcd(q.shape[1], CHUNK)
    incl, _ = _masks(L)

    def step(carry, inp):
        c, n, m = carry
        qc, kc, vc, ic, fc = inp
        b = jnp.cumsum(fc, axis=-1)
        d = jnp.where(incl, b[..., :, None] - b[..., None, :] + ic[..., None, :], NEG)
        init = b + m[..., None]
        m_r = jnp.maximum(init, jnp.max(d, axis=-1))
        p = _masked_exp(d - m_r[..., None], incl) * jnp.einsum('bhrd,bhjd->bhrj', qc, kc)
        s_init = jnp.exp(init - m_r)
        num = (s_init[..., None] * jnp.einsum('bhlk,bhkv->bhlv', qc, c)
               + jnp.einsum('bhrj,bhjv->bhrv', p, vc))
        den = s_init * jnp.einsum('bhlk,bhk->bhl', qc, n) + jnp.sum(p, axis=-1)
        h = num / jnp.maximum(jnp.abs(den), jnp.exp(-m_r))[..., None]
        m_last = m_r[..., -1]
        b_last = b[..., -1]
        sc = jnp.exp(b_last + m - m_last)
        kw = kc * jnp.exp(b_last[..., None] - b + ic - m_last[..., None])[..., None]
        c_new = sc[..., None, None] * c + jnp.einsum('bhjk,bhjv->bhkv', kw, vc)
        n_new = sc[..., None] * n + jnp.sum(kw, axis=2)
        return (c_new, n_new, m_last), h

    xs = (_chunks(q, L), _chunks(k, L), _chunks(v, L), _chunks(i_pre, L), _chunks(log_f, L))
    (c_f, n_f, m_f), h = lax.scan(step, (c0, n0, m0), xs)
    return _unchunk(h), c_f, n_f, m_f


def _layer(x, st, lb, w_in, conv_a_w, conv_a_b, a_log_a, dt_bias_a, norm_a_g,
           conv_b_w, conv_b_b, a_log_b, dt_bias_b, d_skip_b, norm_b_g, norm_c_g,
           i_bias_d, f_bias_d, norm_d_g, w_out, ln1_g, ln1_b, w_up, w_down, ln2_g, ln2_b):
    f32 = jnp.float32
    a_conv, a_s, b_conv, b_s, c_s, d_c, d_n, d_m = (s.astype(f32) for s in st)
    bsz, t, _ = x.shape

    def heads(z, h):
        return z.reshape(bsz, t, h, -1)

    u = jnp.einsum('btd,de->bte', x, w_in).astype(f32)
    (a_qkv, a_z, a_b, a_a, b_z, b_xbc, b_dt, c_q, c_f, c_i, c_g,
     d_q, d_k, d_v, d_o, d_i, d_f) = _split(u, IN_SIZES)

    a_qkv, a_conv_new = _causal_conv(a_qkv, a_conv, conv_a_w, conv_a_b)
    aq, ak, av = _split(jax.nn.silu(a_qkv), (QK_A, QK_A, W_A))
    aq = _l2norm(heads(aq, H_A)) * DK_A ** -0.5
    ak = _l2norm(heads(ak, H_A))
    log_alpha = -jnp.exp(a_log_a) * jax.nn.softplus(a_a + dt_bias_a)
    o_a, a_s_new = _gated_delta(aq, ak, heads(av, H_A), log_alpha, jax.nn.sigmoid(a_b), a_s)
    y_a = _rms_heads(o_a, norm_a_g) * jax.nn.silu(a_z)

    b_xbc, b_conv_new = _causal_conv(b_xbc, b_conv, conv_b_w, conv_b_b)
    bx, bB, bC = _split(jax.nn.silu(b_xbc), (W_B, G_B * N_B, G_B * N_B))
    bx = heads(bx, H_B)
    bB = jnp.repeat(heads(bB, G_B), H_B // G_B, axis=2)
    bC = jnp.repeat(heads(bC, G_B), H_B // G_B, axis=2)
    dt = jax.nn.softplus(b_dt + dt_bias_b)
    o_b, b_s_new = _scalar_decay_attn(bC, bB, bx * dt[..., None], -jnp.exp(a_log_b) * dt, b_s)
    o_b = (o_b + d_skip_b[:, None] * bx).reshape(bsz, t, W_B) * jax.nn.silu(b_z)
    y_b = _rms_heads(o_b.reshape(bsz, t, G_B, -1), norm_b_g)

    lb = lb.reshape(H_C, DK_C)
    fx = heads(c_f, H_C)
    f_gate = lb + (1.0 - lb) * jax.nn.sigmoid(fx)
    log_f = jnp.log(f_gate)
    kc = 1.0 - f_gate
    o_c, c_s_new = _vector_decay_attn(heads(c_q, H_C), kc, heads(c_i, H_C), log_f, c_s)
    y_c = _rms_heads(o_c, norm_c_g) * jax.nn.sigmoid(c_g)

    h_d, d_c_new, d_n_new, d_m_new = _mlstm(
        heads(d_q, H_D), heads(d_k, H_D) * DK_D ** -0.5, heads(d_v, H_D),
        d_i + i_bias_d, jax.nn.log_sigmoid(d_f + f_bias_d), d_c, d_n, d_m)
    y_d = _rms_heads(h_d, norm_d_g) * jax.nn.sigmoid(d_o)

    mix = jnp.concatenate([y_a, y_b, y_c, y_d], axis=-1).astype(x.dtype) @ w_out
    h1 = _layernorm(DEEPNORM_ALPHA * x + mix, ln1_g, ln1_b)
    ff = jnp.square(jax.nn.relu(h1 @ w_up)) @ w_down
    out = _layernorm(DEEPNORM_ALPHA * h1 + ff, ln2_g, ln2_b)
    return out, (a_conv_new, a_s_new, b_conv_new, b_s_new, c_s_new, d_c_new, d_n_new, d_m_new)


def _decoder(x, states, emb_ln_g, emb_ln_b, lb_logits_c, layer_w):
    p = jax.nn.softmax(lb_logits_c.astype(jnp.float32), axis=0)
    lb_all = jnp.cumsum(p, axis=0) - p[0]
    h = _layernorm(x, emb_ln_g, emb_ln_b)
    new = []
    for l in range(DEPTH):
        h, st = _layer(h, tuple(s[l] for s in states), lb_all[l], *[w[l] for w in layer_w])
        new.append(st)
    return h, tuple(jnp.stack(z) for z in zip(*new))


def setup_inputs(seed: int = 0) -> dict:
    key = jax.random.key(seed)
    ks = iter(jax.random.split(key, 64))

    def nrm(shape, scale=1.0):
        return scale * jax.random.normal(next(ks), shape, jnp.float32)

    def gain(shape):
        return 1.0 + nrm(shape, 0.02)

    def unif(shape, lo, hi):
        return jax.random.uniform(next(ks), shape, jnp.float32, lo, hi)

    def dt_bias(shape):
        dt = jnp.exp(unif(shape, math.log(1e-3), math.log(1e-1)))
        return dt + jnp.log(-jnp.expm1(-dt))

    L = DEPTH
    return {
        'x_prompt': nrm((BATCH, SEQ, D_MODEL)),
        'x_sample': nrm((DEC_BATCH, DEC_SEQ, D_MODEL)),
        'state_a_conv': nrm((L, DEC_BATCH, CONV_K - 1, CONV_A_CH)),
        'state_a_ssm': nrm((L, DEC_BATCH, H_A, DK_A, DV_A), 0.3),
        'state_b_conv': nrm((L, DEC_BATCH, CONV_K - 1, CONV_B_CH)),
        'state_b_ssm': nrm((L, DEC_BATCH, H_B, N_B, P_B), 0.3),
        'state_c_ssm': nrm((L, DEC_BATCH, H_C, DK_C, DV_C), 0.5),
        'state_d_cmem': nrm((L, DEC_BATCH, H_D, DK_D, DV_D), 0.3),
        'state_d_nvec': nrm((L, DEC_BATCH, H_D, DK_D), 0.3),
        'state_d_mstab': nrm((L, DEC_BATCH, H_D)),
        'emb_ln_g': gain((D_MODEL,)),
        'emb_ln_b': nrm((D_MODEL,), 0.02),
        'lb_logits_c': nrm((L, QK_C), 0.5),
        'w_in': nrm((L, D_MODEL, IN_DIM), D_MODEL ** -0.5),
        'conv_a_w': nrm((L, CONV_K, CONV_A_CH), CONV_K ** -0.5),
        'conv_a_b': nrm((L, CONV_A_CH), 0.02),
        'a_log_a': jnp.log(unif((L, H_A), 1.0, 16.0)),
        'dt_bias_a': dt_bias((L, H_A)),
        'norm_a_g': gain((L, W_A)),
        'conv_b_w': nrm((L, CONV_K, CONV_B_CH), CONV_K ** -0.5),
        'conv_b_b': nrm((L, CONV_B_CH), 0.02),
        'a_log_b': jnp.log(unif((L, H_B), 1.0, 16.0)),
        'dt_bias_b': dt_bias((L, H_B)),
        'd_skip_b': gain((L, H_B)),
        'norm_b_g': gain((L, W_B)),
        'norm_c_g': gain((L, W_C)),
        'i_bias_d': nrm((L, H_D), 0.1),
        'f_bias_d': jnp.linspace(3.0, 6.0, H_D, dtype=jnp.float32) + nrm((L, H_D), 0.1),
        'norm_d_g': gain((L, W_D)),
        'w_out': nrm((L, MIX_OUT, D_MODEL), DEEPNORM_BETA * MIX_OUT ** -0.5),
        'ln1_g': gain((L, D_MODEL)),
        'ln1_b': nrm((L, D_MODEL), 0.02),
        'w_up': nrm((L, D_MODEL, D_FF), D_MODEL ** -0.5),
        'w_down': nrm((L, D_FF, D_MODEL), DEEPNORM_BETA * D_FF ** -0.5),
        'ln2_g': gain((L, D_MODEL)),
        'ln2_b': nrm((L, D_MODEL), 0.02),
    }


def reference(x_prompt, x_sample, state_a_conv, state_a_ssm, state_b_conv, state_b_ssm,
              state_c_ssm, state_d_cmem, state_d_nvec, state_d_mstab,
              emb_ln_g, emb_ln_b, lb_logits_c, w_in, conv_a_w, conv_a_b, a_log_a, dt_bias_a,
              norm_a_g, conv_b_w, conv_b_b, a_log_b, dt_bias_b, d_skip_b, norm_b_g, norm_c_g,
              i_bias_d, f_bias_d, norm_d_g, w_out, ln1_g, ln1_b, w_up, w_down, ln2_g, ln2_b):
    layer_w = (w_in, conv_a_w, conv_a_b, a_log_a, dt_bias_a, norm_a_g,
               conv_b_w, conv_b_b, a_log_b, dt_bias_b, d_skip_b, norm_b_g, norm_c_g,
               i_bias_d, f_bias_d, norm_d_g, w_out, ln1_g, ln1_b, w_up, w_down, ln2_g, ln2_b)
    sample_states = (state_a_conv, state_a_ssm, state_b_conv, state_b_ssm,
                     state_c_ssm, state_d_cmem, state_d_nvec, state_d_mstab)
    prompt_states = tuple(jnp.zeros((DEPTH, x_prompt.shape[0]) + s.shape[2:], jnp.float32)
                          for s in sample_states)
    y_prompt, (pa_conv, pa_ssm, pb_conv, pb_ssm, pc_ssm, pd_cmem, pd_nvec, pd_mstab) = _decoder(
        x_prompt, prompt_states, emb_ln_g, emb_ln_b, lb_logits_c, layer_w)
    y_sample, (sa_conv, sa_ssm, sb_conv, sb_ssm, sc_ssm, sd_cmem, sd_nvec, sd_mstab) = _decoder(
        x_sample, sample_states, emb_ln_g, emb_ln_b, lb_logits_c, layer_w)
    return (y_prompt, y_sample,
            pa_conv, pa_ssm, pb_conv, pb_ssm, pc_ssm, pd_cmem, pd_nvec, pd_mstab,
            sa_conv, sa_ssm, sb_conv, sb_ssm, sc_ssm, sd_cmem, sd_nvec, sd_mstab)
```

```python
import math
import numpy as np
import concourse.bass as bass
import concourse.mybir as mybir
from concourse.bass_utils import run_bass_kernel_spmd

F32 = mybir.dt.float32
BF16 = mybir.dt.bfloat16
F32R = mybir.dt.float32r
ALU = mybir.AluOpType
AF = mybir.ActivationFunctionType
AX = mybir.AxisListType

SEM_ROT = 30000
import os
FP32_MIX = '1' == '1'
FP32_FFN = '1' == '1'
STQ = 'sp'
LOWP = '1' == '1'
DUMMY = int('0')
PIPE = '1' == '1'
PIPE2 = '1' == '1'
STAGE = '1' == '1'
NOSTG = '0' == '1'
EMBED_WAIT = True
EPS = 1e-6
DEPTH = 2
ALPHA = (2 * DEPTH) ** 0.25
IN_DIM = 3860
MIX_RANGE = [(0, 1032), (1032, 1804), (1804, 2828), (2828, 3860)]


class Tk:
    __slots__ = ("t", "w", "r", "ex")

    def __init__(self, t, ex=False):
        self.t = t
        self.w = {}
        self.r = {}
        self.ex = ex

    def __getitem__(self, idx):
        return self.t[idx]


class Eng:
    def __init__(self, K, name, h):
        self.K = K
        self.name = name
        self.h = h
        self.sems = []
        self.n = 0
        self.seen = {}
        self.newsem()

    def newsem(self):
        s = self.K.nc.alloc_semaphore(f"s_{self.name}_{len(self.sems)}")
        self.sems.append(s)
        self.n = 0


class KB:
    def __init__(self, nc, ndma_sems=32):
        self.nc = nc
        self.pe = Eng(self, "pe", nc.tensor)
        self.act = Eng(self, "act", nc.scalar)
        self.dve = Eng(self, "dve", nc.vector)
        self.pool = Eng(self, "pool", nc.gpsimd)
        self.sp = Eng(self, "sp", nc.sync)
        self.dsems = [[nc.alloc_semaphore(f"dma{i}"), 0] for i in range(ndma_sems)]
        self.dnext = 0
        self.ninst = 0
        self.defer = None

    def _wait(self, eng, ev):
        sem, val = ev
        key = id(sem)
        if eng.seen.get(key, 0) >= val:
            return
        if self.defer is not None:
            cur = self.defer.get(key)
            if cur is None or cur[1] < val:
                self.defer[key] = (sem, val)
            return
        eng.h.wait_ge(sem, val)
        eng.seen[key] = val

    def _flush(self, eng, keep_last):
        items = list(self.defer.values())
        self.defer = None
        last = None
        if keep_last and items:
            last = items.pop()
        for sem, val in items:
            eng.h.wait_ge(sem, val)
            eng.seen[id(sem)] = val
        return last

    def _deps(self, eng, w, r, skip_same=False):
        evs = []
        for b in r:
            evs.extend(b.w.values())
            if b.ex:
                evs.extend(b.r.values())
        for b in w:
            evs.extend(b.w.values())
            evs.extend(b.r.values())
        mysem = eng.sems[-1]
        for ev in evs:
            if skip_same and ev[0] is mysem:
                continue
            self._wait(eng, ev)

    def op(self, eng, fn, w=(), r=()):
        if eng.n >= SEM_ROT:
            eng.newsem()
        self.defer = {} if EMBED_WAIT else None
        self._deps(eng, w, r, skip_same=(eng is self.pe))
        last = self._flush(eng, True) if EMBED_WAIT else None
        ins = fn(eng.h)
        if last is not None:
            ins._wait_ge(last[0], last[1])
            eng.seen[id(last[0])] = last[1]
        eng.n += 1
        ins.then_inc(eng.sems[-1], 1)
        ev = (eng.sems[-1], eng.n)
        for b in w:
            b.w[id(ev[0])] = ev
        for b in r:
            b.r[id(ev[0])] = ev
        self.ninst += 1
        return ins

    def dma(self, eng, out_ap, in_ap, w=(), r=(), **kw):
        slot = self.dsems[self.dnext]
        self.dnext = (self.dnext + 1) % len(self.dsems)
        sem, val = slot
        if val > 0:
            self._wait(eng, (sem, val))
        self._deps(eng, w, r)
        ins = eng.h.dma_start(out=out_ap, in_=in_ap, **kw)
        slot[1] = val + 16
        ins.then_inc(sem, 16)
        ev = (sem, slot[1])
        for b in w:
            b.w[id(ev[0])] = ev
        for b in r:
            b.r[id(ev[0])] = ev
        self.ninst += 1
        return ev

    def barrier(self):
        engs = (self.pe, self.act, self.dve, self.pool, self.sp)
        for e in engs:
            for sem, val in self.dsems:
                if val > 0:
                    self._wait(e, (sem, val))
            for o in engs:
                if o is not e and o.n > 0:
                    self._wait(e, (o.sems[-1], o.n))

    def finish(self):
        for sem, val in self.dsems:
            if val > 0:
                self._wait(self.sp, (sem, val))
        for e in (self.pe, self.act, self.dve, self.pool):
            if e.n > 0:
                self._wait(self.sp, (e.sems[-1], e.n))


def _const_table(NS):
    cols = {}
    parts = []
    off = [0]

    def add(name, arr):
        arr = np.asarray(arr, np.float32)
        a = np.zeros((128, arr.shape[1]), np.float32)
        a[: arr.shape[0]] = arr
        cols[name] = (off[0], arr.shape[1])
        off[0] += arr.shape[1]
        parts.append(a)

    i64 = np.arange(64)
    t, r = i64[:, None], i64[None, :]
    add("ident", np.eye(128))
    add("ones", np.ones((128, 64)))
    add("I4", np.eye(4))
    add("oh4", np.repeat(np.eye(4), 64, axis=1))
    add("zeros", np.zeros((64, 264)))
    for sfx, same in (("", np.ones((64, 64), bool)), ("S", (t // 4) == (r // 4))):
        incl = (t <= r) & same
        gt = (t > r) & same
        add("incl" + sfx, incl)
        add("gt" + sfx, gt)
        add("neg4" + sfx, np.tile(np.where(incl, 0.0, -30000.0), (1, 4)))
        add("strict" + sfx, (t < r) & same)
    add("rel", ((t <= r).astype(np.float32) - (t <= 31).astype(np.float32)) * np.ones((64, 64)))
    seqoh = (i64[:, None] // 4) == np.arange(NS)[None, :]
    add("seqoh", seqoh)
    cm = ((i64[None, :] // 4) == np.arange(NS)[:, None]).astype(np.float32).reshape(1, NS * 64)
    add("colmask", np.tile(cm, (64, 1)))
    return np.concatenate(parts, axis=1), cols


PROW = {}
_o = 0
for _n, _w in (("norm_a_g", 256), ("norm_b_g", 256), ("norm_c_g", 256), ("norm_d_g", 256),
               ("lbl0", 256), ("lbl1", 256), ("a_log_a", 4), ("dt_bias_a", 4), ("a_log_b", 4),
               ("dt_bias_b", 4), ("d_skip_b", 4), ("pad", 4)):
    PROW[_n] = (_o, _w)
    _o += _w
NPROW = _o
NPCOL = 6 * 4 + 6 + 4 * 4 + 4 + 2


def build(nc, T=2048, NS=16, L=2):
    K = KB(nc)
    NSK = NS * 4
    assert NSK == 64 and T % 128 == 0
    NTP = T // 128
    NT = T + NSK
    NTILES = NTP + 1
    NCH = T // 64 + 1
    cst_np, CO = _const_table(NS)
    NCST = cst_np.shape[1]

    def din(name, shape):
        return nc.dram_tensor(name, list(shape), F32, kind="ExternalInput").ap()

    def dout(name, shape):
        return nc.dram_tensor(name, list(shape), F32, kind="ExternalOutput").ap()

    xin = din("xin", (NT, 1024))
    w_in = din("w_in", (L, 1024, IN_DIM))
    w_out = din("w_out", (L, 1024, 1024))
    w_up = din("w_up", (L, 1024, 4096))
    w_down = din("w_down", (L, 4096, 1024))
    prow_d = din("prow", (L, 128, NPROW))
    lnrow_d = din("lnrow", (2 * L + 1, 128, 2048))
    pcol_d = din("pcol", (L, 128, NPCOL))
    cst_d = din("cst", (128, NCST))
    s_conv_d = [din("sa_conv", (L, NS * 3, 768)), din("sb_conv", (L, NS * 3, 512))]
    s_ssm_d = [din("sa_ssm", (L, NS, 4, 64, 64)), din("sb_ssm", (L, NS, 4, 64, 64)),
               din("sc_ssm", (L, NS, 4, 64, 64)), din("sd_cmem", (L, NS, 4, 64, 64))]
    s_nvec_d = din("sd_nvec", (L, NS * 4, 64))
    s_m_d = din("sd_m", (L, 4, NS))

    y_d = dout("y", (NT, 1024))
    p_conv_o = [dout("pa_conv", (L, 3, 768)), dout("pb_conv", (L, 3, 512))]
    p_ssm_o = [dout("pa_ssm", (L, 4, 64, 64)), dout("pb_ssm", (L, 4, 64, 64)),
               dout("pc_ssm", (L, 4, 64, 64)), dout("pd_cmem", (L, 4, 64, 64))]
    p_nvec_o = dout("pd_nvec", (L, 4, 64))
    p_m_o = dout("pd_m", (L, 4, 1))
    o_conv_o = [dout("oa_conv", (L, NS * 3, 768)), dout("ob_conv", (L, NS * 3, 512))]
    o_ssm_o = [dout("oa_ssm", (L, NS, 4, 64, 64)), dout("ob_ssm", (L, NS, 4, 64, 64)),
               dout("oc_ssm", (L, NS, 4, 64, 64)), dout("od_cmem", (L, NS, 4, 64, 64))]
    o_nvec_o = dout("od_nvec", (L, NS * 4, 64))
    o_m_o = dout("od_m", (L, 4, NS))

    def sb(name, shape, dt=F32):
        return Tk(nc.alloc_sbuf_tensor(name, list(shape), dt))

    Rt = [Tk(None) for _ in range(NTILES)]
    R_all = nc.alloc_sbuf_tensor("R", [128, NTILES, 1024], F32)
    for i in range(NTILES):
        Rt[i].t = R_all[:, i, :]
    AT_all = nc.alloc_sbuf_tensor("AT", [128, 8, NT], BF16)
    ATc = [Tk(AT_all[:, :, 64 * c:64 * c + 64]) for c in range(NCH)]
    CST = sb("CST", (128, NCST))
    PRW = sb("PRW", (128, NPROW))
    LNR = sb("LNR", (128, 2048))
    PCL = sb("PCL", (128, NPCOL))
    XF = sb("XF", (128, 8 * 64))
    ARENA_COLS = 20400 - 3856
    arena_t = nc.alloc_sbuf_tensor("ARENA", [128, ARENA_COLS], F32)
    aoff = [0]

    def ar(ncols, dt=F32, shape=None):
        ncols = (ncols + 7) // 8 * 8
        assert aoff[0] + ncols <= ARENA_COLS, ("arena overflow", aoff[0], ncols)
        ap = arena_t[:, aoff[0]:aoff[0] + ncols]
        aoff[0] += ncols
        if dt == BF16:
            ap = ap.bitcast(BF16)
        return Tk(ap)

    OPB_COLS = 3856
    opb_t = nc.alloc_sbuf_tensor("OPB", [128, OPB_COLS], F32)
    boff = [0]

    def arb(ncols):
        ncols = (ncols + 7) // 8 * 8
        assert boff[0] + ncols <= OPB_COLS, ("opb overflow", boff[0], ncols)
        ap = opb_t[:, boff[0]:boff[0] + ncols]
        boff[0] += ncols
        return Tk(ap)

    PJ = Tk(nc.alloc_psum_tensor("PJ", [128, 1024], F32), ex=True)
    PDUM = Tk(nc.alloc_psum_tensor("PDUM", [128, 512], F32), ex=True)
    PX = Tk(nc.alloc_psum_tensor("PX", [128, 512], F32), ex=True)
    PD = Tk(nc.alloc_psum_tensor("PD", [128, 512], F32), ex=True)
    PA = Tk(nc.alloc_psum_tensor("PA", [128, 512], F32), ex=True)
    PO = Tk(nc.alloc_psum_tensor("PO", [128, 512], F32), ex=True)
    PS = Tk(nc.alloc_psum_tensor("PS", [128, 512], F32), ex=True)

    pe, act, dve, pool, sp = K.pe, K.act, K.dve, K.pool, K.sp
    stq = {'sp': sp, 'pool': pool, 'act': act}[STQ]

    def TT(eng, out, a, b, op, w, r):
        K.op(eng, lambda e: e.tensor_tensor(out=out, in0=a, in1=b, op=op), w=w, r=r)

    def TS(eng, out, a, s1, op0, w, r, s2=None, op1=None):
        if op1 is None:
            K.op(eng, lambda e: e.tensor_scalar(out=out, in0=a, scalar1=s1, scalar2=None, op0=op0), w=w, r=r)
        else:
            K.op(eng, lambda e: e.tensor_scalar(out=out, in0=a, scalar1=s1, scalar2=s2, op0=op0, op1=op1), w=w, r=r)

    def STT(out, a, s, b, op0, op1, w, r):
        K.op(dve, lambda e: e.scalar_tensor_tensor(out=out, in0=a, scalar=s, in1=b, op0=op0, op1=op1), w=w, r=r)

    def ACT(out, a, func, w, r, bias=None, scale=None):
        kw = {}
        if bias is not None:
            kw["bias"] = bias
        if scale is not None:
            kw["scale"] = scale
        K.op(act, lambda e: e.activation(out=out, in_=a, func=func, **kw), w=w, r=r)

    def SIG(out, a, w, r):
        ACT(out, a, AF.Exp, w=w, r=r, scale=-1.0)
        ACT(out, out, AF.Ln, w=w, r=w, bias=1.0)
        ACT(out, out, AF.Exp, w=w, r=w, scale=-1.0)

    def RSQ(out, a, w, r):
        ACT(out, a, AF.Ln, w=w, r=r)
        ACT(out, out, AF.Exp, w=w, r=w, scale=-0.5)

    def CPA(out, a, w, r):
        K.op(act, lambda e: e.copy(out=out, in_=a), w=w, r=r)

    def CPV(out, a, w, r, eng=None):
        K.op(eng or dve, lambda e: e.tensor_copy(out=out, in_=a), w=w, r=r)

    dstate = {"on": False, "n": 0}

    def _dummy():
        if DUMMY and dstate["on"]:
            dstate["n"] += 1
            if dstate["n"] % DUMMY == 0:
                K.op(pe, lambda e: e.matmul(PDUM[0:128, 0:min(512, NT)], lhsT=AT_all[:, 0, 0:128], rhs=AT_all[:, 1, 0:min(512, NT)],
                                            start=True, stop=True), w=[PDUM])

    def MM(out, lhsT, rhs, st, sp_, w, r):
        K.op(pe, lambda e: e.matmul(out, lhsT=lhsT, rhs=rhs, start=st, stop=sp_), w=w, r=r)
        if sp_:
            _dummy()

    def TPo(out, a, idn, w, r):
        K.op(pe, lambda e: e.transpose(out=out, in_=a, identity=idn), w=w, r=r)
        _dummy()

    def RECIP(out, a, w, r):
        K.op(dve, lambda e: e.reciprocal(out=out, in_=a), w=w, r=r)

    def MEMSET(eng, out, v, w):
        K.op(eng, lambda e: e.memset(out, v), w=w)

    def C(name, rows=64, sample=False):
        o, n = CO[name + ("S" if sample else "")]
        return CST[0:rows, o:o + n]

    ident = lambda n: CST[0:n, 0:n]

    def b3(ap2, n):
        return ap2.unsqueeze(2).broadcast_to([ap2.shape[0], ap2.shape[1], n])

    def bm(ap2, n):
        return ap2.unsqueeze(1).broadcast_to([ap2.shape[0], n, ap2.shape[1]])

    def v3(ap2, a):
        return ap2.rearrange("p (a b) -> p a b", a=a)

    K.dma(sp, CST[:, :], cst_d[:, :], w=[CST])

    STATs = [sb("STAT0", (128, 16)), sb("STAT1", (128, 16))]
    MVs = [sb("MV0", (128, 8)), sb("MV1", (128, 8))]
    XTs = [Tk(arena_t[:, 0:1024]), Tk(arena_t[:, 1024:2048])]

    def ln_tile(Xtk, Xap, P, OUTtk, OUTap, par=0):
        STAT, MV = STATs[par], MVs[par]
        for hf in range(2):
            K.op(dve, lambda e: e.bn_stats(out=STAT[0:P, hf * 6:hf * 6 + 6], in_=Xap[:, hf * 512:(hf + 1) * 512]),
                 w=[STAT], r=[Xtk])
        K.op(dve, lambda e: e.bn_aggr(out=MV[0:P, 0:2], in_=STAT[0:P, 0:12]), w=[MV], r=[STAT])
        TS(dve, MV[0:P, 2:3], MV[0:P, 1:2], EPS, ALU.add, w=[MV], r=[MV])
        RSQ(MV[0:P, 3:4], MV[0:P, 2:3], w=[MV], r=[MV])
        TS(dve, OUTap, Xap, MV[0:P, 0:1], ALU.subtract, w=[OUTtk], r=[Xtk, MV], s2=MV[0:P, 3:4], op1=ALU.mult)
        TT(pool, OUTap, OUTap, LNR[0:P, 0:1024], ALU.mult, w=[OUTtk], r=[OUTtk, LNR])
        TT(pool, OUTap, OUTap, LNR[0:P, 1024:2048], ALU.add, w=[OUTtk], r=[OUTtk, LNR])

    def to_AT(SRCtk, SRCap, i, P):
        cs = [ATc[2 * i]] + ([ATc[2 * i + 1]] if P == 128 else [])
        for hf in range(2):
            PB = (PX, PD, PA, PO)[(2 * i + hf) % 4]
            for b in range(4):
                TPo(PB[0:128, b * 128:b * 128 + P], SRCap[0:P, (hf * 4 + b) * 128:(hf * 4 + b + 1) * 128],
                    ident(P), w=[PB], r=[SRCtk, CST])
            CPA(AT_all[:, hf * 4:hf * 4 + 4, 128 * i:128 * i + P], v3(PB[0:128, 0:512], 4)[:, :, 0:P], w=cs, r=[PB])
            if i == 0:
                CPV(v3(XF[:, 0:512], 8)[:, hf * 4:hf * 4 + 4, :], v3(PB[0:128, 0:512], 4)[:, :, 0:64], w=[XF], r=[PB])

    def tile_P(i):
        return 128 if i < NTP else NSK

    K.dma(sp, LNR[:, :], lnrow_d[0], w=[LNR])
    for i in range(NTILES):
        P = tile_P(i)
        K.dma(sp, Rt[i][0:P, :], xin[128 * i:128 * i + P, :], w=[Rt[i]])
        XT = XTs[i % 2]
        ln_tile(Rt[i], Rt[i][0:P, :], P, XT, XT[0:P, :], i % 2)
        to_AT(XT, XT, i, P)
        K.op(act, lambda e: e.mul(out=Rt[i][0:P, :], in_=XT[0:P, :], mul=ALPHA), w=[Rt[i]], r=[XT])

    aoff[0] = 0
    Wm = ar(8 * 1032 // 2, BF16)
    Wo = ar(2 * 1024 // 2, BF16)
    U_main = ar(1032)
    RAW = [ar(6 * NS * 7), ar(6 * 67)]
    RAWS = RAW[0]
    CV = ar(768)
    SM = ar(64)
    SM2 = ar(64)
    DS = ar(64)
    LL = ar(8)
    LM = ar(256)
    LLS = ar(4 * NS)
    QT = arb(256)
    QPT = arb(256)
    KT = arb(256)
    WT = arb(256)
    QN = ar(256)
    KN = ar(256)
    KP = arb(256)
    RH = arb(512)
    UW = ar(512)
    DT = ar(256)
    ATT = arb(256)
    BA = [arb(256), arb(256)]
    BTA = [arb(256), arb(256)]
    RR = [arb(256), arb(256)]
    TMP = ar(512)
    TMP2 = ar(256)
    Ob = ar(264)
    VA = arb(264)
    Yb = ar(256)
    YT = ar(64, BF16)
    SPs = ar(264)
    SSs = [ar(264), ar(264)]
    SNV = ar(64)
    QM = ar(512)
    WM_ = QM
    WST = [ar(1032), ar(1032)]
    YT32 = ar(128)
    VM = ar(264)
    GI = ar(72)
    GF = ar(72)
    LF = ar(72)
    BC = ar(72)
    AA = ar(72)
    MGX = ar(72)
    EXS = ar(72)
    EXK = ar(72)
    EXM = ar(72)
    NMG = ar(72)
    AM = ar(256)
    DG = ar(64)
    MINI = ar(16)
    MFIN = ar(16)
    NEGF = ar(8)
    ONESROW = ar(72)
    LBR = ar(256)
    NEA = ar(8)
    mixer_arena_end = aoff[0]

    Wm3 = Wm[:, 0:8 * 1032].rearrange("p (k c) -> p k c", k=8)
    Wo3 = Wo[:, 0:2048].rearrange("p (k c) -> p k c", k=2)

    def prow(name, P=64):
        o, n = PROW[name]
        return PRW[0:P, o:o + n]

    def tp4(srcs, DSTtk, rd, RO=lambda a: a):
        for h in range(4):
            TPo(PX[0:64, h * 64:(h + 1) * 64], srcs[h], ident(64), w=[PX], r=rd + [CST])
        CPA(RO(DSTtk[0:64, 0:256]), PX[0:64, 0:256], w=[DSTtk], r=[PX])

    def mixer_pass(l, m):
        c0, c1 = MIX_RANGE[m]
        ncol = c1 - c0
        dvw = 66 if m == 3 else 64
        nconv = {0: 6, 1: 4}.get(m, 0)
        convoff = {0: 0, 1: 256}.get(m, 0)
        XF3 = v3(XF[:, 0:512], 8)
        ngrp0 = (ncol + 511) // 512
        zl = C("zeros")
        if nconv and FP32_MIX:
            MM(PX[0:128, 0:nconv * 64], zl[:, 0:128], CST[0:64, 0:nconv * 64], True, False, w=[PX], r=[CST])
        for k in range(8):
            ws = WST[k % 2]
            if not NOSTG:
                K.dma(stq, ws[:, 0:ncol], w_in[l, k * 128:(k + 1) * 128, c0:c1], w=[ws])
            for g in range(ngrp0 if FP32_MIX else 0):
                wd = min(512, ncol - g * 512)
                if g == 2:
                    MM(PS[0:64, 384:384 + wd], XF3[:, k, :], ws[:, g * 512:g * 512 + wd], k == 0, k == 7, w=[PS], r=[XF, ws])
                else:
                    MM(PJ[0:64, g * 512:g * 512 + wd], XF3[:, k, :], ws[:, g * 512:g * 512 + wd], k == 0, k == 7, w=[PJ], r=[XF, ws])
            for b in range(nconv if FP32_MIX else 0):
                MM(PX[0:128, b * 64:(b + 1) * 64], ws[:, convoff + b * 128:convoff + (b + 1) * 128], XF3[:, k, :],
                   False, (k == 7 and b == nconv - 1), w=[PX], r=[XF, ws])
            if STAGE:
                K.op(pool, lambda e: e.tensor_copy(out=Wm3[:, k, 0:ncol], in_=ws[:, 0:ncol]), w=[Wm], r=[ws])
            else:
                K.dma(pool, Wm3[:, k, 0:ncol], w_in[l, k * 128:(k + 1) * 128, c0:c1], w=[Wm])
        for b in range(2):
            if not NOSTG:
                K.dma(stq, WST[b][:, 0:1024], w_out[l, 256 * m + 128 * b:256 * m + 128 * (b + 1), :], w=[WST[b]])
            if STAGE:
                K.op(pool, lambda e: e.tensor_copy(out=Wo3[:, b, :], in_=WST[b][:, 0:1024]), w=[Wo], r=[WST[b]])
            else:
                K.dma(pool, Wo3[:, b, :], w_out[l, 256 * m + 128 * b:256 * m + 128 * (b + 1), :], w=[Wo])
        MEMSET(dve, SPs[0:64, 0:264], 0.0, w=[SPs])
        if m == 3:
            K.dma(sp, MINI[0:4, 0:NS], s_m_d[l], w=[MINI])
            MEMSET(dve, MGX[0:4, 0:1], 0.0, w=[MGX])
            MEMSET(dve, BC[0:4, 0:1], 0.0, w=[BC])
            MEMSET(dve, ONESROW[0:4, 0:72], 1.0, w=[ONESROW])
            TS(dve, NEGF[0:4, 0:1], PCL[0:4, 51:52], -1.0, ALU.mult, w=[NEGF], r=[PCL])

        def chunk_body(c):
            sample = (c == NCH - 1)
            tok0 = 64 * c
            i_tile, pb = c // 2, 64 * (c % 2)
            Mk = lambda name, rows=64: C(name, rows, sample)
            lp = LOWP and (0 < c < NCH - 1)
            RO = (lambda a: a.bitcast(F32R)) if lp else (lambda a: a)
            ROW = (lambda a: a.bitcast(F32R)) if LOWP else (lambda a: a)
            if m == 0:
                bgr, cgr = [(768, 256), (1024, 8)], [(0, 512), (512, 256)]
                ber, cer = [(768, 1032)], [(0, 768)]
            elif m == 1:
                bgr, cgr = [(0, 256), (768, 4)], [(256, 256), (512, 256)]
                ber, cer = [(0, 256), (768, 772)], [(256, 768)]
            elif m == 2:
                bgr, cgr, ber, cer = [(0, 512), (512, 512)], [], [(0, 1024)], []
            else:
                bgr, cgr, ber, cer = [(0, 512), (512, 512), (1024, 8)], [], [(0, 1032)], []
            paired_even = (not sample) and c >= 2 and c % 2 == 0 and c + 1 <= NCH - 2
            paired_odd = (not sample) and c >= 3 and c % 2 == 1
            need_raw = bool(nconv) and (sample or c == NCH - 2 or (paired_even and c + 1 == NCH - 2))
            grps = bgr + (cgr if need_raw else [])
            evr = ber + (cer if need_raw else [])
            U = WST[1] if ((not sample) and c % 2 == 1) else U_main
            do_proj = not ((c == 0 and FP32_MIX) or paired_odd)
            Pm = 128 if paired_even else 64
            if do_proj:
                rtk = [ATc[c], Wm] + ([ATc[c + 1]] if paired_even else [])
                for (g0, wd) in grps:
                    for k in range(8):
                        if g0 == 1024:
                            MM(PS[0:Pm, 384:384 + wd], AT_all[:, k, tok0:tok0 + Pm], Wm3[:, k, g0:g0 + wd],
                               k == 0, k == 7, w=[PS], r=rtk)
                        else:
                            MM(PJ[0:Pm, g0:g0 + wd], AT_all[:, k, tok0:tok0 + Pm], Wm3[:, k, g0:g0 + wd],
                               k == 0, k == 7, w=[PJ], r=rtk)
            if nconv and not (c == 0 and FP32_MIX):
                for b in range(nconv):
                    for k in range(8):
                        MM(PX[0:128, b * 64:(b + 1) * 64], Wm3[:, k, convoff + b * 128:convoff + (b + 1) * 128],
                           AT_all[:, k, tok0:tok0 + 64], k == 0, k == 7, w=[PX], r=[ATc[c], Wm])
            yield 'P1'
            if c == 0 and FP32_MIX:
                CPA(U[0:64, 0:min(ncol, 1024)], PJ[0:64, 0:min(ncol, 1024)], w=[U], r=[PJ])
                if ncol > 1024:
                    CPA(U[0:64, 1024:ncol], PS[0:64, 384:384 + ncol - 1024], w=[U], r=[PS])
            elif paired_odd:
                for (e0, e1) in evr:
                    K.dma(sp, WST[1][0:64, e0:e1], U_main[64:128, e0:e1], w=[WST[1]], r=[U_main])
            else:
                for (e0, e1) in evr:
                    e1j = min(e1, 1024)
                    CPA(U[0:Pm, e0:e1j], PJ[0:Pm, e0:e1j], w=[U], r=[PJ])
                    if e1 > 1024:
                        CPA(U[0:Pm, 1024:e1], PS[0:Pm, 384:384 + e1 - 1024], w=[U], r=[PS])

            yield 'step'
            if nconv:
                nch_ = nconv * 128
                cst_o = (p_conv_o if not sample else o_conv_o)[m]
                if sample:
                    for j in range(3):
                        K.dma(sp, cst_o[l].rearrange("(s j) c -> j s c", j=3)[j],
                              U[1 + j:64:4, convoff:convoff + nch_], r=[U])
                elif c == NCH - 2:
                    K.dma(sp, cst_o[l], U[61:64, convoff:convoff + nch_], r=[U])
                wof = 0 if m == 0 else 30
                if not sample:
                    Rw = RAW[c % 2]
                    Rp = RAW[(c + 1) % 2]
                    R3 = v3(Rw[:, 0:6 * 67], 6)
                    P3 = v3(Rp[:, 0:6 * 67], 6)
                    CPA(R3[:, 0:nconv, 3:67], v3(PX[0:128, 0:nconv * 64], nconv), w=[Rw], r=[PX])
                    if c == 0:
                        MEMSET(dve, R3[:, 0:nconv, 0:3], 0.0, w=[Rw])
                    else:
                        CPV(R3[:, 0:nconv, 0:3], P3[:, 0:nconv, 64:67], w=[Rw], r=[Rp])
                    srcs = lambda b, j: R3[:, b, j:j + 64]
                    outv = lambda b: CV[:, b * 64:(b + 1) * 64]
                    rtk = Rw
                else:
                    R4 = RAWS[:, 0:6 * NS * 7].rearrange("p (b s j) -> p b s j", b=6, s=NS)
                    K.dma(sp, CV[0:NS * 3, 0:nch_], s_conv_d[m][l], w=[CV])
                    cb = CV
                    cbap = CV[0:NS * 3, 0:nch_]
                    for b in range(nconv):
                        TPo(PD[0:128, b * 48:(b + 1) * 48], cbap[:, b * 128:(b + 1) * 128],
                            ident(48),
                            w=[PD], r=[cb, CST])
                    for b in range(nconv):
                        CPA(R4[:, b, :, 0:3], PD[0:128, b * 48:(b + 1) * 48].rearrange("p (s j) -> p s j", j=3),
                            w=[RAWS], r=[PD])
                        CPA(R4[:, b, :, 3:7], PX[0:128, b * 64:(b + 1) * 64].rearrange("p (s j) -> p s j", j=4),
                            w=[RAWS], r=[PX])
                    srcs = lambda b, j: R4[:, b, :, j:j + 4]
                    outv = lambda b: CV[:, b * 64:(b + 1) * 64].rearrange("p (s j) -> p s j", j=4)
                    rtk = RAWS
                for b in range(nconv):
                    wc = lambda j: PCL[:, wof + b * 4 + j:wof + b * 4 + j + 1]
                    bc_ = PCL[:, wof + nconv * 4 + b:wof + nconv * 4 + b + 1]
                    TS(dve, outv(b), srcs(b, 0), wc(0), ALU.mult, w=[CV], r=[rtk, PCL], s2=bc_, op1=ALU.add)
                    for j in range(1, 4):
                        STT(outv(b), srcs(b, j), wc(j), outv(b), ALU.mult, ALU.add, w=[CV], r=[rtk, PCL, CV])
                        if j == 2:
                            yield 'step'
                    yield 'step'
                SIG(CV[:, 384:384 + nconv * 64], CV[:, 0:nconv * 64], w=[CV], r=[CV])
                yield 'step'
                TT(pool, CV[:, 0:nconv * 64], CV[:, 0:nconv * 64], CV[:, 384:384 + nconv * 64], ALU.mult, w=[CV], r=[CV])
                yield 'step'
                for b in range(nconv):
                    TPo(PJ[0:64, b * 128:(b + 1) * 128], CV[:, b * 64:(b + 1) * 64], ident(128), w=[PJ], r=[CV, CST])
                yield 'step'
                CPA(U[0:64, convoff:convoff + nch_], PJ[0:64, 0:nch_], w=[U], r=[PJ])

            yield 'P2'
            Utk = U
            if m in (0, 1):
                ell = LL[0:64, 0:4]
                if m == 0:
                    q_ap, k_ap, v_ap = U[0:64, 0:256], U[0:64, 256:512], U[0:64, 512:768]
                    z_ap, b_ap, a_ap = U[0:64, 768:1024], U[0:64, 1024:1028], U[0:64, 1028:1032]
                    dtb, nea = prow("dt_bias_a"), NEA[0:64, 0:4]
                else:
                    z_ap, x_ap = U[0:64, 0:256], U[0:64, 256:512]
                    Bm, Cm, a_ap = U[0:64, 512:640], U[0:64, 640:768], U[0:64, 768:772]
                    dtb, nea = prow("dt_bias_b"), NEA[0:64, 4:8]
                    for g in range(2):
                        TPo(PX[0:64, g * 64:(g + 1) * 64], Cm[:, g * 64:(g + 1) * 64], ident(64), w=[PX], r=[U, CST])
                        TPo(PX[0:64, 128 + g * 64:128 + (g + 1) * 64], Bm[:, g * 64:(g + 1) * 64], ident(64), w=[PX], r=[U, CST])
                    CPA(ROW(QT[0:64, 0:128]), PX[0:64, 0:128], w=[QT], r=[PX])
                    CPA(ROW(KT[0:64, 0:128]), PX[0:64, 128:256], w=[KT], r=[PX])
                TT(dve, SM2[0:64, 20:24], a_ap, dtb, ALU.add, w=[SM2], r=[U, PRW])
                ACT(SM2[0:64, 20:24], SM2[0:64, 20:24], AF.Exp, w=[SM2], r=[SM2])
                ACT(SM2[0:64, 24:28], SM2[0:64, 20:24], AF.Ln, w=[SM2], r=[SM2], bias=1.0)
                TT(dve, ell, SM2[0:64, 24:28], nea, ALU.mult, w=[LL], r=[SM2, NEA])
                TT(dve, v3(LM[0:64, 0:256], 4), bm(Mk("gt"), 4), b3(ell, 64), ALU.mult, w=[LM], r=[LL, CST])
                MM(PS[0:64, 0:4], Mk("incl"), ell, True, True, w=[PS], r=[CST, LL])
                MM(PS[0:64, 4:8], Mk("gt"), ell, True, True, w=[PS], r=[CST, LL])
                if not sample:
                    MM(PS[0:64, 8:12], C("ones"), ell, True, True, w=[PS], r=[CST, LL])
                    nds = 4
                else:
                    TT(dve, v3(LLS[0:64, 0:4 * NS], 4), b3(ell, NS), bm(C("seqoh"), 4), ALU.mult, w=[LLS], r=[LL, CST])
                    MM(PS[0:64, 8:8 + 4 * NS], C("ones"), LLS[0:64, 0:4 * NS], True, True, w=[PS], r=[CST, LLS])
                    nds = 4 * NS
                MM(PD[0:64, 0:256], ident(64), Mk("neg4"), True, False, w=[PD], r=[CST])
                for h in range(4):
                    MM(PD[0:64, h * 64:(h + 1) * 64], LM[0:64, h * 64:(h + 1) * 64], Mk("incl"), False, h == 3,
                       w=[PD], r=[LM, CST])
                ACT(SM[0:64, 0:8], PS[0:64, 0:8], AF.Exp, w=[SM], r=[PS])
                ACT(DS[0:64, 0:nds], PS[0:64, 8:8 + nds], AF.Exp, w=[DS], r=[PS])
                ACT(DT[0:64, 0:256], PD[0:64, 0:256], AF.Exp, w=[DT], r=[PD])
                eg, ek = SM[0:64, 0:4], SM[0:64, 4:8]
            if m == 0:
                TT(dve, TMP[0:64, 0:512], U[0:64, 0:512], U[0:64, 0:512], ALU.mult, w=[TMP], r=[U])
                K.op(dve, lambda e: e.tensor_reduce(out=SM2[0:64, 0:8], in_=v3(TMP[0:64, 0:512], 8), axis=AX.X, op=ALU.add),
                     w=[SM2], r=[TMP])
                TS(dve, SM2[0:64, 0:8], SM2[0:64, 0:8], EPS, ALU.add, w=[SM2], r=[SM2])
                RSQ(SM2[0:64, 8:16], SM2[0:64, 0:8], w=[SM2], r=[SM2])
                TT(dve, v3(QN[0:64, 0:256], 4), v3(q_ap, 4), b3(SM2[0:64, 8:12], 64), ALU.mult, w=[QN], r=[U, SM2])
                TS(dve, QN[0:64, 0:256], QN[0:64, 0:256], 0.125, ALU.mult, w=[QN], r=[QN])
                TT(dve, v3(KN[0:64, 0:256], 4), v3(k_ap, 4), b3(SM2[0:64, 12:16], 64), ALU.mult, w=[KN], r=[U, SM2])
                SIG(SM2[0:64, 16:20], b_ap, w=[SM2], r=[U])
                beta = SM2[0:64, 16:20]
                tp4([KN[0:64, h * 64:(h + 1) * 64] for h in range(4)], KT, [KN], ROW)
                TT(dve, v3(TMP2[0:64, 0:256], 4), v3(DT[0:64, 0:256], 4), bm(Mk("strict"), 4), ALU.mult, w=[TMP2], r=[DT, CST])
                TT(dve, v3(TMP2[0:64, 0:256], 4), v3(TMP2[0:64, 0:256], 4), b3(beta, 64), ALU.mult, w=[TMP2], r=[TMP2, SM2])
                TS(dve, TMP2[0:64, 0:256], TMP2[0:64, 0:256], -1.0, ALU.mult, w=[TMP2], r=[TMP2])
                RH3 = ROW(v3(RH[0:64, 0:512], 4))
                side = [
                    lambda: tp4([QN[0:64, h * 64:(h + 1) * 64] for h in range(4)], QT, [QN], ROW),
                    lambda: (TT(pool, v3(TMP[0:64, 0:256], 4), v3(QN[0:64, 0:256], 4), b3(eg, 64), ALU.mult, w=[TMP], r=[QN, SM]),
                             tp4([TMP[0:64, h * 64:(h + 1) * 64] for h in range(4)], QPT, [TMP], ROW)),
                    lambda: TT(dve, ROW(v3(KP[0:64, 0:256], 4)), v3(KN[0:64, 0:256], 4), b3(ek, 64), ALU.mult, w=[KP], r=[KN, SM]),
                    lambda: (CPV(RH3[:, :, 0:64], v3(v_ap, 4), w=[RH], r=[U]),
                             TT(dve, RH3[:, :, 64:128], v3(KN[0:64, 0:256], 4), b3(eg, 64), ALU.mult, w=[RH], r=[KN, SM])),
                ]
                for h in range(4):
                    MM(PA[0:64, h * 64:(h + 1) * 64], RO(KT[0:64, h * 64:(h + 1) * 64]), RO(KT[0:64, h * 64:(h + 1) * 64]),
                       True, True, w=[PA], r=[KT])
                TT(dve, ROW(BA[0][0:64, 0:256]), PA[0:64, 0:256], TMP2[0:64, 0:256], ALU.mult, w=[BA[0]], r=[PA, TMP2])
                for h in range(4):
                    TPo(PA[0:64, h * 64:(h + 1) * 64], BA[0][0:64, h * 64:(h + 1) * 64], ident(64), w=[PA], r=[BA[0], CST])
                CPA(ROW(BTA[0][0:64, 0:256]), PA[0:64, 0:256], w=[BTA[0]], r=[PA])
                TT(dve, ROW(v3(RR[0][0:64, 0:256], 4)), v3(BA[0][0:64, 0:256], 4), bm(ident(64), 4), ALU.add, w=[RR[0]], r=[BA[0], CST])
                cur = 0
                for lev in range(5):
                    nx = 1 - cur
                    for h in range(4):
                        sl = slice(h * 64, (h + 1) * 64)
                        MM(PA[0:64, sl], RO(BA[cur][0:64, sl]), RO(BTA[cur][0:64, sl]), True, True, w=[PA], r=[BA[cur], BTA[cur]])
                    CPA(ROW(BTA[nx][0:64, 0:256]), PA[0:64, 0:256], w=[BTA[nx]], r=[PA])
                    if lev < 4:
                        for h in range(4):
                            sl = slice(h * 64, (h + 1) * 64)
                            MM(PD[0:64, sl], RO(BTA[cur][0:64, sl]), RO(BA[cur][0:64, sl]), True, True, w=[PD], r=[BA[cur], BTA[cur]])
                        CPV(ROW(BA[nx][0:64, 0:256]), PD[0:64, 0:256], w=[BA[nx]], r=[PD])
                    for h in range(4):
                        sl = slice(h * 64, (h + 1) * 64)
                        MM(PO[0:64, sl], RO(BTA[nx][0:64, sl]), RO(RR[cur][0:64, sl]), True, True, w=[PO], r=[BTA[nx], RR[cur]])
                    TT(dve, ROW(RR[nx][0:64, 0:256]), PO[0:64, 0:256], RR[cur][0:64, 0:256], ALU.add, w=[RR[nx]], r=[PO, RR[cur]])
                    cur = nx
                    if side:
                        side.pop(0)()
                while side:
                    side.pop(0)()
                T2T = RR[cur]
                for h in range(4):
                    MM(PO[0:64, h * 128:(h + 1) * 128], RO(T2T[0:64, h * 64:(h + 1) * 64]), RO(RH[0:64, h * 128:(h + 1) * 128]),
                       True, True, w=[PO], r=[T2T, RH])
                TT(dve, v3(UW[0:64, 0:512], 4), v3(PO[0:64, 0:512], 4), b3(beta, 128), ALU.mult, w=[UW], r=[PO, SM2])
                UW3 = v3(UW[0:64, 0:512], 4)
                tp4([UW3[:, h, 64:128] for h in range(4)], WT, [UW], ROW)
                qt = lambda h: QT[0:64, h * 64:(h + 1) * 64]
                kt = lambda h: KT[0:64, h * 64:(h + 1) * 64]
                DTm = DT[0:64, 0:256]
                DTtk = DT
            elif m == 1:
                dt_ap = SM2[0:64, 24:28]
                VA3 = ROW(v3(VA[0:64, 0:256], 4))
                TT(dve, VA3, v3(x_ap, 4), b3(dt_ap, 64), ALU.mult, w=[VA], r=[U, SM2])
                Cg = Cm.rearrange("p (g d) -> p g d", g=2).unsqueeze(2).broadcast_to([64, 2, 2, 64])
                Bg = Bm.rearrange("p (g d) -> p g d", g=2).unsqueeze(2).broadcast_to([64, 2, 2, 64])
                eg4 = eg.rearrange("p (g e) -> p g e", g=2).unsqueeze(3).broadcast_to([64, 2, 2, 64])
                ek4 = ek.rearrange("p (g e) -> p g e", g=2).unsqueeze(3).broadcast_to([64, 2, 2, 64])
                TT(dve, TMP[0:64, 0:256].rearrange("p (g e d) -> p g e d", g=2, e=2), Cg, eg4, ALU.mult, w=[TMP], r=[U, SM])
                tp4([TMP[0:64, h * 64:(h + 1) * 64] for h in range(4)], QPT, [TMP], ROW)
                TT(dve, ROW(KP[0:64, 0:256].rearrange("p (g e d) -> p g e d", g=2, e=2)), Bg, ek4, ALU.mult, w=[KP], r=[U, SM])
                qt = lambda h: QT[0:64, (h // 2) * 64:(h // 2 + 1) * 64]
                kt = lambda h: KT[0:64, (h // 2) * 64:(h // 2 + 1) * 64]
                DTm = DT[0:64, 0:256]
                DTtk = DT
            elif m == 2:
                q_ap, fx_ap, v_ap, g_ap = U[0:64, 0:256], U[0:64, 256:512], U[0:64, 512:768], U[0:64, 768:1024]
                lb = LBR[0:64, 0:256]
                SIG(TMP[0:64, 0:256], fx_ap, w=[TMP], r=[U])
                TS(dve, TMP[0:64, 256:512], lb, -1.0, ALU.mult, w=[TMP], r=[LBR], s2=1.0, op1=ALU.add)
                TT(dve, TMP[0:64, 0:256], TMP[0:64, 0:256], TMP[0:64, 256:512], ALU.mult, w=[TMP], r=[TMP])
                TT(dve, TMP[0:64, 0:256], TMP[0:64, 0:256], lb, ALU.add, w=[TMP], r=[TMP, LBR])
                ACT(TMP2[0:64, 0:256], TMP[0:64, 0:256], AF.Ln, w=[TMP2], r=[TMP])
                TS(dve, KN[0:64, 0:256], TMP[0:64, 0:256], -1.0, ALU.mult, w=[KN], r=[TMP], s2=1.0, op1=ALU.add)
                ellc = TMP2[0:64, 0:256]
                MM(PA[0:64, 0:256], (Mk("incl") if sample else C("rel")), ellc, True, True, w=[PA], r=[CST, TMP2])
                MM(PD[0:64, 0:256], Mk("incl"), ellc, True, True, w=[PD], r=[CST, TMP2])
                MM(PO[0:64, 0:256], Mk("gt"), ellc, True, True, w=[PO], r=[CST, TMP2])
                if not sample:
                    for h in range(4):
                        MM(PS[0:64, h:h + 1], TMP2[0:64, h * 64:(h + 1) * 64], C("ones")[:, 0:1], True, True, w=[PS], r=[TMP2, CST])
                    nds = 4
                else:
                    for h in range(4):
                        MM(PS[0:64, h * NS:(h + 1) * NS], TMP2[0:64, h * 64:(h + 1) * 64], C("seqoh"), True, True,
                           w=[PS], r=[TMP2, CST])
                    nds = 4 * NS
                ACT(DS[0:64, 0:nds], PS[0:64, 0:nds], AF.Exp, w=[DS], r=[PS])
                ACT(QN[0:64, 0:256], PA[0:64, 0:256], AF.Exp, w=[QN], r=[PA])
                ACT(LM[0:64, 0:256], PA[0:64, 0:256], AF.Exp, w=[LM], r=[PA], scale=-1.0)
                ACT(Yb[0:64, 0:256], PD[0:64, 0:256], AF.Exp, w=[Yb], r=[PD])
                ACT(TMP[0:64, 0:256], PO[0:64, 0:256], AF.Exp, w=[TMP], r=[PO])
                TT(dve, QN[0:64, 0:256], QN[0:64, 0:256], q_ap, ALU.mult, w=[QN], r=[QN, U])
                TT(dve, LM[0:64, 0:256], LM[0:64, 0:256], KN[0:64, 0:256], ALU.mult, w=[LM], r=[LM, KN])
                TT(dve, Yb[0:64, 0:256], Yb[0:64, 0:256], q_ap, ALU.mult, w=[Yb], r=[Yb, U])
                TT(dve, ROW(KP[0:64, 0:256]), TMP[0:64, 0:256], KN[0:64, 0:256], ALU.mult, w=[KP], r=[TMP, KN])
                tp4([QN[0:64, h * 64:(h + 1) * 64] for h in range(4)], QT, [QN], ROW)
                tp4([LM[0:64, h * 64:(h + 1) * 64] for h in range(4)], KT, [LM], ROW)
                tp4([Yb[0:64, h * 64:(h + 1) * 64] for h in range(4)], QPT, [Yb], ROW)
                CPA(ROW(VA[0:64, 0:256]), v_ap, w=[VA], r=[U])
                qt = lambda h: QT[0:64, h * 64:(h + 1) * 64]
                kt = lambda h: KT[0:64, h * 64:(h + 1) * 64]
                CPV(v3(DT[0:64, 0:256], 4), bm(Mk("incl"), 4), w=[DT], r=[CST])
                DTm = DT[0:64, 0:256]
                DTtk = DT
            else:
                q_ap, k_ap, v_ap, og_ap = U[0:64, 0:256], U[0:64, 256:512], U[0:64, 512:768], U[0:64, 768:1024]
                TPo(PS[0:4, 0:64], U[0:64, 1024:1028], ident(64), w=[PS], r=[U, CST])
                TPo(PS[0:4, 64:128], U[0:64, 1028:1032], ident(64), w=[PS], r=[U, CST])
                TS(pool, KN[0:64, 0:256], k_ap, 0.125, ALU.mult, w=[KN], r=[U])
                tp4([U[0:64, h * 64:(h + 1) * 64] for h in range(4)], QT, [U], ROW)
                tp4([KN[0:64, h * 64:(h + 1) * 64] for h in range(4)], KT, [KN], ROW)
                VA3 = ROW(VA[0:64, 0:264].rearrange("p (h d) -> p h d", h=4))
                CPV(VA3[:, :, 0:64], v3(v_ap, 4), w=[VA], r=[U])
                CPV(VA3[:, :, 64:65], C("ones")[:, 0:1].unsqueeze(1).broadcast_to([64, 4, 1]), w=[VA], r=[CST])
                CPV(VA3[:, :, 65:66], C("zeros")[:, 0:1].unsqueeze(1).broadcast_to([64, 4, 1]), w=[VA], r=[CST])
                TS(dve, GI[0:4, 0:64], PS[0:4, 0:64], PCL[0:4, 50:51], ALU.add, w=[GI], r=[PS, PCL])
                ACT(GF[0:4, 0:64], PS[0:4, 64:128], AF.Exp, w=[GF], r=[PS, NEGF], bias=NEGF[0:4, 0:1], scale=-1.0)
                ACT(GF[0:4, 0:64], GF[0:4, 0:64], AF.Ln, w=[GF], r=[GF], bias=1.0)
                TS(dve, LF[0:4, 0:64], GF[0:4, 0:64], -1.0, ALU.mult, w=[LF], r=[GF])
                if not sample:
                    K.op(dve, lambda e: e.tensor_tensor_scan(out=BC[0:4, 1:65], data0=ONESROW[0:4, 0:64], data1=LF[0:4, 0:64],
                                                             initial=BC[0:4, 0:1], op0=ALU.mult, op1=ALU.add),
                         w=[BC], r=[BC, LF, ONESROW])
                    TT(dve, AA[0:4, 0:64], GI[0:4, 0:64], BC[0:4, 1:65], ALU.subtract, w=[AA], r=[GI, BC])
                    K.op(dve, lambda e: e.tensor_tensor_scan(out=MGX[0:4, 1:65], data0=AA[0:4, 0:64], data1=AA[0:4, 0:64],
                                                             initial=MGX[0:4, 0:1], op0=ALU.max, op1=ALU.max),
                         w=[MGX], r=[MGX, AA])
                    mg = MGX[0:4, 1:65]
                    mprev = MGX[0:4, 0:1].broadcast_to([4, 64])
                    mend = MGX[0:4, 64:65].broadcast_to([4, 64])
                    TT(dve, EXS[0:4, 0:64], mprev, mg, ALU.subtract, w=[EXS], r=[MGX])
                    TT(dve, EXK[0:4, 0:64], AA[0:4, 0:64], mend, ALU.subtract, w=[EXK], r=[AA, MGX])
                    TT(dve, EXM[0:4, 0:64], BC[0:4, 1:65], mg, ALU.add, w=[EXM], r=[BC, MGX])
                    TS(dve, NMG[0:4, 0:64], mg, -1.0, ALU.mult, w=[NMG], r=[MGX])
                    TT(dve, DG[0:4, 8:9], MGX[0:4, 0:1], MGX[0:4, 64:65], ALU.subtract, w=[DG], r=[MGX])
                    TS(dve, DG[0:4, 0:4], C("I4", 4), DG[0:4, 8:9], ALU.mult, w=[DG], r=[DG, CST])
                    ndg = 4
                    if c == NCH - 2:
                        CPV(MFIN[0:4, 0:1], EXM[0:4, 63:64], w=[MFIN], r=[EXM])
                        K.dma(sp, p_m_o[l], MFIN[0:4, 0:1], r=[MFIN])
                else:
                    L3 = LF[0:4, 0:64].rearrange("p (s j) -> p s j", j=4)
                    B3 = BC[0:4, 0:64].rearrange("p (s j) -> p s j", j=4)
                    A3 = AA[0:4, 0:64].rearrange("p (s j) -> p s j", j=4)
                    G3 = GI[0:4, 0:64].rearrange("p (s j) -> p s j", j=4)
                    M3 = MGX[0:4, 0:64].rearrange("p (s j) -> p s j", j=4)
                    CPV(B3[:, :, 0:1], L3[:, :, 0:1], w=[BC], r=[LF])
                    for j in range(1, 4):
                        TT(dve, B3[:, :, j:j + 1], B3[:, :, j - 1:j], L3[:, :, j:j + 1], ALU.add, w=[BC], r=[BC, LF])
                    TT(dve, AA[0:4, 0:64], GI[0:4, 0:64], BC[0:4, 0:64], ALU.subtract, w=[AA], r=[GI, BC])
                    TT(dve, M3[:, :, 0:1], A3[:, :, 0:1], MINI[0:4, 0:NS].unsqueeze(2), ALU.max, w=[MGX], r=[AA, MINI])
                    for j in range(1, 4):
                        TT(dve, M3[:, :, j:j + 1], M3[:, :, j - 1:j], A3[:, :, j:j + 1], ALU.max, w=[MGX], r=[MGX, AA])
                    mg = MGX[0:4, 0:64]
                    mprev3 = MINI[0:4, 0:NS].unsqueeze(2).broadcast_to([4, NS, 4])
                    mend3 = M3[:, :, 3:4].broadcast_to([4, NS, 4])
                    TT(dve, EXS[0:4, 0:64].rearrange("p (s j) -> p s j", j=4), mprev3, M3, ALU.subtract, w=[EXS], r=[MGX, MINI])
                    TT(dve, EXK[0:4, 0:64].rearrange("p (s j) -> p s j", j=4), A3, mend3, ALU.subtract, w=[EXK], r=[AA, MGX])
                    TT(dve, EXM[0:4, 0:64], BC[0:4, 0:64], mg, ALU.add, w=[EXM], r=[BC, MGX])
                    TS(dve, NMG[0:4, 0:64], mg, -1.0, ALU.mult, w=[NMG], r=[MGX])
                    TT(dve, DG[0:4, 0:NS], MINI[0:4, 0:NS], M3[:, :, 3], ALU.subtract, w=[DG], r=[MINI, MGX])
                    TT(dve, AM[0:4, 0:4 * NS].rearrange("p (h s) -> p h s", h=4), bm(DG[0:4, 0:NS], 4),
                       b3(C("I4", 4), NS), ALU.mult, w=[AM], r=[DG, CST])
                    ndg = 4 * NS
                    CPV(MFIN[0:4, 0:NS], EXM[0:4, 0:64].rearrange("p (s j) -> p s j", j=4)[:, :, 3], w=[MFIN], r=[EXM])
                    K.dma(sp, o_m_o[l], MFIN[0:4, 0:NS], r=[MFIN])
                MM(PS[0:64, 128:132], EXS[0:4, 0:64], C("I4", 4), True, True, w=[PS], r=[EXS, CST])
                MM(PS[0:64, 132:136], EXK[0:4, 0:64], C("I4", 4), True, True, w=[PS], r=[EXK, CST])
                MM(PS[0:64, 136:140], EXM[0:4, 0:64], C("I4", 4), True, True, w=[PS], r=[EXM, CST])
                ACT(SM[0:64, 0:8], PS[0:64, 128:136], AF.Exp, w=[SM], r=[PS])
                ACT(SM[0:64, 8:12], PS[0:64, 136:140], AF.Exp, w=[SM], r=[PS], scale=-1.0)
                sinit, kw_, emm = SM[0:64, 0:4], SM[0:64, 4:8], SM[0:64, 8:12]
                if not sample:
                    MM(PS[0:64, 160:164], C("ones", 4), DG[0:4, 0:4], True, True, w=[PS], r=[CST, DG])
                else:
                    MM(PS[0:64, 160:160 + ndg], C("ones", 4), AM[0:4, 0:ndg], True, True, w=[PS], r=[CST, AM])
                ACT(DS[0:64, 0:ndg], PS[0:64, 160:160 + ndg], AF.Exp, w=[DS], r=[PS])
                for h in range(4):
                    TS(dve, TMP2[0:4, h * 64:(h + 1) * 64], AA[0:4, 0:64], C("I4", 4)[:, h:h + 1], ALU.mult, w=[TMP2], r=[AA, CST])
                MM(PD[0:64, 0:256], ident(64), Mk("neg4"), True, False, w=[PD], r=[CST])
                for h in range(4):
                    sl = slice(h * 64, (h + 1) * 64)
                    MM(PD[0:64, sl], TMP2[0:4, sl], C("ones", 4), False, False, w=[PD], r=[TMP2, CST])
                    MM(PD[0:64, sl], C("oh4", 4)[:, sl], NMG[0:4, 0:64], False, h == 3,
                       w=[PD], r=[NMG, CST])
                ACT(DT[0:64, 0:256], PD[0:64, 0:256], AF.Exp, w=[DT], r=[PD])
                TT(dve, v3(TMP[0:64, 0:256], 4), v3(q_ap, 4), b3(sinit, 64), ALU.mult, w=[TMP], r=[U, SM])
                tp4([TMP[0:64, h * 64:(h + 1) * 64] for h in range(4)], QPT, [TMP], ROW)
                TT(dve, ROW(v3(KP[0:64, 0:256], 4)), v3(KN[0:64, 0:256], 4), b3(kw_, 64), ALU.mult, w=[KP], r=[KN, SM])
                qt = lambda h: QT[0:64, h * 64:(h + 1) * 64]
                kt = lambda h: KT[0:64, h * 64:(h + 1) * 64]
                DTm = DT[0:64, 0:256]
                DTtk = DT
                if not sample:
                    CPV(MGX[0:4, 0:1], MGX[0:4, 64:65], w=[MGX], r=[MGX])
                    CPV(BC[0:4, 0:1], BC[0:4, 64:65], w=[BC], r=[BC])

            for h in range(4):
                MM(PA[0:64, h * 64:(h + 1) * 64], RO(kt(h)), RO(qt(h)), True, True, w=[PA], r=[KT, QT])
            TT(dve, ROW(ATT[0:64, 0:256]), PA[0:64, 0:256], DTm, ALU.mult, w=[ATT], r=[PA, DTtk])

            nseq = NS if sample else 1
            hw = 4 * dvw
            ssm_in = s_ssm_d[m]
            ssm_out = (o_ssm_o if sample else p_ssm_o)[m]

            def load_state(s):
                St = SSs[s % 2]
                S3 = St[0:64, 0:hw].rearrange("p (h d) -> p h d", h=4)
                K.dma(sp, S3[:, :, 0:64], ssm_in[l, s].rearrange("h k v -> k h v"), w=[St])
                if m == 3:
                    K.dma(sp, SNV[0:4, 0:64], s_nvec_d[l, 4 * s:4 * s + 4, :], w=[SNV])
                    TPo(PX[0:64, 256:260], SNV[0:4, 0:64], ident(4), w=[PX], r=[SNV, CST])
                    CPA(S3[:, :, 64], PX[0:64, 256:260], w=[St], r=[PX])
                    CPV(S3[:, :, 65:66], C("zeros")[:, 0:1].unsqueeze(1).broadcast_to([64, 4, 1]), w=[St], r=[CST])
                return St

            def store_state(s, St):
                S3 = St[0:64, 0:hw].rearrange("p (h d) -> p h d", h=4)
                if sample:
                    K.dma(sp, ssm_out[l, s].rearrange("h k v -> k h v"), S3[:, :, 0:64], r=[St])
                else:
                    K.dma(sp, ssm_out[l].rearrange("h k v -> k h v"), S3[:, :, 0:64], r=[St])
                if m == 3:
                    CPV(TMP2[0:64, 0:4], S3[:, :, 64], w=[TMP2], r=[St])
                    TPo(PX[0:4, 264:328], TMP2[0:64, 0:4], ident(64), w=[PX], r=[TMP2, CST])
                    CPA(SNV[0:4, 0:64], PX[0:4, 264:328], w=[SNV], r=[PX])
                    if sample:
                        K.dma(sp, o_nvec_o[l, 4 * s:4 * s + 4, :], SNV[0:4, 0:64], r=[SNV])
                    else:
                        K.dma(sp, p_nvec_o[l], SNV[0:4, 0:64], r=[SNV])

            def valh(h):
                if m == 0:
                    return RO(VA[0:64, h * 64:(h + 1) * 64])
                if m == 1:
                    return RO(VA[0:64, h * 64:(h + 1) * 64])
                if m == 2:
                    return RO(VA[0:64, h * 64:(h + 1) * 64])
                return RO(VA[0:64, h * 66:(h + 1) * 66])
            valtk = VA

            if not sample:
                St = SPs
                S3 = St[0:64, 0:hw].rearrange("p (h d) -> p h d", h=4)
                if m == 0:
                    for h in range(4):
                        MM(PO[0:64, h * 64:(h + 1) * 64], WT[0:64, h * 64:(h + 1) * 64], S3[:, h, 0:64], True, True,
                           w=[PO], r=[WT, St])
                    TT(dve, ROW(v3(VA[0:64, 0:256], 4)), UW3[:, :, 0:64], v3(PO[0:64, 0:256], 4), ALU.subtract, w=[VA], r=[UW, PO])
                for h in range(4):
                    MM(PO[0:64, h * dvw:(h + 1) * dvw], RO(ATT[0:64, h * 64:(h + 1) * 64]), valh(h), True, False,
                       w=[PO], r=[ATT, valtk])
                    MM(PO[0:64, h * dvw:(h + 1) * dvw], QPT[0:64, h * 64:(h + 1) * 64], S3[:, h, :], False, True,
                       w=[PO], r=[QPT, St])
                CPA(Ob[0:64, 0:hw], PO[0:64, 0:hw], w=[Ob], r=[PO])
                for h in range(4):
                    MM(PS[0:64, h * dvw:(h + 1) * dvw], RO(KP[0:64, h * 64:(h + 1) * 64]), valh(h), True, True,
                       w=[PS], r=[KP, valtk])
                TT(dve, S3, S3, b3(DS[0:64, 0:4], dvw), ALU.mult, w=[St], r=[St, DS])
                TT(dve, St[0:64, 0:hw], St[0:64, 0:hw], PS[0:64, 0:hw], ALU.add, w=[St], r=[St, PS])
                if c == NCH - 2:
                    store_state(0, St)
            else:
                colm = C("colmask")
                seqoh = C("seqoh")
                zeros = C("zeros")

                def build_masked(SRC, DST, sg):
                    cm4 = colm[:, sg * 128:(sg + 1) * 128].rearrange("p (s r) -> p s r", s=2)
                    TT(dve, DST[0:64, 0:512].rearrange("p (h s r) -> p h s r", h=4, s=2),
                       v3(SRC[0:64, 0:256], 4).unsqueeze(2).broadcast_to([64, 4, 2, 64]),
                       cm4.unsqueeze(1).broadcast_to([64, 4, 2, 64]), ALU.mult, w=[DST], r=[SRC, CST])

                def upd_state(s, St, S3):
                    vsrc = VA[0:64, 0:hw]
                    TS(dve, VM[0:64, 0:hw], vsrc, seqoh[:, s:s + 1], ALU.mult, w=[VM], r=[valtk, CST])
                    for h in range(4):
                        MM(PS[0:64, h * dvw:(h + 1) * dvw], KP[0:64, h * 64:(h + 1) * 64],
                           VM[0:64, h * dvw:(h + 1) * dvw], True, True, w=[PS], r=[KP, VM])
                    dss = DS[0:64, 0:4 * NS].rearrange("p (h s) -> p h s", h=4)[:, :, s]
                    TT(dve, S3, S3, b3(dss, dvw), ALU.mult, w=[St], r=[St, DS])
                    TT(dve, St[0:64, 0:hw], St[0:64, 0:hw], PS[0:64, 0:hw], ALU.add, w=[St], r=[St, PS])
                    store_state(s, St)

                obase = 0
                if m == 0:
                    MM(PO[0:64, 0:256], ident(64), zeros[:, 0:256], True, False, w=[PO], r=[CST])
                    for sg in range(NS // 2):
                        build_masked(WT, WM_, sg)
                        for s4 in range(2):
                            s = sg * 2 + s4
                            St = load_state(s)
                            S3 = St[0:64, 0:hw].rearrange("p (h d) -> p h d", h=4)
                            for h in range(4):
                                MM(PO[0:64, h * 64:(h + 1) * 64],
                                   WM_[0:64, (h * 2 + s4) * 64:(h * 2 + s4 + 1) * 64], S3[:, h, 0:64],
                                   False, (s == NS - 1 and h == 3), w=[PO], r=[WM_, St])
                    TT(dve, ROW(v3(VA[0:64, 0:256], 4)), UW3[:, :, 0:64], v3(PO[0:64, 0:256], 4), ALU.subtract, w=[VA], r=[UW, PO])
                    obase = 256
                MM(PO[0:64, obase:obase + hw], ident(64), zeros[:, 0:hw], True, False, w=[PO], r=[CST])
                for h in range(4):
                    MM(PO[0:64, obase + h * dvw:obase + (h + 1) * dvw], ATT[0:64, h * 64:(h + 1) * 64], valh(h), False, False,
                       w=[PO], r=[ATT, valtk])
                for sg in range(NS // 2):
                    build_masked(QPT, QM, sg)
                    for s4 in range(2):
                        s = sg * 2 + s4
                        St = load_state(s)
                        S3 = St[0:64, 0:hw].rearrange("p (h d) -> p h d", h=4)
                        for h in range(4):
                            MM(PO[0:64, obase + h * dvw:obase + (h + 1) * dvw],
                               QM[0:64, (h * 2 + s4) * 64:(h * 2 + s4 + 1) * 64], S3[:, h, :],
                               False, (s == NS - 1 and h == 3), w=[PO], r=[QM, St])
                        upd_state(s, St, S3)
                CPA(Ob[0:64, 0:hw], PO[0:64, obase:obase + hw], w=[Ob], r=[PO])

            yield 'RM'
            O3 = Ob[0:64, 0:hw].rearrange("p (h d) -> p h d", h=4)
            gname = ("norm_a_g", "norm_b_g", "norm_c_g", "norm_d_g")[m]
            gsrc = {0: None, 1: None, 2: None, 3: None}
            if m in (0, 1):
                SIG(TMP2[0:64, 0:256], z_ap, w=[TMP2], r=[U])
                TT(pool, TMP2[0:64, 0:256], TMP2[0:64, 0:256], z_ap, ALU.mult, w=[TMP2], r=[TMP2, U])
            elif m == 2:
                SIG(TMP2[0:64, 0:256], g_ap, w=[TMP2], r=[U])
            else:
                SIG(TMP2[0:64, 0:256], og_ap, w=[TMP2], r=[U])
            if m != 1:
                TT(pool, TMP2[0:64, 0:256], TMP2[0:64, 0:256], prow(gname), ALU.mult, w=[TMP2], r=[TMP2, PRW])
            yield 'step'
            if m == 3:
                ACT(SM2[0:64, 32:36], O3[:, :, 64], AF.Abs, w=[SM2], r=[Ob])
                TT(dve, SM2[0:64, 32:36], SM2[0:64, 32:36], emm, ALU.max, w=[SM2], r=[SM2, SM])
                yield 'step'
                RECIP(SM2[0:64, 36:40], SM2[0:64, 32:36], w=[SM2], r=[SM2])
                TT(dve, v3(TMP[0:64, 0:256], 4), O3[:, :, 0:64], b3(SM2[0:64, 36:40], 64), ALU.mult, w=[TMP], r=[Ob, SM2])
                yield 'step'
                osrc, otk = TMP[0:64, 0:256], TMP
            elif m == 1:
                TT(dve, v3(TMP[0:64, 0:256], 4), v3(x_ap, 4), b3(prow("d_skip_b"), 64), ALU.mult, w=[TMP], r=[U, PRW])
                TT(dve, TMP[0:64, 0:256], TMP[0:64, 0:256], Ob[0:64, 0:256], ALU.add, w=[TMP], r=[TMP, Ob])
                yield 'step'
                TT(dve, TMP[0:64, 0:256], TMP[0:64, 0:256], TMP2[0:64, 0:256], ALU.mult, w=[TMP], r=[TMP, TMP2])
                osrc, otk = TMP[0:64, 0:256], TMP
            else:
                osrc, otk = Ob[0:64, 0:256], Ob
            ng = 2 if m == 1 else 4
            gw = 256 // ng
            TT(dve, TMP[0:64, 256:512], osrc, osrc, ALU.mult, w=[TMP], r=[otk, TMP])
            yield 'step'
            K.op(dve, lambda e: e.tensor_reduce(out=SM2[0:64, 40:40 + ng], in_=v3(TMP[0:64, 256:512], ng), axis=AX.X, op=ALU.add),
                 w=[SM2], r=[TMP])
            TS(dve, SM2[0:64, 40:40 + ng], SM2[0:64, 40:40 + ng], 1.0 / gw, ALU.mult, w=[SM2], r=[SM2], s2=EPS, op1=ALU.add)
            yield 'step'
            RSQ(SM2[0:64, 44:44 + ng], SM2[0:64, 40:40 + ng], w=[SM2], r=[SM2])
            yield 'step'
            TT(dve, v3(Yb[0:64, 0:256], ng), v3(osrc, ng), b3(SM2[0:64, 44:44 + ng], gw), ALU.mult, w=[Yb], r=[otk, SM2])
            if m == 1:
                TT(dve, Yb[0:64, 0:256], Yb[0:64, 0:256], prow(gname), ALU.mult, w=[Yb], r=[Yb, PRW])
            else:
                TT(dve, Yb[0:64, 0:256], Yb[0:64, 0:256], TMP2[0:64, 0:256], ALU.mult, w=[Yb], r=[Yb, TMP2])
            yield 'step'
            for b in range(2):
                TPo(PD[0:128, b * 64:(b + 1) * 64], Yb[0:64, b * 128:(b + 1) * 128], ident(64), w=[PD], r=[Yb, CST])
            PW = (PA, PO)
            yield 'step'
            if c == 0 and FP32_MIX:
                CPA(YT32[:, 0:128], PD[0:128, 0:128], w=[YT32], r=[PD])
                for hf in range(2):
                    for b in range(2):
                        MM(PW[hf][pb:pb + 64, 0:512], YT32[:, b * 64:(b + 1) * 64], WST[b][:, hf * 512:(hf + 1) * 512],
                           b == 0, b == 1, w=[PW[hf]], r=[YT32, WST[b]])
            else:
                CPA(YT[:, 0:128], PD[0:128, 0:128], w=[YT], r=[PD])
                yield 'step'
                for hf in range(2):
                    for b in range(2):
                        MM(PW[hf][pb:pb + 64, 0:512], YT[:, b * 64:(b + 1) * 64], Wo3[:, b, hf * 512:(hf + 1) * 512],
                           b == 0, b == 1, w=[PW[hf]], r=[YT, Wo])
            for hf in range(2):
                TT(dve, Rt[i_tile][pb:pb + 64, hf * 512:(hf + 1) * 512], Rt[i_tile][pb:pb + 64, hf * 512:(hf + 1) * 512],
                   PW[hf][pb:pb + 64, 0:512], ALU.add, w=[Rt[i_tile]], r=[Rt[i_tile], PW[hf]])

        gens = [chunk_body(c) for c in range(NCH)]

        def run_until(g, marker):
            for v in g:
                if v == marker:
                    return

        def step(g, end):
            try:
                v = next(g)
            except StopIteration:
                return False
            return v != end

        if PIPE:
            run_until(gens[0], 'P2')
            for c in range(NCH):
                run_until(gens[c], 'RM')
                if c + 1 < NCH:
                    run_until(gens[c + 1], 'P1')
                    if PIPE2 and c + 1 != 1 and c + 1 != NCH - 1:
                        la = lb = True
                        while la or lb:
                            if la:
                                la = step(gens[c], None)
                            if lb:
                                lb = step(gens[c + 1], 'P2')
                    else:
                        run_until(gens[c], None)
                        run_until(gens[c + 1], 'P2')
                else:
                    run_until(gens[c], None)
        else:
            for c in range(NCH):
                run_until(gens[c], None)

    aoff[0] = 1024
    Wu = [ar(8 * 512 // 2, BF16)] * 2
    Wd = [ar(4 * 1024 // 2, BF16)] * 2
    RL = ar(512)
    ACB = ar(4 * 512 // 2, BF16)
    WU32 = ar(4096)
    WD32 = ar(4096)
    A32 = ar(256)
    RL0 = ar(64)
    assert aoff[0] <= ARENA_COLS

    def ffn(l):
        PUP = [PX, PD, PA, PO]
        dsel = [0]
        ngr = (T + 511) // 512
        groups = [(g * 512, min(512, T - g * 512)) for g in range(ngr)] + [(T, NSK)]
        XF3 = v3(XF[:, 0:512], 8)
        for fc in range(8):
            wu, wd = Wu[fc % 2], Wd[fc % 2]
            wu3 = wu[:, 0:4096].rearrange("p (k c) -> p k c", k=8)
            wd3 = wd[:, 0:4096].rearrange("p (k c) -> p k c", k=4)
            wu32 = v3(WU32[:, 0:4096], 8)
            wd32 = v3(WD32[:, 0:4096], 4)
            for k in range(8):
                if not NOSTG:
                    K.dma(stq, wu32[:, k, :], w_up[l, k * 128:(k + 1) * 128, fc * 512:(fc + 1) * 512], w=[WU32])
            for j in range(4):
                if not NOSTG:
                    K.dma(stq, wd32[:, j, :], w_down[l, fc * 512 + j * 128:fc * 512 + (j + 1) * 128, :], w=[WD32])
            if STAGE:
                K.op(pool, lambda e: e.tensor_copy(out=wu[:, 0:4096], in_=WU32[:, 0:4096]), w=[wu], r=[WU32])
                K.op(act, lambda e: e.copy(out=wd[:, 0:4096], in_=WD32[:, 0:4096]), w=[wd], r=[WD32])
            else:
                for k in range(8):
                    K.dma(pool, wu3[:, k, :], w_up[l, k * 128:(k + 1) * 128, fc * 512:(fc + 1) * 512], w=[wu])
                for j in range(4):
                    K.dma(pool, wd3[:, j, :], w_down[l, fc * 512 + j * 128:fc * 512 + (j + 1) * 128, :], w=[wd])
            a32 = v3(A32[:, 0:256], 4)
            for j in range(4 if FP32_FFN else 0):
                for k in range(8):
                    MM(PS[0:128, j * 64:(j + 1) * 64], wu32[:, k, j * 128:(j + 1) * 128], XF3[:, k, :], k == 0, k == 7,
                       w=[PS], r=[WU32, XF])
                ACT(RL0[:, 0:64], PS[0:128, j * 64:(j + 1) * 64], AF.Relu, w=[RL0], r=[PS])
                TT(dve, a32[:, j, :], RL0[:, 0:64], RL0[:, 0:64], ALU.mult, w=[A32], r=[RL0])
            for hf in range(2 if FP32_FFN else 0):
                for j in range(4):
                    MM(PJ[0:64, hf * 512:(hf + 1) * 512], a32[:, j, :], wd32[:, j, hf * 512:(hf + 1) * 512], j == 0, j == 3,
                       w=[PJ], r=[A32, WD32])
            if FP32_FFN:
                TT(dve, Rt[0][0:64, :], Rt[0][0:64, :], PJ[0:64, 0:1024], ALU.add, w=[Rt[0]], r=[Rt[0], PJ])
            ac3 = ACB[:, 0:2048].rearrange("p (j n) -> p j n", j=4)
            for (g0, n) in groups:
                cks = [ATc[cc] for cc in range(g0 // 64, (g0 + n) // 64)]
                for j in range(4):
                    for k in range(8):
                        MM(PUP[j][:, 0:n], wu3[:, k, j * 128:(j + 1) * 128], AT_all[:, k, g0:g0 + n], k == 0, k == 7,
                           w=[PUP[j]], r=[wu] + cks)
                    ACT(RL[:, 0:n], PUP[j][:, 0:n], AF.Relu, w=[RL], r=[PUP[j]])
                    TT(pool, ac3[:, j, 0:n], RL[:, 0:n], RL[:, 0:n], ALU.mult, w=[ACB], r=[RL])
                for t0 in range(0, n, 128):
                    P = min(128, n - t0)
                    i = (g0 + t0) // 128
                    r0 = 64 if (i == 0 and FP32_FFN) else 0
                    dsel[0] += 1
                    for hf in range(2):
                        pw = (PJ, PJ) if dsel[0] % 2 == 0 else (PDUM, PS)
                        po = pw[hf][r0:P, hf * 512:(hf + 1) * 512] if pw[hf] is PJ else pw[hf][r0:P, 0:512]
                        for j in range(4):
                            MM(po, ac3[:, j, t0 + r0:t0 + P], wd3[:, j, hf * 512:(hf + 1) * 512],
                               j == 0, j == 3, w=[pw[hf]], r=[ACB, wd])
                        TT(dve, Rt[i][r0:P, hf * 512:(hf + 1) * 512], Rt[i][r0:P, hf * 512:(hf + 1) * 512], po, ALU.add,
                           w=[Rt[i]], r=[Rt[i], pw[hf]])

    for l in range(L):
        K.barrier()
        K.dma(sp, PRW[:, :], prow_d[l], w=[PRW])
        K.dma(sp, PCL[:, :], pcol_d[l], w=[PCL])
        ACT(NEA[0:64, 0:4], prow("a_log_a"), AF.Exp, w=[NEA], r=[PRW])
        ACT(NEA[0:64, 4:8], prow("a_log_b"), AF.Exp, w=[NEA], r=[PRW])
        TS(dve, NEA[0:64, 0:8], NEA[0:64, 0:8], -1.0, ALU.mult, w=[NEA], r=[NEA])
        l0, l1 = prow("lbl0"), prow("lbl1")
        TT(dve, TMP[0:64, 0:256], l0, l1, ALU.max, w=[TMP], r=[PRW])
        TT(dve, TMP[0:64, 256:512], l0, TMP[0:64, 0:256], ALU.subtract, w=[TMP], r=[PRW, TMP])
        TT(dve, TMP2[0:64, 0:256], l1, TMP[0:64, 0:256], ALU.subtract, w=[TMP2], r=[PRW, TMP])
        ACT(TMP[0:64, 256:512], TMP[0:64, 256:512], AF.Exp, w=[TMP], r=[TMP])
        ACT(TMP2[0:64, 0:256], TMP2[0:64, 0:256], AF.Exp, w=[TMP2], r=[TMP2])
        TT(dve, TMP[0:64, 0:256], TMP[0:64, 256:512], TMP2[0:64, 0:256], ALU.add, w=[TMP], r=[TMP, TMP2])
        RECIP(TMP[0:64, 0:256], TMP[0:64, 0:256], w=[TMP], r=[TMP])
        TT(dve, TMP[0:64, 256:512], TMP[0:64, 256:512], TMP[0:64, 0:256], ALU.mult, w=[TMP], r=[TMP])
        TT(dve, TMP2[0:64, 0:256], TMP2[0:64, 0:256], TMP[0:64, 0:256], ALU.mult, w=[TMP2], r=[TMP2, TMP])
        if l == 0:
            TT(dve, LBR[0:64, 0:256], TMP[0:64, 256:512], TMP[0:64, 256:512], ALU.subtract, w=[LBR], r=[TMP])
        else:
            TT(dve, LBR[0:64, 0:256], TMP[0:64, 256:512], TMP2[0:64, 0:256], ALU.add, w=[LBR], r=[TMP, TMP2])
            TT(dve, LBR[0:64, 0:256], LBR[0:64, 0:256], TMP[0:64, 256:512], ALU.subtract, w=[LBR], r=[LBR, TMP])
        dstate["on"] = True
        for m in [int(ch) for ch in '0123']:
            mixer_pass(l, m)
        dstate["on"] = False
        K.barrier()
        K.dma(sp, LNR[:, :], lnrow_d[1 + 2 * l], w=[LNR])
        for i in range(NTILES):
            P = tile_P(i)
            XT = XTs[i % 2]
            ln_tile(Rt[i], Rt[i][0:P, :], P, XT, XT[0:P, :], i % 2)
            to_AT(XT, XT, i, P)
            K.op(act, lambda e: e.mul(out=Rt[i][0:P, :], in_=XT[0:P, :], mul=ALPHA), w=[Rt[i]], r=[XT])
        K.barrier()
        ffn(l)
        K.barrier()
        K.dma(sp, LNR[:, :], lnrow_d[2 + 2 * l], w=[LNR])
        for i in range(NTILES):
            P = tile_P(i)
            XT = XTs[i % 2]
            ln_tile(Rt[i], Rt[i][0:P, :], P, XT, XT[0:P, :], i % 2)
            if l == L - 1:
                K.dma(sp, y_d[128 * i:128 * i + P, :], XT[0:P, :], r=[XT])
            else:
                to_AT(XT, XT, i, P)
                K.op(act, lambda e: e.mul(out=Rt[i][0:P, :], in_=XT[0:P, :], mul=ALPHA), w=[Rt[i]], r=[XT])
    K.finish()
    return nc, cst_np


def _rep(row, n=128):
    return np.ascontiguousarray(np.broadcast_to(np.asarray(row, np.float32)[None, :], (n, row.shape[-1])))


def make_in_maps(inp, T, NS, L, ncores, cst_np):
    f = lambda a: np.ascontiguousarray(np.asarray(a, np.float32))
    prow = np.zeros((L, 128, NPROW), np.float32)
    pcol = np.zeros((L, 128, NPCOL), np.float32)
    lnrow = np.zeros((2 * L + 1, 128, 2048), np.float32)
    lnrow[0] = _rep(np.concatenate([f(inp["emb_ln_g"]), f(inp["emb_ln_b"])]))
    lbl = f(inp["lb_logits_c"])
    for l in range(L):
        lnrow[1 + 2 * l] = _rep(np.concatenate([f(inp["ln1_g"])[l], f(inp["ln1_b"])[l]]))
        lnrow[2 + 2 * l] = _rep(np.concatenate([f(inp["ln2_g"])[l], f(inp["ln2_b"])[l]]))
        for name in ("norm_a_g", "norm_b_g", "norm_c_g", "norm_d_g", "a_log_a", "dt_bias_a", "a_log_b",
                     "dt_bias_b", "d_skip_b"):
            o, n = PROW[name]
            prow[l, :, o:o + n] = f(inp[name])[l][None, :]
        prow[l, :, PROW["lbl0"][0]:PROW["lbl0"][0] + 256] = lbl[0][None, :]
        prow[l, :, PROW["lbl1"][0]:PROW["lbl1"][0] + 256] = lbl[1][None, :]
        caw = f(inp["conv_a_w"])[l]
        pcol[l, :, 0:24] = caw.reshape(4, 6, 128).transpose(2, 1, 0).reshape(128, 24)
        pcol[l, :, 24:30] = f(inp["conv_a_b"])[l].reshape(6, 128).T
        cbw = f(inp["conv_b_w"])[l]
        pcol[l, :, 30:46] = cbw.reshape(4, 4, 128).transpose(2, 1, 0).reshape(128, 16)
        pcol[l, :, 46:50] = f(inp["conv_b_b"])[l].reshape(4, 128).T
        pcol[l, 0:4, 50] = f(inp["i_bias_d"])[l]
        pcol[l, 0:4, 51] = f(inp["f_bias_d"])[l]
    maps = []
    for c in range(ncores):
        sl = slice(NS * c, NS * (c + 1))
        m = {
            "xin": np.concatenate([f(inp["x_prompt"])[c], f(inp["x_sample"])[sl].reshape(NS * 4, 1024)], axis=0),
            "w_in": f(inp["w_in"]), "w_out": f(inp["w_out"]), "w_up": f(inp["w_up"]), "w_down": f(inp["w_down"]),
            "prow": prow, "lnrow": lnrow, "pcol": pcol, "cst": cst_np,
            "sa_conv": f(f(inp["state_a_conv"])[:, sl].reshape(L, NS * 3, 768)),
            "sb_conv": f(f(inp["state_b_conv"])[:, sl].reshape(L, NS * 3, 512)),
            "sa_ssm": f(f(inp["state_a_ssm"])[:, sl]), "sb_ssm": f(f(inp["state_b_ssm"])[:, sl]),
            "sc_ssm": f(f(inp["state_c_ssm"])[:, sl]), "sd_cmem": f(f(inp["state_d_cmem"])[:, sl]),
            "sd_nvec": f(f(inp["state_d_nvec"])[:, sl].reshape(L, NS * 4, 64)),
            "sd_m": f(f(inp["state_d_mstab"])[:, sl].transpose(0, 2, 1)),
        }
        maps.append(m)
    return maps


def gather(results, T, NS, L, ncores):
    R = results
    cat = lambda name: np.stack([np.asarray(R[c][name]) for c in range(ncores)], axis=1)
    y = [np.asarray(R[c]["y"]) for c in range(ncores)]
    y_prompt = np.stack([yy[:T] for yy in y], axis=0)
    y_sample = np.concatenate([yy[T:].reshape(NS, 4, 1024) for yy in y], axis=0)
    pa_conv = cat("pa_conv")
    pb_conv = cat("pb_conv")
    pss = [cat(n) for n in ("pa_ssm", "pb_ssm", "pc_ssm", "pd_cmem")]
    pd_nvec = cat("pd_nvec")
    pd_m = cat("pd_m")[..., 0]
    catS = lambda name, shp: np.concatenate([np.asarray(R[c][name]).reshape((L, NS) + shp) for c in range(ncores)], axis=1)
    oa_conv = catS("oa_conv", (3, 768))
    ob_conv = catS("ob_conv", (3, 512))
    oss = [catS(n, (4, 64, 64)) for n in ("oa_ssm", "ob_ssm", "oc_ssm", "od_cmem")]
    od_nvec = catS("od_nvec", (4, 64))
    od_m = np.concatenate([np.asarray(R[c]["od_m"]).transpose(0, 2, 1) for c in range(ncores)], axis=1)
    outs = (y_prompt, y_sample, pa_conv, pss[0], pb_conv, pss[1], pss[2], pss[3], pd_nvec, pd_m,
            oa_conv, oss[0], ob_conv, oss[1], oss[2], oss[3], od_nvec, od_m)
    return tuple(np.ascontiguousarray(o.astype(np.float32)) for o in outs)


def kernel(**inputs):
    T, NS, L, ncores = 2048, 16, 2, 8
    nc = bass.Bass("TRN2", target_bir_lowering=False)
    nc, cst_np = build(nc, T, NS, L)
    maps = make_in_maps(inputs, T, NS, L, ncores, cst_np)
    res = run_bass_kernel_spmd(nc, maps, core_ids=list(range(ncores)))
    return gather(res.results, T, NS, L, ncores)
```

```python
import math
import numpy as np
import concourse.bass as bass
import concourse.mybir as mybir
from concourse.bass_utils import run_bass_kernel_spmd

F32 = mybir.dt.float32
BF16 = mybir.dt.bfloat16
F32R = mybir.dt.float32r
ALU = mybir.AluOpType
AF = mybir.ActivationFunctionType
AX = mybir.AxisListType

SEM_ROT = 30000
import os
FP32_MIX = '1' == '1'
FP32_FFN = '1' == '1'
STQ = 'sp'
LOWP = '1' == '1'
DUMMY = int('0')
PIPE = '1' == '1'
PIPE2 = '1' == '1'
STAGE = '1' == '1'
NOSTG = '0' == '1'
EMBED_WAIT = True
EPS = 1e-6
DEPTH = 2
ALPHA = (2 * DEPTH) ** 0.25
IN_DIM = 3860
MIX_RANGE = [(0, 1032), (1032, 1804), (1804, 2828), (2828, 3860)]


class Tk:
    __slots__ = ("t", "w", "r", "ex")

    def __init__(self, t, ex=False):
        self.t = t
        self.w = {}
        self.r = {}
        self.ex = ex

    def __getitem__(self, idx):
        return self.t[idx]


class Eng:
    def __init__(self, K, name, h):
        self.K = K
        self.name = name
        self.h = h
        self.sems = []
        self.n = 0
        self.seen = {}
        self.newsem()

    def newsem(self):
        s = self.K.nc.alloc_semaphore(f"s_{self.name}_{len(self.sems)}")
        self.sems.append(s)
        self.n = 0


class KB:
    def __init__(self, nc, ndma_sems=32):
        self.nc = nc
        self.pe = Eng(self, "pe", nc.tensor)
        self.act = Eng(self, "act", nc.scalar)
        self.dve = Eng(self, "dve", nc.vector)
        self.pool = Eng(self, "pool", nc.gpsimd)
        self.sp = Eng(self, "sp", nc.sync)
        self.dsems = [[nc.alloc_semaphore(f"dma{i}"), 0] for i in range(ndma_sems)]
        self.dnext = 0
        self.ninst = 0
        self.defer = None

    def _wait(self, eng, ev):
        sem, val = ev
        key = id(sem)
        if eng.seen.get(key, 0) >= val:
            return
        if self.defer is not None:
            cur = self.defer.get(key)
            if cur is None or cur[1] < val:
                self.defer[key] = (sem, val)
            return
        eng.h.wait_ge(sem, val)
        eng.seen[key] = val

    def _flush(self, eng, keep_last):
        items = list(self.defer.values())
        self.defer = None
        last = None
        if keep_last and items:
            last = items.pop()
        for sem, val in items:
            eng.h.wait_ge(sem, val)
            eng.seen[id(sem)] = val
        return last

    def _deps(self, eng, w, r, skip_same=False):
        evs = []
        for b in r:
            evs.extend(b.w.values())
            if b.ex:
                evs.extend(b.r.values())
        for b in w:
            evs.extend(b.w.values())
            evs.extend(b.r.values())
        mysem = eng.sems[-1]
        for ev in evs:
            if skip_same and ev[0] is mysem:
                continue
            self._wait(eng, ev)

    def op(self, eng, fn, w=(), r=()):
        if eng.n >= SEM_ROT:
            eng.newsem()
        self.defer = {} if EMBED_WAIT else None
        self._deps(eng, w, r, skip_same=(eng is self.pe))
        last = self._flush(eng, True) if EMBED_WAIT else None
        ins = fn(eng.h)
        if last is not None:
            ins._wait_ge(last[0], last[1])
            eng.seen[id(last[0])] = last[1]
        eng.n += 1
        ins.then_inc(eng.sems[-1], 1)
        ev = (eng.sems[-1], eng.n)
        for b in w:
            b.w[id(ev[0])] = ev
        for b in r:
            b.r[id(ev[0])] = ev
        self.ninst += 1
        return ins

    def dma(self, eng, out_ap, in_ap, w=(), r=(), **kw):
        slot = self.dsems[self.dnext]
        self.dnext = (self.dnext + 1) % len(self.dsems)
        sem, val = slot
        if val > 0:
            self._wait(eng, (sem, val))
        self._deps(eng, w, r)
        ins = eng.h.dma_start(out=out_ap, in_=in_ap, **kw)
        slot[1] = val + 16
        ins.then_inc(sem, 16)
        ev = (sem, slot[1])
        for b in w:
            b.w[id(ev[0])] = ev
        for b in r:
            b.r[id(ev[0])] = ev
        self.ninst += 1
        return ev

    def barrier(self):
        engs = (self.pe, self.act, self.dve, self.pool, self.sp)
        for e in engs:
            for sem, val in self.dsems:
                if val > 0:
                    self._wait(e, (sem, val))
            for o in engs:
                if o is not e and o.n > 0:
                    self._wait(e, (o.sems[-1], o.n))

    def finish(self):
        for sem, val in self.dsems:
            if val > 0:
                self._wait(self.sp, (sem, val))
        for e in (self.pe, self.act, self.dve, self.pool):
            if e.n > 0:
                self._wait(self.sp, (e.sems[-1], e.n))


def _const_table(NS):
    cols = {}
    parts = []
    off = [0]

    def add(name, arr):
        arr = np.asarray(arr, np.float32)
        a = np.zeros((128, arr.shape[1]), np.float32)
        a[: arr.shape[0]] = arr
        cols[name] = (off[0], arr.shape[1])
        off[0] += arr.shape[1]
        parts.append(a)

    i64 = np.arange(64)
    t, r = i64[:, None], i64[None, :]
    add("ident", np.eye(128))
    add("ones", np.ones((128, 64)))
    add("I4", np.eye(4))
    add("oh4", np.repeat(np.eye(4), 64, axis=1))
    add("zeros", np.zeros((64, 264)))
    for sfx, same in (("", np.ones((64, 64), bool)), ("S", (t // 4) == (r // 4))):
        incl = (t <= r) & same
        gt = (t > r) & same
        add("incl" + sfx, incl)
        add("gt" + sfx, gt)
        add("neg4" + sfx, np.tile(np.where(incl, 0.0, -30000.0), (1, 4)))
        add("strict" + sfx, (t < r) & same)
    add("rel", ((t <= r).astype(np.float32) - (t <= 31).astype(np.float32)) * np.ones((64, 64)))
    seqoh = (i64[:, None] // 4) == np.arange(NS)[None, :]
    add("seqoh", seqoh)
    cm = ((i64[None, :] // 4) == np.arange(NS)[:, None]).astype(np.float32).reshape(1, NS * 64)
    add("colmask", np.tile(cm, (64, 1)))
    return np.concatenate(parts, axis=1), cols


PROW = {}
_o = 0
for _n, _w in (("norm_a_g", 256), ("norm_b_g", 256), ("norm_c_g", 256), ("norm_d_g", 256),
               ("lbl0", 256), ("lbl1", 256), ("a_log_a", 4), ("dt_bias_a", 4), ("a_log_b", 4),
               ("dt_bias_b", 4), ("d_skip_b", 4), ("pad", 4)):
    PROW[_n] = (_o, _w)
    _o += _w
NPROW = _o
NPCOL = 6 * 4 + 6 + 4 * 4 + 4 + 2


def build(nc, T=2048, NS=16, L=2):
    K = KB(nc)
    NSK = NS * 4
    assert NSK == 64 and T % 128 == 0
    NTP = T // 128
    NT = T + NSK
    NTILES = NTP + 1
    NCH = T // 64 + 1
    cst_np, CO = _const_table(NS)
    NCST = cst_np.shape[1]

    def din(name, shape):
        return nc.dram_tensor(name, list(shape), F32, kind="ExternalInput").ap()

    def dout(name, shape):
        return nc.dram_tensor(name, list(shape), F32, kind="ExternalOutput").ap()

    xin = din("xin", (NT, 1024))
    w_in = din("w_in", (L, 1024, IN_DIM))
    w_out = din("w_out", (L, 1024, 1024))
    w_up = din("w_up", (L, 1024, 4096))
    w_down = din("w_down", (L, 4096, 1024))
    prow_d = din("prow", (L, 128, NPROW))
    lnrow_d = din("lnrow", (2 * L + 1, 128, 2048))
    pcol_d = din("pcol", (L, 128, NPCOL))
    cst_d = din("cst", (128, NCST))
    s_conv_d = [din("sa_conv", (L, NS * 3, 768)), din("sb_conv", (L, NS * 3, 512))]
    s_ssm_d = [din("sa_ssm", (L, NS, 4, 64, 64)), din("sb_ssm", (L, NS, 4, 64, 64)),
               din("sc_ssm", (L, NS, 4, 64, 64)), din("sd_cmem", (L, NS, 4, 64, 64))]
    s_nvec_d = din("sd_nvec", (L, NS * 4, 64))
    s_m_d = din("sd_m", (L, 4, NS))

    y_d = dout("y", (NT, 1024))
    p_conv_o = [dout("pa_conv", (L, 3, 768)), dout("pb_conv", (L, 3, 512))]
    p_ssm_o = [dout("pa_ssm", (L, 4, 64, 64)), dout("pb_ssm", (L, 4, 64, 64)),
               dout("pc_ssm", (L, 4, 64, 64)), dout("pd_cmem", (L, 4, 64, 64))]
    p_nvec_o = dout("pd_nvec", (L, 4, 64))
    p_m_o = dout("pd_m", (L, 4, 1))
    o_conv_o = [dout("oa_conv", (L, NS * 3, 768)), dout("ob_conv", (L, NS * 3, 512))]
    o_ssm_o = [dout("oa_ssm", (L, NS, 4, 64, 64)), dout("ob_ssm", (L, NS, 4, 64, 64)),
               dout("oc_ssm", (L, NS, 4, 64, 64)), dout("od_cmem", (L, NS, 4, 64, 64))]
    o_nvec_o = dout("od_nvec", (L, NS * 4, 64))
    o_m_o = dout("od_m", (L, 4, NS))

    def sb(name, shape, dt=F32):
        return Tk(nc.alloc_sbuf_tensor(name, list(shape), dt))

    Rt = [Tk(None) for _ in range(NTILES)]
    R_all = nc.alloc_sbuf_tensor("R", [128, NTILES, 1024], F32)
    for i in range(NTILES):
        Rt[i].t = R_all[:, i, :]
    AT_all = nc.alloc_sbuf_tensor("AT", [128, 8, NT], BF16)
    ATc = [Tk(AT_all[:, :, 64 * c:64 * c + 64]) for c in range(NCH)]
    CST = sb("CST", (128, NCST))
    PRW = sb("PRW", (128, NPROW))
    LNR = sb("LNR", (128, 2048))
    PCL = sb("PCL", (128, NPCOL))
    XF = sb("XF", (128, 8 * 64))
    ARENA_COLS = 20400 - 3856
    arena_t = nc.alloc_sbuf_tensor("ARENA", [128, ARENA_COLS], F32)
    aoff = [0]

    def ar(ncols, dt=F32, shape=None):
        ncols = (ncols + 7) // 8 * 8
        assert aoff[0] + ncols <= ARENA_COLS, ("arena overflow", aoff[0], ncols)
        ap = arena_t[:, aoff[0]:aoff[0] + ncols]
        aoff[0] += ncols
        if dt == BF16:
            ap = ap.bitcast(BF16)
        return Tk(ap)

    OPB_COLS = 3856
    opb_t = nc.alloc_sbuf_tensor("OPB", [128, OPB_COLS], F32)
    boff = [0]

    def arb(ncols):
        ncols = (ncols + 7) // 8 * 8
        assert boff[0] + ncols <= OPB_COLS, ("opb overflow", boff[0], ncols)
        ap = opb_t[:, boff[0]:boff[0] + ncols]
        boff[0] += ncols
        return Tk(ap)

    PJ = Tk(nc.alloc_psum_tensor("PJ", [128, 1024], F32), ex=True)
    PDUM = Tk(nc.alloc_psum_tensor("PDUM", [128, 512], F32), ex=True)
    PX = Tk(nc.alloc_psum_tensor("PX", [128, 512], F32), ex=True)
    PD = Tk(nc.alloc_psum_tensor("PD", [128, 512], F32), ex=True)
    PA = Tk(nc.alloc_psum_tensor("PA", [128, 512], F32), ex=True)
    PO = Tk(nc.alloc_psum_tensor("PO", [128, 512], F32), ex=True)
    PS = Tk(nc.alloc_psum_tensor("PS", [128, 512], F32), ex=True)

    pe, act, dve, pool, sp = K.pe, K.act, K.dve, K.pool, K.sp
    stq = {'sp': sp, 'pool': pool, 'act': act}[STQ]

    def TT(eng, out, a, b, op, w, r):
        K.op(eng, lambda e: e.tensor_tensor(out=out, in0=a, in1=b, op=op), w=w, r=r)

    def TS(eng, out, a, s1, op0, w, r, s2=None, op1=None):
        if op1 is None:
            K.op(eng, lambda e: e.tensor_scalar(out=out, in0=a, scalar1=s1, scalar2=None, op0=op0), w=w, r=r)
        else:
            K.op(eng, lambda e: e.tensor_scalar(out=out, in0=a, scalar1=s1, scalar2=s2, op0=op0, op1=op1), w=w, r=r)

    def STT(out, a, s, b, op0, op1, w, r):
        K.op(dve, lambda e: e.scalar_tensor_tensor(out=out, in0=a, scalar=s, in1=b, op0=op0, op1=op1), w=w, r=r)

    def ACT(out, a, func, w, r, bias=None, scale=None):
        kw = {}
        if bias is not None:
            kw["bias"] = bias
        if scale is not None:
            kw["scale"] = scale
        K.op(act, lambda e: e.activation(out=out, in_=a, func=func, **kw), w=w, r=r)

    def SIG(out, a, w, r):
        ACT(out, a, AF.Exp, w=w, r=r, scale=-1.0)
        ACT(out, out, AF.Ln, w=w, r=w, bias=1.0)
        ACT(out, out, AF.Exp, w=w, r=w, scale=-1.0)

    def RSQ(out, a, w, r):
        ACT(out, a, AF.Ln, w=w, r=r)
        ACT(out, out, AF.Exp, w=w, r=w, scale=-0.5)

    def CPA(out, a, w, r):
        K.op(act, lambda e: e.copy(out=out, in_=a), w=w, r=r)

    def CPV(out, a, w, r, eng=None):
        K.op(eng or dve, lambda e: e.tensor_copy(out=out, in_=a), w=w, r=r)

    dstate = {"on": False, "n": 0}

    def _dummy():
        if DUMMY and dstate["on"]:
            dstate["n"] += 1
            if dstate["n"] % DUMMY == 0:
                K.op(pe, lambda e: e.matmul(PDUM[0:128, 0:min(512, NT)], lhsT=AT_all[:, 0, 0:128], rhs=AT_all[:, 1, 0:min(512, NT)],
                                            start=True, stop=True), w=[PDUM])

    def MM(out, lhsT, rhs, st, sp_, w, r):
        K.op(pe, lambda e: e.matmul(out, lhsT=lhsT, rhs=rhs, start=st, stop=sp_), w=w, r=r)
        if sp_:
            _dummy()

    def TPo(out, a, idn, w, r):
        K.op(pe, lambda e: e.transpose(out=out, in_=a, identity=idn), w=w, r=r)
        _dummy()

    def RECIP(out, a, w, r):
        K.op(dve, lambda e: e.reciprocal(out=out, in_=a), w=w, r=r)

    def MEMSET(eng, out, v, w):
        K.op(eng, lambda e: e.memset(out, v), w=w)

    def C(name, rows=64, sample=False):
        o, n = CO[name + ("S" if sample else "")]
        return CST[0:rows, o:o + n]

    ident = lambda n: CST[0:n, 0:n]

    def b3(ap2, n):
        return ap2.unsqueeze(2).broadcast_to([ap2.shape[0], ap2.shape[1], n])

    def bm(ap2, n):
        return ap2.unsqueeze(1).broadcast_to([ap2.shape[0], n, ap2.shape[1]])

    def v3(ap2, a):
        return ap2.rearrange("p (a b) -> p a b", a=a)

    K.dma(sp, CST[:, :], cst_d[:, :], w=[CST])

    STATs = [sb("STAT0", (128, 16)), sb("STAT1", (128, 16))]
    MVs = [sb("MV0", (128, 8)), sb("MV1", (128, 8))]
    XTs = [Tk(arena_t[:, 0:1024]), Tk(arena_t[:, 1024:2048])]

    def ln_tile(Xtk, Xap, P, OUTtk, OUTap, par=0):
        STAT, MV = STATs[par], MVs[par]
        for hf in range(2):
            K.op(dve, lambda e: e.bn_stats(out=STAT[0:P, hf * 6:hf * 6 + 6], in_=Xap[:, hf * 512:(hf + 1) * 512]),
                 w=[STAT], r=[Xtk])
        K.op(dve, lambda e: e.bn_aggr(out=MV[0:P, 0:2], in_=STAT[0:P, 0:12]), w=[MV], r=[STAT])
        TS(dve, MV[0:P, 2:3], MV[0:P, 1:2], EPS, ALU.add, w=[MV], r=[MV])
        RSQ(MV[0:P, 3:4], MV[0:P, 2:3], w=[MV], r=[MV])
        TS(dve, OUTap, Xap, MV[0:P, 0:1], ALU.subtract, w=[OUTtk], r=[Xtk, MV], s2=MV[0:P, 3:4], op1=ALU.mult)
        TT(pool, OUTap, OUTap, LNR[0:P, 0:1024], ALU.mult, w=[OUTtk], r=[OUTtk, LNR])
        TT(pool, OUTap, OUTap, LNR[0:P, 1024:2048], ALU.add, w=[OUTtk], r=[OUTtk, LNR])

    def to_AT(SRCtk, SRCap, i, P):
        cs = [ATc[2 * i]] + ([ATc[2 * i + 1]] if P == 128 else [])
        for hf in range(2):
            PB = (PX, PD, PA, PO)[(2 * i + hf) % 4]
            for b in range(4):
                TPo(PB[0:128, b * 128:b * 128 + P], SRCap[0:P, (hf * 4 + b) * 128:(hf * 4 + b + 1) * 128],
                    ident(P), w=[PB], r=[SRCtk, CST])
            CPA(AT_all[:, hf * 4:hf * 4 + 4, 128 * i:128 * i + P], v3(PB[0:128, 0:512], 4)[:, :, 0:P], w=cs, r=[PB])
            if i == 0:
                CPV(v3(XF[:, 0:512], 8)[:, hf * 4:hf * 4 + 4, :], v3(PB[0:128, 0:512], 4)[:, :, 0:64], w=[XF], r=[PB])

    def tile_P(i):
        return 128 if i < NTP else NSK

    K.dma(sp, LNR[:, :], lnrow_d[0], w=[LNR])
    for i in range(NTILES):
        P = tile_P(i)
        K.dma(sp, Rt[i][0:P, :], xin[128 * i:128 * i + P, :], w=[Rt[i]])
        XT = XTs[i % 2]
        ln_tile(Rt[i], Rt[i][0:P, :], P, XT, XT[0:P, :], i % 2)
        to_AT(XT, XT, i, P)
        K.op(act, lambda e: e.mul(out=Rt[i][0:P, :], in_=XT[0:P, :], mul=ALPHA), w=[Rt[i]], r=[XT])

    aoff[0] = 0
    Wm = ar(8 * 1032 // 2, BF16)
    Wo = ar(2 * 1024 // 2, BF16)
    U_main = ar(1032)
    RAW = [ar(6 * NS * 7), ar(6 * 67)]
    RAWS = RAW[0]
    CV = ar(768)
    SM = ar(64)
    SM2 = ar(64)
    DS = ar(64)
    LL = ar(8)
    LM = ar(256)
    LLS = ar(4 * NS)
    QT = arb(256)
    QPT = arb(256)
    KT = arb(256)
    WT = arb(256)
    QN = ar(256)
    KN = ar(256)
    KP = arb(256)
    RH = arb(512)
    UW = ar(512)
    DT = ar(256)
    ATT = arb(256)
    BA = [arb(256), arb(256)]
    BTA = [arb(256), arb(256)]
    RR = [arb(256), arb(256)]
    TMP = ar(512)
    TMP2 = ar(256)
    Ob = ar(264)
    VA = arb(264)
    Yb = ar(256)
    YT = ar(64, BF16)
    SPs = ar(264)
    SSs = [ar(264), ar(264)]
    SNV = ar(64)
    QM = ar(512)
    WM_ = QM
    WST = [ar(1032), ar(1032)]
    YT32 = ar(128)
    VM = ar(264)
    GI = ar(72)
    GF = ar(72)
    LF = ar(72)
    BC = ar(72)
    AA = ar(72)
    MGX = ar(72)
    EXS = ar(72)
    EXK = ar(72)
    EXM = ar(72)
    NMG = ar(72)
    AM = ar(256)
    DG = ar(64)
    MINI = ar(16)
    MFIN = ar(16)
    NEGF = ar(8)
    ONESROW = ar(72)
    LBR = ar(256)
    NEA = ar(8)
    mixer_arena_end = aoff[0]

    Wm3 = Wm[:, 0:8 * 1032].rearrange("p (k c) -> p k c", k=8)
    Wo3 = Wo[:, 0:2048].rearrange("p (k c) -> p k c", k=2)

    def prow(name, P=64):
        o, n = PROW[name]
        return PRW[0:P, o:o + n]

    tpc = [0]

    def tp4(srcs, DSTtk, rd, RO=lambda a: a):
        tpc[0] += 1
        PB = (PX, PJ)[tpc[0] % 2]
        for h in range(4):
            TPo(PB[0:64, h * 64:(h + 1) * 64], srcs[h], ident(64), w=[PB], r=rd + [CST])
        CPA(RO(DSTtk[0:64, 0:256]), PB[0:64, 0:256], w=[DSTtk], r=[PB])

    def mixer_pass(l, m):
        c0, c1 = MIX_RANGE[m]
        ncol = c1 - c0
        dvw = 66 if m == 3 else 64
        nconv = {0: 6, 1: 4}.get(m, 0)
        convoff = {0: 0, 1: 256}.get(m, 0)
        XF3 = v3(XF[:, 0:512], 8)
        ngrp0 = (ncol + 511) // 512
        zl = C("zeros")
        if nconv and FP32_MIX:
            MM(PX[0:128, 0:nconv * 64], zl[:, 0:128], CST[0:64, 0:nconv * 64], True, False, w=[PX], r=[CST])
        for k in range(8):
            ws = WST[k % 2]
            if not NOSTG:
                K.dma(stq, ws[:, 0:ncol], w_in[l, k * 128:(k + 1) * 128, c0:c1], w=[ws])
            for g in range(ngrp0 if FP32_MIX else 0):
                wd = min(512, ncol - g * 512)
                if g == 2:
                    MM(PS[0:64, 384:384 + wd], XF3[:, k, :], ws[:, g * 512:g * 512 + wd], k == 0, k == 7, w=[PS], r=[XF, ws])
                else:
                    MM(PJ[0:64, g * 512:g * 512 + wd], XF3[:, k, :], ws[:, g * 512:g * 512 + wd], k == 0, k == 7, w=[PJ], r=[XF, ws])
            for b in range(nconv if FP32_MIX else 0):
                MM(PX[0:128, b * 64:(b + 1) * 64], ws[:, convoff + b * 128:convoff + (b + 1) * 128], XF3[:, k, :],
                   False, (k == 7 and b == nconv - 1), w=[PX], r=[XF, ws])
            if STAGE:
                K.op(pool, lambda e: e.tensor_copy(out=Wm3[:, k, 0:ncol], in_=ws[:, 0:ncol]), w=[Wm], r=[ws])
            else:
                K.dma(pool, Wm3[:, k, 0:ncol], w_in[l, k * 128:(k + 1) * 128, c0:c1], w=[Wm])
        for b in range(2):
            if not NOSTG:
                K.dma(stq, WST[b][:, 0:1024], w_out[l, 256 * m + 128 * b:256 * m + 128 * (b + 1), :], w=[WST[b]])
            if STAGE:
                K.op(pool, lambda e: e.tensor_copy(out=Wo3[:, b, :], in_=WST[b][:, 0:1024]), w=[Wo], r=[WST[b]])
            else:
                K.dma(pool, Wo3[:, b, :], w_out[l, 256 * m + 128 * b:256 * m + 128 * (b + 1), :], w=[Wo])
        MEMSET(dve, SPs[0:64, 0:264], 0.0, w=[SPs])
        if m == 3:
            K.dma(sp, MINI[0:4, 0:NS], s_m_d[l], w=[MINI])
            MEMSET(dve, MGX[0:4, 0:1], 0.0, w=[MGX])
            MEMSET(dve, BC[0:4, 0:1], 0.0, w=[BC])
            MEMSET(dve, ONESROW[0:4, 0:72], 1.0, w=[ONESROW])
            TS(dve, NEGF[0:4, 0:1], PCL[0:4, 51:52], -1.0, ALU.mult, w=[NEGF], r=[PCL])

        def chunk_body(c):
            sample = (c == NCH - 1)
            tok0 = 64 * c
            i_tile, pb = c // 2, 64 * (c % 2)
            Mk = lambda name, rows=64: C(name, rows, sample)
            lp = LOWP and (0 < c < NCH - 1)
            RO = (lambda a: a.bitcast(F32R)) if lp else (lambda a: a)
            ROW = (lambda a: a.bitcast(F32R)) if LOWP else (lambda a: a)
            if m == 0:
                bgr, cgr = [(768, 256), (1024, 8)], [(0, 512), (512, 256)]
                ber, cer = [(768, 1032)], [(0, 768)]
            elif m == 1:
                bgr, cgr = [(0, 256), (768, 4)], [(256, 256), (512, 256)]
                ber, cer = [(0, 256), (768, 772)], [(256, 768)]
            elif m == 2:
                bgr, cgr, ber, cer = [(0, 512), (512, 512)], [], [(0, 1024)], []
            else:
                bgr, cgr, ber, cer = [(0, 512), (512, 512), (1024, 8)], [], [(0, 1032)], []
            paired_even = (not sample) and c >= 2 and c % 2 == 0 and c + 1 <= NCH - 2
            paired_odd = (not sample) and c >= 3 and c % 2 == 1
            need_raw = bool(nconv) and (sample or c == NCH - 2 or (paired_even and c + 1 == NCH - 2))
            grps = bgr + (cgr if need_raw else [])
            evr = ber + (cer if need_raw else [])
            U = WST[1] if ((not sample) and c % 2 == 1) else U_main
            do_proj = not ((c == 0 and FP32_MIX) or paired_odd)
            Pm = 128 if paired_even else 64
            if do_proj:
                rtk = [ATc[c], Wm] + ([ATc[c + 1]] if paired_even else [])
                for (g0, wd) in grps:
                    for k in range(8):
                        if g0 == 1024:
                            MM(PS[0:Pm, 384:384 + wd], AT_all[:, k, tok0:tok0 + Pm], Wm3[:, k, g0:g0 + wd],
                               k == 0, k == 7, w=[PS], r=rtk)
                        else:
                            MM(PJ[0:Pm, g0:g0 + wd], AT_all[:, k, tok0:tok0 + Pm], Wm3[:, k, g0:g0 + wd],
                               k == 0, k == 7, w=[PJ], r=rtk)
            if nconv and not (c == 0 and FP32_MIX):
                for b in range(nconv):
                    for k in range(8):
                        MM(PX[0:128, b * 64:(b + 1) * 64], Wm3[:, k, convoff + b * 128:convoff + (b + 1) * 128],
                           AT_all[:, k, tok0:tok0 + 64], k == 0, k == 7, w=[PX], r=[ATc[c], Wm])
            yield 'P1'
            if c == 0 and FP32_MIX:
                CPA(U[0:64, 0:min(ncol, 1024)], PJ[0:64, 0:min(ncol, 1024)], w=[U], r=[PJ])
                if ncol > 1024:
                    CPA(U[0:64, 1024:ncol], PS[0:64, 384:384 + ncol - 1024], w=[U], r=[PS])
            elif paired_odd:
                for (e0, e1) in evr:
                    K.dma(sp, WST[1][0:64, e0:e1], U_main[64:128, e0:e1], w=[WST[1]], r=[U_main])
            else:
                for (e0, e1) in evr:
                    e1j = min(e1, 1024)
                    CPA(U[0:Pm, e0:e1j], PJ[0:Pm, e0:e1j], w=[U], r=[PJ])
                    if e1 > 1024:
                        CPA(U[0:Pm, 1024:e1], PS[0:Pm, 384:384 + e1 - 1024], w=[U], r=[PS])

            yield 'step'
            if nconv:
                nch_ = nconv * 128
                cst_o = (p_conv_o if not sample else o_conv_o)[m]
                if sample:
                    for j in range(3):
                        K.dma(sp, cst_o[l].rearrange("(s j) c -> j s c", j=3)[j],
                              U[1 + j:64:4, convoff:convoff + nch_], r=[U])
                elif c == NCH - 2:
                    K.dma(sp, cst_o[l], U[61:64, convoff:convoff + nch_], r=[U])
                wof = 0 if m == 0 else 30
                if not sample:
                    Rw = RAW[c % 2]
                    Rp = RAW[(c + 1) % 2]
                    R3 = v3(Rw[:, 0:6 * 67], 6)
                    P3 = v3(Rp[:, 0:6 * 67], 6)
                    CPA(R3[:, 0:nconv, 3:67], v3(PX[0:128, 0:nconv * 64], nconv), w=[Rw], r=[PX])
                    if c == 0:
                        MEMSET(dve, R3[:, 0:nconv, 0:3], 0.0, w=[Rw])
                    else:
                        CPV(R3[:, 0:nconv, 0:3], P3[:, 0:nconv, 64:67], w=[Rw], r=[Rp])
                    srcs = lambda b, j: R3[:, b, j:j + 64]
                    outv = lambda b: CV[:, b * 64:(b + 1) * 64]
                    rtk = Rw
                else:
                    R4 = RAWS[:, 0:6 * NS * 7].rearrange("p (b s j) -> p b s j", b=6, s=NS)
                    K.dma(sp, CV[0:NS * 3, 0:nch_], s_conv_d[m][l], w=[CV])
                    cb = CV
                    cbap = CV[0:NS * 3, 0:nch_]
                    for b in range(nconv):
                        TPo(PD[0:128, b * 48:(b + 1) * 48], cbap[:, b * 128:(b + 1) * 128],
                            ident(48),
                            w=[PD], r=[cb, CST])
                    for b in range(nconv):
                        CPA(R4[:, b, :, 0:3], PD[0:128, b * 48:(b + 1) * 48].rearrange("p (s j) -> p s j", j=3),
                            w=[RAWS], r=[PD])
                        CPA(R4[:, b, :, 3:7], PX[0:128, b * 64:(b + 1) * 64].rearrange("p (s j) -> p s j", j=4),
                            w=[RAWS], r=[PX])
                    srcs = lambda b, j: R4[:, b, :, j:j + 4]
                    outv = lambda b: CV[:, b * 64:(b + 1) * 64].rearrange("p (s j) -> p s j", j=4)
                    rtk = RAWS
                for b in range(nconv):
                    wc = lambda j: PCL[:, wof + b * 4 + j:wof + b * 4 + j + 1]
                    bc_ = PCL[:, wof + nconv * 4 + b:wof + nconv * 4 + b + 1]
                    TS(dve, outv(b), srcs(b, 0), wc(0), ALU.mult, w=[CV], r=[rtk, PCL], s2=bc_, op1=ALU.add)
                    for j in range(1, 4):
                        STT(outv(b), srcs(b, j), wc(j), outv(b), ALU.mult, ALU.add, w=[CV], r=[rtk, PCL, CV])
                        if j == 2:
                            yield 'step'
                    yield 'step'
                SIG(CV[:, 384:384 + nconv * 64], CV[:, 0:nconv * 64], w=[CV], r=[CV])
                yield 'step'
                TT(pool, CV[:, 0:nconv * 64], CV[:, 0:nconv * 64], CV[:, 384:384 + nconv * 64], ALU.mult, w=[CV], r=[CV])
                yield 'step'
                for b in range(nconv):
                    TPo(PJ[0:64, b * 128:(b + 1) * 128], CV[:, b * 64:(b + 1) * 64], ident(128), w=[PJ], r=[CV, CST])
                yield 'step'
                CPA(U[0:64, convoff:convoff + nch_], PJ[0:64, 0:nch_], w=[U], r=[PJ])

            yield 'P2'
            Utk = U
            if m in (0, 1):
                ell = LL[0:64, 0:4]
                if m == 0:
                    q_ap, k_ap, v_ap = U[0:64, 0:256], U[0:64, 256:512], U[0:64, 512:768]
                    z_ap, b_ap, a_ap = U[0:64, 768:1024], U[0:64, 1024:1028], U[0:64, 1028:1032]
                    dtb, nea = prow("dt_bias_a"), NEA[0:64, 0:4]
                else:
                    z_ap, x_ap = U[0:64, 0:256], U[0:64, 256:512]
                    Bm, Cm, a_ap = U[0:64, 512:640], U[0:64, 640:768], U[0:64, 768:772]
                    dtb, nea = prow("dt_bias_b"), NEA[0:64, 4:8]
                    for g in range(2):
                        TPo(PX[0:64, g * 64:(g + 1) * 64], Cm[:, g * 64:(g + 1) * 64], ident(64), w=[PX], r=[U, CST])
                        TPo(PX[0:64, 128 + g * 64:128 + (g + 1) * 64], Bm[:, g * 64:(g + 1) * 64], ident(64), w=[PX], r=[U, CST])
                    CPA(ROW(QT[0:64, 0:128]), PX[0:64, 0:128], w=[QT], r=[PX])
                    CPA(ROW(KT[0:64, 0:128]), PX[0:64, 128:256], w=[KT], r=[PX])
                TT(dve, SM2[0:64, 20:24], a_ap, dtb, ALU.add, w=[SM2], r=[U, PRW])
                ACT(SM2[0:64, 20:24], SM2[0:64, 20:24], AF.Exp, w=[SM2], r=[SM2])
                ACT(SM2[0:64, 24:28], SM2[0:64, 20:24], AF.Ln, w=[SM2], r=[SM2], bias=1.0)
                TT(dve, ell, SM2[0:64, 24:28], nea, ALU.mult, w=[LL], r=[SM2, NEA])
                TT(dve, v3(LM[0:64, 0:256], 4), bm(Mk("gt"), 4), b3(ell, 64), ALU.mult, w=[LM], r=[LL, CST])
                MM(PS[0:64, 0:4], Mk("incl"), ell, True, True, w=[PS], r=[CST, LL])
                MM(PS[0:64, 4:8], Mk("gt"), ell, True, True, w=[PS], r=[CST, LL])
                if not sample:
                    MM(PS[0:64, 8:12], C("ones"), ell, True, True, w=[PS], r=[CST, LL])
                    nds = 4
                else:
                    TT(dve, v3(LLS[0:64, 0:4 * NS], 4), b3(ell, NS), bm(C("seqoh"), 4), ALU.mult, w=[LLS], r=[LL, CST])
                    MM(PS[0:64, 8:8 + 4 * NS], C("ones"), LLS[0:64, 0:4 * NS], True, True, w=[PS], r=[CST, LLS])
                    nds = 4 * NS
                MM(PD[0:64, 0:256], ident(64), Mk("neg4"), True, False, w=[PD], r=[CST])
                for h in range(4):
                    MM(PD[0:64, h * 64:(h + 1) * 64], LM[0:64, h * 64:(h + 1) * 64], Mk("incl"), False, h == 3,
                       w=[PD], r=[LM, CST])
                ACT(SM[0:64, 0:8], PS[0:64, 0:8], AF.Exp, w=[SM], r=[PS])
                ACT(DS[0:64, 0:nds], PS[0:64, 8:8 + nds], AF.Exp, w=[DS], r=[PS])
                ACT(DT[0:64, 0:256], PD[0:64, 0:256], AF.Exp, w=[DT], r=[PD])
                eg, ek = SM[0:64, 0:4], SM[0:64, 4:8]
            if m == 0:
                TT(dve, TMP[0:64, 0:512], U[0:64, 0:512], U[0:64, 0:512], ALU.mult, w=[TMP], r=[U])
                K.op(dve, lambda e: e.tensor_reduce(out=SM2[0:64, 0:8], in_=v3(TMP[0:64, 0:512], 8), axis=AX.X, op=ALU.add),
                     w=[SM2], r=[TMP])
                TS(dve, SM2[0:64, 0:8], SM2[0:64, 0:8], EPS, ALU.add, w=[SM2], r=[SM2])
                RSQ(SM2[0:64, 8:16], SM2[0:64, 0:8], w=[SM2], r=[SM2])
                TT(dve, v3(QN[0:64, 0:256], 4), v3(q_ap, 4), b3(SM2[0:64, 8:12], 64), ALU.mult, w=[QN], r=[U, SM2])
                TS(dve, QN[0:64, 0:256], QN[0:64, 0:256], 0.125, ALU.mult, w=[QN], r=[QN])
                TT(dve, v3(KN[0:64, 0:256], 4), v3(k_ap, 4), b3(SM2[0:64, 12:16], 64), ALU.mult, w=[KN], r=[U, SM2])
                SIG(SM2[0:64, 16:20], b_ap, w=[SM2], r=[U])
                beta = SM2[0:64, 16:20]
                tp4([KN[0:64, h * 64:(h + 1) * 64] for h in range(4)], KT, [KN], ROW)
                TT(dve, v3(TMP2[0:64, 0:256], 4), v3(DT[0:64, 0:256], 4), bm(Mk("strict"), 4), ALU.mult, w=[TMP2], r=[DT, CST])
                TT(dve, v3(TMP2[0:64, 0:256], 4), v3(TMP2[0:64, 0:256], 4), b3(beta, 64), ALU.mult, w=[TMP2], r=[TMP2, SM2])
                TS(dve, TMP2[0:64, 0:256], TMP2[0:64, 0:256], -1.0, ALU.mult, w=[TMP2], r=[TMP2])
                RH3 = ROW(v3(RH[0:64, 0:512], 4))
                side = [
                    lambda: tp4([QN[0:64, h * 64:(h + 1) * 64] for h in range(4)], QT, [QN], ROW),
                    lambda: (TT(pool, v3(TMP[0:64, 0:256], 4), v3(QN[0:64, 0:256], 4), b3(eg, 64), ALU.mult, w=[TMP], r=[QN, SM]),
                             tp4([TMP[0:64, h * 64:(h + 1) * 64] for h in range(4)], QPT, [TMP], ROW)),
                    lambda: TT(dve, ROW(v3(KP[0:64, 0:256], 4)), v3(KN[0:64, 0:256], 4), b3(ek, 64), ALU.mult, w=[KP], r=[KN, SM]),
                    lambda: (CPV(RH3[:, :, 0:64], v3(v_ap, 4), w=[RH], r=[U]),
                             TT(dve, RH3[:, :, 64:128], v3(KN[0:64, 0:256], 4), b3(eg, 64), ALU.mult, w=[RH], r=[KN, SM])),
                ]
                for h in range(4):
                    MM(PA[0:64, h * 64:(h + 1) * 64], RO(KT[0:64, h * 64:(h + 1) * 64]), RO(KT[0:64, h * 64:(h + 1) * 64]),
                       True, True, w=[PA], r=[KT])
                TT(dve, ROW(BA[0][0:64, 0:256]), PA[0:64, 0:256], TMP2[0:64, 0:256], ALU.mult, w=[BA[0]], r=[PA, TMP2])
                for h in range(4):
                    TPo(PA[0:64, h * 64:(h + 1) * 64], BA[0][0:64, h * 64:(h + 1) * 64], ident(64), w=[PA], r=[BA[0], CST])
                CPA(ROW(BTA[0][0:64, 0:256]), PA[0:64, 0:256], w=[BTA[0]], r=[PA])
                TT(dve, ROW(v3(RR[0][0:64, 0:256], 4)), v3(BA[0][0:64, 0:256], 4), bm(ident(64), 4), ALU.add, w=[RR[0]], r=[BA[0], CST])
                cur = 0
                for lev in range(5):
                    nx = 1 - cur
                    for h in range(4):
                        sl = slice(h * 64, (h + 1) * 64)
                        MM(PA[0:64, sl], RO(BA[cur][0:64, sl]), RO(BTA[cur][0:64, sl]), True, True, w=[PA], r=[BA[cur], BTA[cur]])
                    CPA(ROW(BTA[nx][0:64, 0:256]), PA[0:64, 0:256], w=[BTA[nx]], r=[PA])
                    if lev < 4:
                        for h in range(4):
                            sl = slice(h * 64, (h + 1) * 64)
                            MM(PD[0:64, sl], RO(BTA[cur][0:64, sl]), RO(BA[cur][0:64, sl]), True, True, w=[PD], r=[BA[cur], BTA[cur]])
                        CPV(ROW(BA[nx][0:64, 0:256]), PD[0:64, 0:256], w=[BA[nx]], r=[PD])
                    for h in range(4):
                        sl = slice(h * 64, (h + 1) * 64)
                        MM(PO[0:64, sl], RO(BTA[nx][0:64, sl]), RO(RR[cur][0:64, sl]), True, True, w=[PO], r=[BTA[nx], RR[cur]])
                    TT(dve, ROW(RR[nx][0:64, 0:256]), PO[0:64, 0:256], RR[cur][0:64, 0:256], ALU.add, w=[RR[nx]], r=[PO, RR[cur]])
                    cur = nx
                    if side:
                        side.pop(0)()
                while side:
                    side.pop(0)()
                T2T = RR[cur]
                for h in range(4):
                    MM(PO[0:64, h * 128:(h + 1) * 128], RO(T2T[0:64, h * 64:(h + 1) * 64]), RO(RH[0:64, h * 128:(h + 1) * 128]),
                       True, True, w=[PO], r=[T2T, RH])
                TT(dve, v3(UW[0:64, 0:512], 4), v3(PO[0:64, 0:512], 4), b3(beta, 128), ALU.mult, w=[UW], r=[PO, SM2])
                UW3 = v3(UW[0:64, 0:512], 4)
                tp4([UW3[:, h, 64:128] for h in range(4)], WT, [UW], ROW)
                qt = lambda h: QT[0:64, h * 64:(h + 1) * 64]
                kt = lambda h: KT[0:64, h * 64:(h + 1) * 64]
                DTm = DT[0:64, 0:256]
                DTtk = DT
            elif m == 1:
                dt_ap = SM2[0:64, 24:28]
                VA3 = ROW(v3(VA[0:64, 0:256], 4))
                TT(dve, VA3, v3(x_ap, 4), b3(dt_ap, 64), ALU.mult, w=[VA], r=[U, SM2])
                Cg = Cm.rearrange("p (g d) -> p g d", g=2).unsqueeze(2).broadcast_to([64, 2, 2, 64])
                Bg = Bm.rearrange("p (g d) -> p g d", g=2).unsqueeze(2).broadcast_to([64, 2, 2, 64])
                eg4 = eg.rearrange("p (g e) -> p g e", g=2).unsqueeze(3).broadcast_to([64, 2, 2, 64])
                ek4 = ek.rearrange("p (g e) -> p g e", g=2).unsqueeze(3).broadcast_to([64, 2, 2, 64])
                TT(dve, TMP[0:64, 0:256].rearrange("p (g e d) -> p g e d", g=2, e=2), Cg, eg4, ALU.mult, w=[TMP], r=[U, SM])
                tp4([TMP[0:64, h * 64:(h + 1) * 64] for h in range(4)], QPT, [TMP], ROW)
                TT(dve, ROW(KP[0:64, 0:256].rearrange("p (g e d) -> p g e d", g=2, e=2)), Bg, ek4, ALU.mult, w=[KP], r=[U, SM])
                qt = lambda h: QT[0:64, (h // 2) * 64:(h // 2 + 1) * 64]
                kt = lambda h: KT[0:64, (h // 2) * 64:(h // 2 + 1) * 64]
                DTm = DT[0:64, 0:256]
                DTtk = DT
            elif m == 2:
                q_ap, fx_ap, v_ap, g_ap = U[0:64, 0:256], U[0:64, 256:512], U[0:64, 512:768], U[0:64, 768:1024]
                lb = LBR[0:64, 0:256]
                SIG(TMP[0:64, 0:256], fx_ap, w=[TMP], r=[U])
                TS(dve, TMP[0:64, 256:512], lb, -1.0, ALU.mult, w=[TMP], r=[LBR], s2=1.0, op1=ALU.add)
                TT(dve, TMP[0:64, 0:256], TMP[0:64, 0:256], TMP[0:64, 256:512], ALU.mult, w=[TMP], r=[TMP])
                TT(dve, TMP[0:64, 0:256], TMP[0:64, 0:256], lb, ALU.add, w=[TMP], r=[TMP, LBR])
                ACT(TMP2[0:64, 0:256], TMP[0:64, 0:256], AF.Ln, w=[TMP2], r=[TMP])
                TS(dve, KN[0:64, 0:256], TMP[0:64, 0:256], -1.0, ALU.mult, w=[KN], r=[TMP], s2=1.0, op1=ALU.add)
                ellc = TMP2[0:64, 0:256]
                MM(PA[0:64, 0:256], (Mk("incl") if sample else C("rel")), ellc, True, True, w=[PA], r=[CST, TMP2])
                MM(PD[0:64, 0:256], Mk("incl"), ellc, True, True, w=[PD], r=[CST, TMP2])
                MM(PO[0:64, 0:256], Mk("gt"), ellc, True, True, w=[PO], r=[CST, TMP2])
                if not sample:
                    for h in range(4):
                        MM(PS[0:64, h:h + 1], TMP2[0:64, h * 64:(h + 1) * 64], C("ones")[:, 0:1], True, True, w=[PS], r=[TMP2, CST])
                    nds = 4
                else:
                    for h in range(4):
                        MM(PS[0:64, h * NS:(h + 1) * NS], TMP2[0:64, h * 64:(h + 1) * 64], C("seqoh"), True, True,
                           w=[PS], r=[TMP2, CST])
                    nds = 4 * NS
                ACT(DS[0:64, 0:nds], PS[0:64, 0:nds], AF.Exp, w=[DS], r=[PS])
                ACT(QN[0:64, 0:256], PA[0:64, 0:256], AF.Exp, w=[QN], r=[PA])
                ACT(LM[0:64, 0:256], PA[0:64, 0:256], AF.Exp, w=[LM], r=[PA], scale=-1.0)
                ACT(Yb[0:64, 0:256], PD[0:64, 0:256], AF.Exp, w=[Yb], r=[PD])
                ACT(TMP[0:64, 0:256], PO[0:64, 0:256], AF.Exp, w=[TMP], r=[PO])
                TT(dve, QN[0:64, 0:256], QN[0:64, 0:256], q_ap, ALU.mult, w=[QN], r=[QN, U])
                TT(dve, LM[0:64, 0:256], LM[0:64, 0:256], KN[0:64, 0:256], ALU.mult, w=[LM], r=[LM, KN])
                TT(dve, Yb[0:64, 0:256], Yb[0:64, 0:256], q_ap, ALU.mult, w=[Yb], r=[Yb, U])
                TT(dve, ROW(KP[0:64, 0:256]), TMP[0:64, 0:256], KN[0:64, 0:256], ALU.mult, w=[KP], r=[TMP, KN])
                tp4([QN[0:64, h * 64:(h + 1) * 64] for h in range(4)], QT, [QN], ROW)
                tp4([LM[0:64, h * 64:(h + 1) * 64] for h in range(4)], KT, [LM], ROW)
                tp4([Yb[0:64, h * 64:(h + 1) * 64] for h in range(4)], QPT, [Yb], ROW)
                CPA(ROW(VA[0:64, 0:256]), v_ap, w=[VA], r=[U])
                qt = lambda h: QT[0:64, h * 64:(h + 1) * 64]
                kt = lambda h: KT[0:64, h * 64:(h + 1) * 64]
                CPV(v3(DT[0:64, 0:256], 4), bm(Mk("incl"), 4), w=[DT], r=[CST])
                DTm = DT[0:64, 0:256]
                DTtk = DT
            else:
                q_ap, k_ap, v_ap, og_ap = U[0:64, 0:256], U[0:64, 256:512], U[0:64, 512:768], U[0:64, 768:1024]
                TPo(PS[0:4, 0:64], U[0:64, 1024:1028], ident(64), w=[PS], r=[U, CST])
                TPo(PS[0:4, 64:128], U[0:64, 1028:1032], ident(64), w=[PS], r=[U, CST])
                TS(pool, KN[0:64, 0:256], k_ap, 0.125, ALU.mult, w=[KN], r=[U])
                tp4([U[0:64, h * 64:(h + 1) * 64] for h in range(4)], QT, [U], ROW)
                tp4([KN[0:64, h * 64:(h + 1) * 64] for h in range(4)], KT, [KN], ROW)
                VA3 = ROW(VA[0:64, 0:264].rearrange("p (h d) -> p h d", h=4))
                CPV(VA3[:, :, 0:64], v3(v_ap, 4), w=[VA], r=[U])
                CPV(VA3[:, :, 64:65], C("ones")[:, 0:1].unsqueeze(1).broadcast_to([64, 4, 1]), w=[VA], r=[CST])
                CPV(VA3[:, :, 65:66], C("zeros")[:, 0:1].unsqueeze(1).broadcast_to([64, 4, 1]), w=[VA], r=[CST])
                TS(dve, GI[0:4, 0:64], PS[0:4, 0:64], PCL[0:4, 50:51], ALU.add, w=[GI], r=[PS, PCL])
                ACT(GF[0:4, 0:64], PS[0:4, 64:128], AF.Exp, w=[GF], r=[PS, NEGF], bias=NEGF[0:4, 0:1], scale=-1.0)
                ACT(GF[0:4, 0:64], GF[0:4, 0:64], AF.Ln, w=[GF], r=[GF], bias=1.0)
                TS(dve, LF[0:4, 0:64], GF[0:4, 0:64], -1.0, ALU.mult, w=[LF], r=[GF])
                if not sample:
                    K.op(dve, lambda e: e.tensor_tensor_scan(out=BC[0:4, 1:65], data0=ONESROW[0:4, 0:64], data1=LF[0:4, 0:64],
                                                             initial=BC[0:4, 0:1], op0=ALU.mult, op1=ALU.add),
                         w=[BC], r=[BC, LF, ONESROW])
                    TT(dve, AA[0:4, 0:64], GI[0:4, 0:64], BC[0:4, 1:65], ALU.subtract, w=[AA], r=[GI, BC])
                    K.op(dve, lambda e: e.tensor_tensor_scan(out=MGX[0:4, 1:65], data0=AA[0:4, 0:64], data1=AA[0:4, 0:64],
                                                             initial=MGX[0:4, 0:1], op0=ALU.max, op1=ALU.max),
                         w=[MGX], r=[MGX, AA])
                    mg = MGX[0:4, 1:65]
                    mprev = MGX[0:4, 0:1].broadcast_to([4, 64])
                    mend = MGX[0:4, 64:65].broadcast_to([4, 64])
                    TT(dve, EXS[0:4, 0:64], mprev, mg, ALU.subtract, w=[EXS], r=[MGX])
                    TT(dve, EXK[0:4, 0:64], AA[0:4, 0:64], mend, ALU.subtract, w=[EXK], r=[AA, MGX])
                    TT(dve, EXM[0:4, 0:64], BC[0:4, 1:65], mg, ALU.add, w=[EXM], r=[BC, MGX])
                    TS(dve, NMG[0:4, 0:64], mg, -1.0, ALU.mult, w=[NMG], r=[MGX])
                    TT(dve, DG[0:4, 8:9], MGX[0:4, 0:1], MGX[0:4, 64:65], ALU.subtract, w=[DG], r=[MGX])
                    TS(dve, DG[0:4, 0:4], C("I4", 4), DG[0:4, 8:9], ALU.mult, w=[DG], r=[DG, CST])
                    ndg = 4
                    if c == NCH - 2:
                        CPV(MFIN[0:4, 0:1], EXM[0:4, 63:64], w=[MFIN], r=[EXM])
                        K.dma(sp, p_m_o[l], MFIN[0:4, 0:1], r=[MFIN])
                else:
                    L3 = LF[0:4, 0:64].rearrange("p (s j) -> p s j", j=4)
                    B3 = BC[0:4, 0:64].rearrange("p (s j) -> p s j", j=4)
                    A3 = AA[0:4, 0:64].rearrange("p (s j) -> p s j", j=4)
                    G3 = GI[0:4, 0:64].rearrange("p (s j) -> p s j", j=4)
                    M3 = MGX[0:4, 0:64].rearrange("p (s j) -> p s j", j=4)
                    CPV(B3[:, :, 0:1], L3[:, :, 0:1], w=[BC], r=[LF])
                    for j in range(1, 4):
                        TT(dve, B3[:, :, j:j + 1], B3[:, :, j - 1:j], L3[:, :, j:j + 1], ALU.add, w=[BC], r=[BC, LF])
                    TT(dve, AA[0:4, 0:64], GI[0:4, 0:64], BC[0:4, 0:64], ALU.subtract, w=[AA], r=[GI, BC])
                    TT(dve, M3[:, :, 0:1], A3[:, :, 0:1], MINI[0:4, 0:NS].unsqueeze(2), ALU.max, w=[MGX], r=[AA, MINI])
                    for j in range(1, 4):
                        TT(dve, M3[:, :, j:j + 1], M3[:, :, j - 1:j], A3[:, :, j:j + 1], ALU.max, w=[MGX], r=[MGX, AA])
                    mg = MGX[0:4, 0:64]
                    mprev3 = MINI[0:4, 0:NS].unsqueeze(2).broadcast_to([4, NS, 4])
                    mend3 = M3[:, :, 3:4].broadcast_to([4, NS, 4])
                    TT(dve, EXS[0:4, 0:64].rearrange("p (s j) -> p s j", j=4), mprev3, M3, ALU.subtract, w=[EXS], r=[MGX, MINI])
                    TT(dve, EXK[0:4, 0:64].rearrange("p (s j) -> p s j", j=4), A3, mend3, ALU.subtract, w=[EXK], r=[AA, MGX])
                    TT(dve, EXM[0:4, 0:64], BC[0:4, 0:64], mg, ALU.add, w=[EXM], r=[BC, MGX])
                    TS(dve, NMG[0:4, 0:64], mg, -1.0, ALU.mult, w=[NMG], r=[MGX])
                    TT(dve, DG[0:4, 0:NS], MINI[0:4, 0:NS], M3[:, :, 3], ALU.subtract, w=[DG], r=[MINI, MGX])
                    TT(dve, AM[0:4, 0:4 * NS].rearrange("p (h s) -> p h s", h=4), bm(DG[0:4, 0:NS], 4),
                       b3(C("I4", 4), NS), ALU.mult, w=[AM], r=[DG, CST])
                    ndg = 4 * NS
                    CPV(MFIN[0:4, 0:NS], EXM[0:4, 0:64].rearrange("p (s j) -> p s j", j=4)[:, :, 3], w=[MFIN], r=[EXM])
                    K.dma(sp, o_m_o[l], MFIN[0:4, 0:NS], r=[MFIN])
                MM(PS[0:64, 128:132], EXS[0:4, 0:64], C("I4", 4), True, True, w=[PS], r=[EXS, CST])
                MM(PS[0:64, 132:136], EXK[0:4, 0:64], C("I4", 4), True, True, w=[PS], r=[EXK, CST])
                MM(PS[0:64, 136:140], EXM[0:4, 0:64], C("I4", 4), True, True, w=[PS], r=[EXM, CST])
                ACT(SM[0:64, 0:8], PS[0:64, 128:136], AF.Exp, w=[SM], r=[PS])
                ACT(SM[0:64, 8:12], PS[0:64, 136:140], AF.Exp, w=[SM], r=[PS], scale=-1.0)
                sinit, kw_, emm = SM[0:64, 0:4], SM[0:64, 4:8], SM[0:64, 8:12]
                if not sample:
                    MM(PS[0:64, 160:164], C("ones", 4), DG[0:4, 0:4], True, True, w=[PS], r=[CST, DG])
                else:
                    MM(PS[0:64, 160:160 + ndg], C("ones", 4), AM[0:4, 0:ndg], True, True, w=[PS], r=[CST, AM])
                ACT(DS[0:64, 0:ndg], PS[0:64, 160:160 + ndg], AF.Exp, w=[DS], r=[PS])
                for h in range(4):
                    TS(dve, TMP2[0:4, h * 64:(h + 1) * 64], AA[0:4, 0:64], C("I4", 4)[:, h:h + 1], ALU.mult, w=[TMP2], r=[AA, CST])
                MM(PD[0:64, 0:256], ident(64), Mk("neg4"), True, False, w=[PD], r=[CST])
                for h in range(4):
                    sl = slice(h * 64, (h + 1) * 64)
                    MM(PD[0:64, sl], TMP2[0:4, sl], C("ones", 4), False, False, w=[PD], r=[TMP2, CST])
                    MM(PD[0:64, sl], C("oh4", 4)[:, sl], NMG[0:4, 0:64], False, h == 3,
                       w=[PD], r=[NMG, CST])
                ACT(DT[0:64, 0:256], PD[0:64, 0:256], AF.Exp, w=[DT], r=[PD])
                TT(dve, v3(TMP[0:64, 0:256], 4), v3(q_ap, 4), b3(sinit, 64), ALU.mult, w=[TMP], r=[U, SM])
                tp4([TMP[0:64, h * 64:(h + 1) * 64] for h in range(4)], QPT, [TMP], ROW)
                TT(dve, ROW(v3(KP[0:64, 0:256], 4)), v3(KN[0:64, 0:256], 4), b3(kw_, 64), ALU.mult, w=[KP], r=[KN, SM])
                qt = lambda h: QT[0:64, h * 64:(h + 1) * 64]
                kt = lambda h: KT[0:64, h * 64:(h + 1) * 64]
                DTm = DT[0:64, 0:256]
                DTtk = DT
                if not sample:
                    CPV(MGX[0:4, 0:1], MGX[0:4, 64:65], w=[MGX], r=[MGX])
                    CPV(BC[0:4, 0:1], BC[0:4, 64:65], w=[BC], r=[BC])

            for h in range(4):
                MM(PA[0:64, h * 64:(h + 1) * 64], RO(kt(h)), RO(qt(h)), True, True, w=[PA], r=[KT, QT])
            TT(dve, ROW(ATT[0:64, 0:256]), PA[0:64, 0:256], DTm, ALU.mult, w=[ATT], r=[PA, DTtk])

            nseq = NS if sample else 1
            hw = 4 * dvw
            ssm_in = s_ssm_d[m]
            ssm_out = (o_ssm_o if sample else p_ssm_o)[m]

            def load_state(s):
                St = SSs[s % 2]
                S3 = St[0:64, 0:hw].rearrange("p (h d) -> p h d", h=4)
                K.dma(sp, S3[:, :, 0:64], ssm_in[l, s].rearrange("h k v -> k h v"), w=[St])
                if m == 3:
                    K.dma(sp, SNV[0:4, 0:64], s_nvec_d[l, 4 * s:4 * s + 4, :], w=[SNV])
                    TPo(PX[0:64, 256:260], SNV[0:4, 0:64], ident(4), w=[PX], r=[SNV, CST])
                    CPA(S3[:, :, 64], PX[0:64, 256:260], w=[St], r=[PX])
                    CPV(S3[:, :, 65:66], C("zeros")[:, 0:1].unsqueeze(1).broadcast_to([64, 4, 1]), w=[St], r=[CST])
                return St

            def store_state(s, St):
                S3 = St[0:64, 0:hw].rearrange("p (h d) -> p h d", h=4)
                if sample:
                    K.dma(sp, ssm_out[l, s].rearrange("h k v -> k h v"), S3[:, :, 0:64], r=[St])
                else:
                    K.dma(sp, ssm_out[l].rearrange("h k v -> k h v"), S3[:, :, 0:64], r=[St])
                if m == 3:
                    CPV(TMP2[0:64, 0:4], S3[:, :, 64], w=[TMP2], r=[St])
                    TPo(PX[0:4, 264:328], TMP2[0:64, 0:4], ident(64), w=[PX], r=[TMP2, CST])
                    CPA(SNV[0:4, 0:64], PX[0:4, 264:328], w=[SNV], r=[PX])
                    if sample:
                        K.dma(sp, o_nvec_o[l, 4 * s:4 * s + 4, :], SNV[0:4, 0:64], r=[SNV])
                    else:
                        K.dma(sp, p_nvec_o[l], SNV[0:4, 0:64], r=[SNV])

            def valh(h):
                if m == 0:
                    return RO(VA[0:64, h * 64:(h + 1) * 64])
                if m == 1:
                    return RO(VA[0:64, h * 64:(h + 1) * 64])
                if m == 2:
                    return RO(VA[0:64, h * 64:(h + 1) * 64])
                return RO(VA[0:64, h * 66:(h + 1) * 66])
            valtk = VA

            if not sample:
                St = SPs
                S3 = St[0:64, 0:hw].rearrange("p (h d) -> p h d", h=4)
                if m == 0:
                    for h in range(4):
                        MM(PO[0:64, h * 64:(h + 1) * 64], WT[0:64, h * 64:(h + 1) * 64], S3[:, h, 0:64], True, True,
                           w=[PO], r=[WT, St])
                    TT(dve, ROW(v3(VA[0:64, 0:256], 4)), UW3[:, :, 0:64], v3(PO[0:64, 0:256], 4), ALU.subtract, w=[VA], r=[UW, PO])
                for h in range(4):
                    MM(PO[0:64, h * dvw:(h + 1) * dvw], RO(ATT[0:64, h * 64:(h + 1) * 64]), valh(h), True, False,
                       w=[PO], r=[ATT, valtk])
                    MM(PO[0:64, h * dvw:(h + 1) * dvw], QPT[0:64, h * 64:(h + 1) * 64], S3[:, h, :], False, True,
                       w=[PO], r=[QPT, St])
                CPA(Ob[0:64, 0:hw], PO[0:64, 0:hw], w=[Ob], r=[PO])
                for h in range(4):
                    MM(PS[0:64, h * dvw:(h + 1) * dvw], RO(KP[0:64, h * 64:(h + 1) * 64]), valh(h), True, True,
                       w=[PS], r=[KP, valtk])
                TT(dve, S3, S3, b3(DS[0:64, 0:4], dvw), ALU.mult, w=[St], r=[St, DS])
                TT(dve, St[0:64, 0:hw], St[0:64, 0:hw], PS[0:64, 0:hw], ALU.add, w=[St], r=[St, PS])
                if c == NCH - 2:
                    store_state(0, St)
            else:
                colm = C("colmask")
                seqoh = C("seqoh")
                zeros = C("zeros")

                def build_masked(SRC, DST, sg):
                    cm4 = colm[:, sg * 128:(sg + 1) * 128].rearrange("p (s r) -> p s r", s=2)
                    TT(dve, DST[0:64, 0:512].rearrange("p (h s r) -> p h s r", h=4, s=2),
                       v3(SRC[0:64, 0:256], 4).unsqueeze(2).broadcast_to([64, 4, 2, 64]),
                       cm4.unsqueeze(1).broadcast_to([64, 4, 2, 64]), ALU.mult, w=[DST], r=[SRC, CST])

                def upd_state(s, St, S3):
                    vsrc = VA[0:64, 0:hw]
                    TS(dve, VM[0:64, 0:hw], vsrc, seqoh[:, s:s + 1], ALU.mult, w=[VM], r=[valtk, CST])
                    for h in range(4):
                        MM(PS[0:64, h * dvw:(h + 1) * dvw], KP[0:64, h * 64:(h + 1) * 64],
                           VM[0:64, h * dvw:(h + 1) * dvw], True, True, w=[PS], r=[KP, VM])
                    dss = DS[0:64, 0:4 * NS].rearrange("p (h s) -> p h s", h=4)[:, :, s]
                    TT(dve, S3, S3, b3(dss, dvw), ALU.mult, w=[St], r=[St, DS])
                    TT(dve, St[0:64, 0:hw], St[0:64, 0:hw], PS[0:64, 0:hw], ALU.add, w=[St], r=[St, PS])
                    store_state(s, St)

                obase = 0
                if m == 0:
                    MM(PO[0:64, 0:256], ident(64), zeros[:, 0:256], True, False, w=[PO], r=[CST])
                    for sg in range(NS // 2):
                        build_masked(WT, WM_, sg)
                        for s4 in range(2):
                            s = sg * 2 + s4
                            St = load_state(s)
                            S3 = St[0:64, 0:hw].rearrange("p (h d) -> p h d", h=4)
                            for h in range(4):
                                MM(PO[0:64, h * 64:(h + 1) * 64],
                                   WM_[0:64, (h * 2 + s4) * 64:(h * 2 + s4 + 1) * 64], S3[:, h, 0:64],
                                   False, (s == NS - 1 and h == 3), w=[PO], r=[WM_, St])
                    TT(dve, ROW(v3(VA[0:64, 0:256], 4)), UW3[:, :, 0:64], v3(PO[0:64, 0:256], 4), ALU.subtract, w=[VA], r=[UW, PO])
                    obase = 256
                MM(PO[0:64, obase:obase + hw], ident(64), zeros[:, 0:hw], True, False, w=[PO], r=[CST])
                for h in range(4):
                    MM(PO[0:64, obase + h * dvw:obase + (h + 1) * dvw], ATT[0:64, h * 64:(h + 1) * 64], valh(h), False, False,
                       w=[PO], r=[ATT, valtk])
                for sg in range(NS // 2):
                    build_masked(QPT, QM, sg)
                    for s4 in range(2):
                        s = sg * 2 + s4
                        St = load_state(s)
                        S3 = St[0:64, 0:hw].rearrange("p (h d) -> p h d", h=4)
                        for h in range(4):
                            MM(PO[0:64, obase + h * dvw:obase + (h + 1) * dvw],
                               QM[0:64, (h * 2 + s4) * 64:(h * 2 + s4 + 1) * 64], S3[:, h, :],
                               False, (s == NS - 1 and h == 3), w=[PO], r=[QM, St])
                        upd_state(s, St, S3)
                CPA(Ob[0:64, 0:hw], PO[0:64, obase:obase + hw], w=[Ob], r=[PO])

            yield 'RM'
            O3 = Ob[0:64, 0:hw].rearrange("p (h d) -> p h d", h=4)
            gname = ("norm_a_g", "norm_b_g", "norm_c_g", "norm_d_g")[m]
            gsrc = {0: None, 1: None, 2: None, 3: None}
            if m in (0, 1):
                SIG(TMP2[0:64, 0:256], z_ap, w=[TMP2], r=[U])
                TT(pool, TMP2[0:64, 0:256], TMP2[0:64, 0:256], z_ap, ALU.mult, w=[TMP2], r=[TMP2, U])
            elif m == 2:
                SIG(TMP2[0:64, 0:256], g_ap, w=[TMP2], r=[U])
            else:
                SIG(TMP2[0:64, 0:256], og_ap, w=[TMP2], r=[U])
            if m != 1:
                TT(pool, TMP2[0:64, 0:256], TMP2[0:64, 0:256], prow(gname), ALU.mult, w=[TMP2], r=[TMP2, PRW])
            yield 'step'
            if m == 3:
                ACT(SM2[0:64, 32:36], O3[:, :, 64], AF.Abs, w=[SM2], r=[Ob])
                TT(dve, SM2[0:64, 32:36], SM2[0:64, 32:36], emm, ALU.max, w=[SM2], r=[SM2, SM])
                yield 'step'
                RECIP(SM2[0:64, 36:40], SM2[0:64, 32:36], w=[SM2], r=[SM2])
                TT(dve, v3(TMP[0:64, 0:256], 4), O3[:, :, 0:64], b3(SM2[0:64, 36:40], 64), ALU.mult, w=[TMP], r=[Ob, SM2])
                yield 'step'
                osrc, otk = TMP[0:64, 0:256], TMP
            elif m == 1:
                TT(dve, v3(TMP[0:64, 0:256], 4), v3(x_ap, 4), b3(prow("d_skip_b"), 64), ALU.mult, w=[TMP], r=[U, PRW])
                TT(dve, TMP[0:64, 0:256], TMP[0:64, 0:256], Ob[0:64, 0:256], ALU.add, w=[TMP], r=[TMP, Ob])
                yield 'step'
                TT(dve, TMP[0:64, 0:256], TMP[0:64, 0:256], TMP2[0:64, 0:256], ALU.mult, w=[TMP], r=[TMP, TMP2])
                osrc, otk = TMP[0:64, 0:256], TMP
            else:
                osrc, otk = Ob[0:64, 0:256], Ob
            ng = 2 if m == 1 else 4
            gw = 256 // ng
            TT(dve, TMP[0:64, 256:512], osrc, osrc, ALU.mult, w=[TMP], r=[otk, TMP])
            yield 'step'
            K.op(dve, lambda e: e.tensor_reduce(out=SM2[0:64, 40:40 + ng], in_=v3(TMP[0:64, 256:512], ng), axis=AX.X, op=ALU.add),
                 w=[SM2], r=[TMP])
            TS(dve, SM2[0:64, 40:40 + ng], SM2[0:64, 40:40 + ng], 1.0 / gw, ALU.mult, w=[SM2], r=[SM2], s2=EPS, op1=ALU.add)
            yield 'step'
            RSQ(SM2[0:64, 44:44 + ng], SM2[0:64, 40:40 + ng], w=[SM2], r=[SM2])
            yield 'step'
            TT(dve, v3(Yb[0:64, 0:256], ng), v3(osrc, ng), b3(SM2[0:64, 44:44 + ng], gw), ALU.mult, w=[Yb], r=[otk, SM2])
            if m == 1:
                TT(dve, Yb[0:64, 0:256], Yb[0:64, 0:256], prow(gname), ALU.mult, w=[Yb], r=[Yb, PRW])
            else:
                TT(dve, Yb[0:64, 0:256], Yb[0:64, 0:256], TMP2[0:64, 0:256], ALU.mult, w=[Yb], r=[Yb, TMP2])
            yield 'step'
            for b in range(2):
                TPo(PD[0:128, b * 64:(b + 1) * 64], Yb[0:64, b * 128:(b + 1) * 128], ident(64), w=[PD], r=[Yb, CST])
            PW = (PA, PO)
            yield 'step'
            if c == 0 and FP32_MIX:
                CPA(YT32[:, 0:128], PD[0:128, 0:128], w=[YT32], r=[PD])
                for hf in range(2):
                    for b in range(2):
                        MM(PW[hf][pb:pb + 64, 0:512], YT32[:, b * 64:(b + 1) * 64], WST[b][:, hf * 512:(hf + 1) * 512],
                           b == 0, b == 1, w=[PW[hf]], r=[YT32, WST[b]])
            else:
                CPA(YT[:, 0:128], PD[0:128, 0:128], w=[YT], r=[PD])
                yield 'step'
                for hf in range(2):
                    for b in range(2):
                        MM(PW[hf][pb:pb + 64, 0:512], YT[:, b * 64:(b + 1) * 64], Wo3[:, b, hf * 512:(hf + 1) * 512],
                           b == 0, b == 1, w=[PW[hf]], r=[YT, Wo])
            for hf in range(2):
                TT(dve, Rt[i_tile][pb:pb + 64, hf * 512:(hf + 1) * 512], Rt[i_tile][pb:pb + 64, hf * 512:(hf + 1) * 512],
                   PW[hf][pb:pb + 64, 0:512], ALU.add, w=[Rt[i_tile]], r=[Rt[i_tile], PW[hf]])

        gens = [chunk_body(c) for c in range(NCH)]

        def run_until(g, marker):
            for v in g:
                if v == marker:
                    return

        def step(g, end):
            try:
                v = next(g)
            except StopIteration:
                return False
            return v != end

        if PIPE:
            run_until(gens[0], 'P2')
            for c in range(NCH):
                run_until(gens[c], 'RM')
                if c + 1 < NCH:
                    run_until(gens[c + 1], 'P1')
                    if PIPE2 and c + 1 != 1 and c + 1 != NCH - 1:
                        la = lb = True
                        while la or lb:
                            if la:
                                la = step(gens[c], None)
                            if lb:
                                lb = step(gens[c + 1], 'P2')
                    else:
                        run_until(gens[c], None)
                        run_until(gens[c + 1], 'P2')
                else:
                    run_until(gens[c], None)
        else:
            for c in range(NCH):
                run_until(gens[c], None)

    aoff[0] = 1024
    Wu = [ar(8 * 512 // 2, BF16)] * 2
    Wd = [ar(4 * 1024 // 2, BF16)] * 2
    RL = ar(512)
    ACB = ar(4 * 512 // 2, BF16)
    WU32 = ar(4096)
    WD32 = ar(4096)
    A32 = ar(256)
    RL0 = ar(64)
    assert aoff[0] <= ARENA_COLS

    def ffn(l):
        PUP = [PX, PD, PA, PO]
        dsel = [0]
        ngr = (T + 511) // 512
        groups = [(g * 512, min(512, T - g * 512)) for g in range(ngr)] + [(T, NSK)]
        XF3 = v3(XF[:, 0:512], 8)
        for fc in range(8):
            wu, wd = Wu[fc % 2], Wd[fc % 2]
            wu3 = wu[:, 0:4096].rearrange("p (k c) -> p k c", k=8)
            wd3 = wd[:, 0:4096].rearrange("p (k c) -> p k c", k=4)
            wu32 = v3(WU32[:, 0:4096], 8)
            wd32 = v3(WD32[:, 0:4096], 4)
            for k in range(8):
                if not NOSTG:
                    K.dma(stq, wu32[:, k, :], w_up[l, k * 128:(k + 1) * 128, fc * 512:(fc + 1) * 512], w=[WU32])
            for j in range(4):
                if not NOSTG:
                    K.dma(stq, wd32[:, j, :], w_down[l, fc * 512 + j * 128:fc * 512 + (j + 1) * 128, :], w=[WD32])
            if STAGE:
                K.op(pool, lambda e: e.tensor_copy(out=wu[:, 0:4096], in_=WU32[:, 0:4096]), w=[wu], r=[WU32])
                K.op(act, lambda e: e.copy(out=wd[:, 0:4096], in_=WD32[:, 0:4096]), w=[wd], r=[WD32])
            else:
                for k in range(8):
                    K.dma(pool, wu3[:, k, :], w_up[l, k * 128:(k + 1) * 128, fc * 512:(fc + 1) * 512], w=[wu])
                for j in range(4):
                    K.dma(pool, wd3[:, j, :], w_down[l, fc * 512 + j * 128:fc * 512 + (j + 1) * 128, :], w=[wd])
            a32 = v3(A32[:, 0:256], 4)
            for j in range(4 if FP32_FFN else 0):
                for k in range(8):
                    MM(PS[0:128, j * 64:(j + 1) * 64], wu32[:, k, j * 128:(j + 1) * 128], XF3[:, k, :], k == 0, k == 7,
                       w=[PS], r=[WU32, XF])
                ACT(RL0[:, 0:64], PS[0:128, j * 64:(j + 1) * 64], AF.Relu, w=[RL0], r=[PS])
                TT(dve, a32[:, j, :], RL0[:, 0:64], RL0[:, 0:64], ALU.mult, w=[A32], r=[RL0])
            for hf in range(2 if FP32_FFN else 0):
                for j in range(4):
                    MM(PJ[0:64, hf * 512:(hf + 1) * 512], a32[:, j, :], wd32[:, j, hf * 512:(hf + 1) * 512], j == 0, j == 3,
                       w=[PJ], r=[A32, WD32])
            if FP32_FFN:
                TT(dve, Rt[0][0:64, :], Rt[0][0:64, :], PJ[0:64, 0:1024], ALU.add, w=[Rt[0]], r=[Rt[0], PJ])
            ac3 = ACB[:, 0:2048].rearrange("p (j n) -> p j n", j=4)
            for (g0, n) in groups:
                cks = [ATc[cc] for cc in range(g0 // 64, (g0 + n) // 64)]
                for j in range(4):
                    for k in range(8):
                        MM(PUP[j][:, 0:n], wu3[:, k, j * 128:(j + 1) * 128], AT_all[:, k, g0:g0 + n], k == 0, k == 7,
                           w=[PUP[j]], r=[wu] + cks)
                    ACT(RL[:, 0:n], PUP[j][:, 0:n], AF.Relu, w=[RL], r=[PUP[j]])
                    TT(pool, ac3[:, j, 0:n], RL[:, 0:n], RL[:, 0:n], ALU.mult, w=[ACB], r=[RL])
                for t0 in range(0, n, 128):
                    P = min(128, n - t0)
                    i = (g0 + t0) // 128
                    r0 = 64 if (i == 0 and FP32_FFN) else 0
                    dsel[0] += 1
                    for hf in range(2):
                        pw = (PJ, PJ) if dsel[0] % 2 == 0 else (PDUM, PS)
                        po = pw[hf][r0:P, hf * 512:(hf + 1) * 512] if pw[hf] is PJ else pw[hf][r0:P, 0:512]
                        for j in range(4):
                            MM(po, ac3[:, j, t0 + r0:t0 + P], wd3[:, j, hf * 512:(hf + 1) * 512],
                               j == 0, j == 3, w=[pw[hf]], r=[ACB, wd])
                        TT(dve, Rt[i][r0:P, hf * 512:(hf + 1) * 512], Rt[i][r0:P, hf * 512:(hf + 1) * 512], po, ALU.add,
                           w=[Rt[i]], r=[Rt[i], pw[hf]])

    for l in range(L):
        K.barrier()
        K.dma(sp, PRW[:, :], prow_d[l], w=[PRW])
        K.dma(sp, PCL[:, :], pcol_d[l], w=[PCL])
        ACT(NEA[0:64, 0:4], prow("a_log_a"), AF.Exp, w=[NEA], r=[PRW])
        ACT(NEA[0:64, 4:8], prow("a_log_b"), AF.Exp, w=[NEA], r=[PRW])
        TS(dve, NEA[0:64, 0:8], NEA[0:64, 0:8], -1.0, ALU.mult, w=[NEA], r=[NEA])
        l0, l1 = prow("lbl0"), prow("lbl1")
        TT(dve, TMP[0:64, 0:256], l0, l1, ALU.max, w=[TMP], r=[PRW])
        TT(dve, TMP[0:64, 256:512], l0, TMP[0:64, 0:256], ALU.subtract, w=[TMP], r=[PRW, TMP])
        TT(dve, TMP2[0:64, 0:256], l1, TMP[0:64, 0:256], ALU.subtract, w=[TMP2], r=[PRW, TMP])
        ACT(TMP[0:64, 256:512], TMP[0:64, 256:512], AF.Exp, w=[TMP], r=[TMP])
        ACT(TMP2[0:64, 0:256], TMP2[0:64, 0:256], AF.Exp, w=[TMP2], r=[TMP2])
        TT(dve, TMP[0:64, 0:256], TMP[0:64, 256:512], TMP2[0:64, 0:256], ALU.add, w=[TMP], r=[TMP, TMP2])
        RECIP(TMP[0:64, 0:256], TMP[0:64, 0:256], w=[TMP], r=[TMP])
        TT(dve, TMP[0:64, 256:512], TMP[0:64, 256:512], TMP[0:64, 0:256], ALU.mult, w=[TMP], r=[TMP])
        TT(dve, TMP2[0:64, 0:256], TMP2[0:64, 0:256], TMP[0:64, 0:256], ALU.mult, w=[TMP2], r=[TMP2, TMP])
        if l == 0:
            TT(dve, LBR[0:64, 0:256], TMP[0:64, 256:512], TMP[0:64, 256:512], ALU.subtract, w=[LBR], r=[TMP])
        else:
            TT(dve, LBR[0:64, 0:256], TMP[0:64, 256:512], TMP2[0:64, 0:256], ALU.add, w=[LBR], r=[TMP, TMP2])
            TT(dve, LBR[0:64, 0:256], LBR[0:64, 0:256], TMP[0:64, 256:512], ALU.subtract, w=[LBR], r=[LBR, TMP])
        dstate["on"] = True
        for m in [int(ch) for ch in '0123']:
            mixer_pass(l, m)
        dstate["on"] = False
        K.barrier()
        K.dma(sp, LNR[:, :], lnrow_d[1 + 2 * l], w=[LNR])
        for i in range(NTILES):
            P = tile_P(i)
            XT = XTs[i % 2]
            ln_tile(Rt[i], Rt[i][0:P, :], P, XT, XT[0:P, :], i % 2)
            to_AT(XT, XT, i, P)
            K.op(act, lambda e: e.mul(out=Rt[i][0:P, :], in_=XT[0:P, :], mul=ALPHA), w=[Rt[i]], r=[XT])
        K.barrier()
        ffn(l)
        K.barrier()
        K.dma(sp, LNR[:, :], lnrow_d[2 + 2 * l], w=[LNR])
        for i in range(NTILES):
            P = tile_P(i)
            XT = XTs[i % 2]
            ln_tile(Rt[i], Rt[i][0:P, :], P, XT, XT[0:P, :], i % 2)
            if l == L - 1:
                K.dma(sp, y_d[128 * i:128 * i + P, :], XT[0:P, :], r=[XT])
            else:
                to_AT(XT, XT, i, P)
                K.op(act, lambda e: e.mul(out=Rt[i][0:P, :], in_=XT[0:P, :], mul=ALPHA), w=[Rt[i]], r=[XT])
    K.finish()
    return nc, cst_np


def _rep(row, n=128):
    return np.ascontiguousarray(np.broadcast_to(np.asarray(row, np.float32)[None, :], (n, row.shape[-1])))


def make_in_maps(inp, T, NS, L, ncores, cst_np):
    f = lambda a: np.ascontiguousarray(np.asarray(a, np.float32))
    prow = np.zeros((L, 128, NPROW), np.float32)
    pcol = np.zeros((L, 128, NPCOL), np.float32)
    lnrow = np.zeros((2 * L + 1, 128, 2048), np.float32)
    lnrow[0] = _rep(np.concatenate([f(inp["emb_ln_g"]), f(inp["emb_ln_b"])]))
    lbl = f(inp["lb_logits_c"])
    for l in range(L):
        lnrow[1 + 2 * l] = _rep(np.concatenate([f(inp["ln1_g"])[l], f(inp["ln1_b"])[l]]))
        lnrow[2 + 2 * l] = _rep(np.concatenate([f(inp["ln2_g"])[l], f(inp["ln2_b"])[l]]))
        for name in ("norm_a_g", "norm_b_g", "norm_c_g", "norm_d_g", "a_log_a", "dt_bias_a", "a_log_b",
                     "dt_bias_b", "d_skip_b"):
            o, n = PROW[name]
            prow[l, :, o:o + n] = f(inp[name])[l][None, :]
        prow[l, :, PROW["lbl0"][0]:PROW["lbl0"][0] + 256] = lbl[0][None, :]
        prow[l, :, PROW["lbl1"][0]:PROW["lbl1"][0] + 256] = lbl[1][None, :]
        caw = f(inp["conv_a_w"])[l]
        pcol[l, :, 0:24] = caw.reshape(4, 6, 128).transpose(2, 1, 0).reshape(128, 24)
        pcol[l, :, 24:30] = f(inp["conv_a_b"])[l].reshape(6, 128).T
        cbw = f(inp["conv_b_w"])[l]
        pcol[l, :, 30:46] = cbw.reshape(4, 4, 128).transpose(2, 1, 0).reshape(128, 16)
        pcol[l, :, 46:50] = f(inp["conv_b_b"])[l].reshape(4, 128).T
        pcol[l, 0:4, 50] = f(inp["i_bias_d"])[l]
        pcol[l, 0:4, 51] = f(inp["f_bias_d"])[l]
    maps = []
    for c in range(ncores):
        sl = slice(NS * c, NS * (c + 1))
        m = {
            "xin": np.concatenate([f(inp["x_prompt"])[c], f(inp["x_sample"])[sl].reshape(NS * 4, 1024)], axis=0),
            "w_in": f(inp["w_in"]), "w_out": f(inp["w_out"]), "w_up": f(inp["w_up"]), "w_down": f(inp["w_down"]),
            "prow": prow, "lnrow": lnrow, "pcol": pcol, "cst": cst_np,
            "sa_conv": f(f(inp["state_a_conv"])[:, sl].reshape(L, NS * 3, 768)),
            "sb_conv": f(f(inp["state_b_conv"])[:, sl].reshape(L, NS * 3, 512)),
            "sa_ssm": f(f(inp["state_a_ssm"])[:, sl]), "sb_ssm": f(f(inp["state_b_ssm"])[:, sl]),
            "sc_ssm": f(f(inp["state_c_ssm"])[:, sl]), "sd_cmem": f(f(inp["state_d_cmem"])[:, sl]),
            "sd_nvec": f(f(inp["state_d_nvec"])[:, sl].reshape(L, NS * 4, 64)),
            "sd_m": f(f(inp["state_d_mstab"])[:, sl].transpose(0, 2, 1)),
        }
        maps.append(m)
    return maps


def gather(results, T, NS, L, ncores):
    R = results
    cat = lambda name: np.stack([np.asarray(R[c][name]) for c in range(ncores)], axis=1)
    y = [np.asarray(R[c]["y"]) for c in range(ncores)]
    y_prompt = np.stack([yy[:T] for yy in y], axis=0)
    y_sample = np.concatenate([yy[T:].reshape(NS, 4, 1024) for yy in y], axis=0)
    pa_conv = cat("pa_conv")
    pb_conv = cat("pb_conv")
    pss = [cat(n) for n in ("pa_ssm", "pb_ssm", "pc_ssm", "pd_cmem")]
    pd_nvec = cat("pd_nvec")
    pd_m = cat("pd_m")[..., 0]
    catS = lambda name, shp: np.concatenate([np.asarray(R[c][name]).reshape((L, NS) + shp) for c in range(ncores)], axis=1)
    oa_conv = catS("oa_conv", (3, 768))
    ob_conv = catS("ob_conv", (3, 512))
    oss = [catS(n, (4, 64, 64)) for n in ("oa_ssm", "ob_ssm", "oc_ssm", "od_cmem")]
    od_nvec = catS("od_nvec", (4, 64))
    od_m = np.concatenate([np.asarray(R[c]["od_m"]).transpose(0, 2, 1) for c in range(ncores)], axis=1)
    outs = (y_prompt, y_sample, pa_conv, pss[0], pb_conv, pss[1], pss[2], pss[3], pd_nvec, pd_m,
            oa_conv, oss[0], ob_conv, oss[1], oss[2], oss[3], od_nvec, od_m)
    return tuple(np.ascontiguousarray(o.astype(np.float32)) for o in outs)


def kernel(**inputs):
    T, NS, L, ncores = 2048, 16, 2, 8
    nc = bass.Bass("TRN2", target_bir_lowering=False)
    nc, cst_np = build(nc, T, NS, L)
    maps = make_in_maps(inputs, T, NS, L, ncores, cst_np)
    res = run_bass_kernel_spmd(nc, maps, core_ids=list(range(ncores)))
    return gather(res.results, T, NS, L, ncores)
```

```python
import math
import numpy as np
import concourse.bass as bass
import concourse.mybir as mybir
from concourse.bass_utils import run_bass_kernel_spmd

F32 = mybir.dt.float32
BF16 = mybir.dt.bfloat16
F32R = mybir.dt.float32r
ALU = mybir.AluOpType
AF = mybir.ActivationFunctionType
AX = mybir.AxisListType

SEM_ROT = 30000
import os
FP32_MIX = '1' == '1'
FP32_FFN = '1' == '1'
STQ = 'sp'
LOWP = '1' == '1'
DUMMY = int('0')
PIPE = '1' == '1'
PIPE2 = '1' == '1'
STAGE = '1' == '1'
NOSTG = '0' == '1'
EMBED_WAIT = True
TRANSITIVE = True
EPS = 1e-6
DEPTH = 2
ALPHA = (2 * DEPTH) ** 0.25
IN_DIM = 3860
MIX_RANGE = [(0, 1032), (1032, 1804), (1804, 2828), (2828, 3860)]


class Tk:
    __slots__ = ("t", "w", "r", "ex")

    def __init__(self, t, ex=False):
        self.t = t
        self.w = {}
        self.r = {}
        self.ex = ex

    def __getitem__(self, idx):
        return self.t[idx]


class Eng:
    def __init__(self, K, name, h):
        self.K = K
        self.name = name
        self.h = h
        self.sems = []
        self.n = 0
        self.seen = {}
        self.log = []
        self.mpos = {}
        self.newsem()

    def newsem(self):
        s = self.K.nc.alloc_semaphore(f"s_{self.name}_{len(self.sems)}")
        self.sems.append(s)
        self.K.sem_owner[id(s)] = (self, len(self.sems) - 1)
        self.n = 0


class KB:
    def __init__(self, nc, ndma_sems=32):
        self.nc = nc
        self.sem_owner = {}
        self.pe = Eng(self, "pe", nc.tensor)
        self.act = Eng(self, "act", nc.scalar)
        self.dve = Eng(self, "dve", nc.vector)
        self.pool = Eng(self, "pool", nc.gpsimd)
        self.sp = Eng(self, "sp", nc.sync)
        self.dsems = [[nc.alloc_semaphore(f"dma{i}"), 0] for i in range(ndma_sems)]
        self.dnext = 0
        self.ninst = 0
        self.defer = None

    def _wait(self, eng, ev):
        sem, val = ev
        key = id(sem)
        if eng.seen.get(key, 0) >= val:
            return
        if self.defer is not None:
            cur = self.defer.get(key)
            if cur is None or cur[1] < val:
                self.defer[key] = (sem, val)
            return
        eng.h.wait_ge(sem, val)
        self._learn(eng, sem, val)

    def _learn(self, eng, sem, val):
        stack = [(sem, val)]
        while stack:
            sm, vl = stack.pop()
            key = id(sm)
            if eng.seen.get(key, 0) >= vl:
                continue
            eng.seen[key] = vl
            eng.log.append((len(eng.sems) - 1, eng.n + 1, sm, vl))
            own = self.sem_owner.get(key)
            if own is None or not TRANSITIVE:
                continue
            x, k = own
            if x is eng:
                continue
            pos = eng.mpos.get(x.name, 0)
            lg = x.log
            while pos < len(lg) and (lg[pos][0], lg[pos][1]) <= (k, vl):
                stack.append((lg[pos][2], lg[pos][3]))
                pos += 1
            eng.mpos[x.name] = pos

    def _flush(self, eng, keep_last):
        items = list(self.defer.values())
        self.defer = None
        last = None
        if keep_last and items:
            last = items.pop()
            self._learn(eng, last[0], last[1])
        for sem, val in items:
            if eng.seen.get(id(sem), 0) >= val:
                continue
            eng.h.wait_ge(sem, val)
            self._learn(eng, sem, val)
        return last

    def _deps(self, eng, w, r, skip_same=False):
        evs = []
        for b in r:
            evs.extend(b.w.values())
            if b.ex:
                evs.extend(b.r.values())
        for b in w:
            evs.extend(b.w.values())
            evs.extend(b.r.values())
        mysem = eng.sems[-1]
        for ev in evs:
            if skip_same and ev[0] is mysem:
                continue
            self._wait(eng, ev)

    def op(self, eng, fn, w=(), r=()):
        if eng.n >= SEM_ROT:
            eng.newsem()
        self.defer = {} if EMBED_WAIT else None
        self._deps(eng, w, r, skip_same=(eng is self.pe))
        last = self._flush(eng, True) if EMBED_WAIT else None
        ins = fn(eng.h)
        if last is not None:
            ins._wait_ge(last[0], last[1])
        eng.n += 1
        ins.then_inc(eng.sems[-1], 1)
        ev = (eng.sems[-1], eng.n)
        for b in w:
            b.w[id(ev[0])] = ev
        for b in r:
            b.r[id(ev[0])] = ev
        self.ninst += 1
        return ins

    def dma(self, eng, out_ap, in_ap, w=(), r=(), **kw):
        slot = self.dsems[self.dnext]
        self.dnext = (self.dnext + 1) % len(self.dsems)
        sem, val = slot
        if val > 0:
            self._wait(eng, (sem, val))
        self._deps(eng, w, r)
        ins = eng.h.dma_start(out=out_ap, in_=in_ap, **kw)
        slot[1] = val + 16
        ins.then_inc(sem, 16)
        ev = (sem, slot[1])
        for b in w:
            b.w[id(ev[0])] = ev
        for b in r:
            b.r[id(ev[0])] = ev
        self.ninst += 1
        return ev

    def barrier(self):
        engs = (self.pe, self.act, self.dve, self.pool, self.sp)
        for e in engs:
            for sem, val in self.dsems:
                if val > 0:
                    self._wait(e, (sem, val))
            for o in engs:
                if o is not e and o.n > 0:
                    self._wait(e, (o.sems[-1], o.n))

    def finish(self):
        for sem, val in self.dsems:
            if val > 0:
                self._wait(self.sp, (sem, val))
        for e in (self.pe, self.act, self.dve, self.pool):
            if e.n > 0:
                self._wait(self.sp, (e.sems[-1], e.n))


def _const_table(NS):
    cols = {}
    parts = []
    off = [0]

    def add(name, arr):
        arr = np.asarray(arr, np.float32)
        a = np.zeros((128, arr.shape[1]), np.float32)
        a[: arr.shape[0]] = arr
        cols[name] = (off[0], arr.shape[1])
        off[0] += arr.shape[1]
        parts.append(a)

    i64 = np.arange(64)
    t, r = i64[:, None], i64[None, :]
    add("ident", np.eye(128))
    add("ones", np.ones((128, 64)))
    add("I4", np.eye(4))
    add("oh4", np.repeat(np.eye(4), 64, axis=1))
    add("zeros", np.zeros((64, 264)))
    for sfx, same in (("", np.ones((64, 64), bool)), ("S", (t // 4) == (r // 4))):
        incl = (t <= r) & same
        gt = (t > r) & same
        add("incl" + sfx, incl)
        add("gt" + sfx, gt)
        add("neg4" + sfx, np.tile(np.where(incl, 0.0, -30000.0), (1, 4)))
        add("strict" + sfx, (t < r) & same)
    add("rel", ((t <= r).astype(np.float32) - (t <= 31).astype(np.float32)) * np.ones((64, 64)))
    seqoh = (i64[:, None] // 4) == np.arange(NS)[None, :]
    add("seqoh", seqoh)
    cm = ((i64[None, :] // 4) == np.arange(NS)[:, None]).astype(np.float32).reshape(1, NS * 64)
    add("colmask", np.tile(cm, (64, 1)))
    return np.concatenate(parts, axis=1), cols


PROW = {}
_o = 0
for _n, _w in (("norm_a_g", 256), ("norm_b_g", 256), ("norm_c_g", 256), ("norm_d_g", 256),
               ("lbl0", 256), ("lbl1", 256), ("a_log_a", 4), ("dt_bias_a", 4), ("a_log_b", 4),
               ("dt_bias_b", 4), ("d_skip_b", 4), ("pad", 4)):
    PROW[_n] = (_o, _w)
    _o += _w
NPROW = _o
NPCOL = 6 * 4 + 6 + 4 * 4 + 4 + 2


def build(nc, T=2048, NS=16, L=2):
    K = KB(nc)
    NSK = NS * 4
    assert NSK == 64 and T % 128 == 0
    NTP = T // 128
    NT = T + NSK
    NTILES = NTP + 1
    NCH = T // 64 + 1
    cst_np, CO = _const_table(NS)
    NCST = cst_np.shape[1]

    def din(name, shape):
        return nc.dram_tensor(name, list(shape), F32, kind="ExternalInput").ap()

    def dout(name, shape):
        return nc.dram_tensor(name, list(shape), F32, kind="ExternalOutput").ap()

    xin = din("xin", (NT, 1024))
    w_in = din("w_in", (L, 1024, IN_DIM))
    w_out = din("w_out", (L, 1024, 1024))
    w_up = din("w_up", (L, 1024, 4096))
    w_down = din("w_down", (L, 4096, 1024))
    prow_d = din("prow", (L, 128, NPROW))
    lnrow_d = din("lnrow", (2 * L + 1, 128, 2048))
    pcol_d = din("pcol", (L, 128, NPCOL))
    cst_d = din("cst", (128, NCST))
    s_conv_d = [din("sa_conv", (L, NS * 3, 768)), din("sb_conv", (L, NS * 3, 512))]
    s_ssm_d = [din("sa_ssm", (L, NS, 4, 64, 64)), din("sb_ssm", (L, NS, 4, 64, 64)),
               din("sc_ssm", (L, NS, 4, 64, 64)), din("sd_cmem", (L, NS, 4, 64, 64))]
    s_nvec_d = din("sd_nvec", (L, NS * 4, 64))
    s_m_d = din("sd_m", (L, 4, NS))

    y_d = dout("y", (NT, 1024))
    p_conv_o = [dout("pa_conv", (L, 3, 768)), dout("pb_conv", (L, 3, 512))]
    p_ssm_o = [dout("pa_ssm", (L, 4, 64, 64)), dout("pb_ssm", (L, 4, 64, 64)),
               dout("pc_ssm", (L, 4, 64, 64)), dout("pd_cmem", (L, 4, 64, 64))]
    p_nvec_o = dout("pd_nvec", (L, 4, 64))
    p_m_o = dout("pd_m", (L, 4, 1))
    o_conv_o = [dout("oa_conv", (L, NS * 3, 768)), dout("ob_conv", (L, NS * 3, 512))]
    o_ssm_o = [dout("oa_ssm", (L, NS, 4, 64, 64)), dout("ob_ssm", (L, NS, 4, 64, 64)),
               dout("oc_ssm", (L, NS, 4, 64, 64)), dout("od_cmem", (L, NS, 4, 64, 64))]
    o_nvec_o = dout("od_nvec", (L, NS * 4, 64))
    o_m_o = dout("od_m", (L, 4, NS))

    def sb(name, shape, dt=F32):
        return Tk(nc.alloc_sbuf_tensor(name, list(shape), dt))

    Rt = [Tk(None) for _ in range(NTILES)]
    R_all = nc.alloc_sbuf_tensor("R", [128, NTILES, 1024], F32)
    for i in range(NTILES):
        Rt[i].t = R_all[:, i, :]
    AT_all = nc.alloc_sbuf_tensor("AT", [128, 8, NT], BF16)
    ATc = [Tk(AT_all[:, :, 64 * c:64 * c + 64]) for c in range(NCH)]
    CST = sb("CST", (128, NCST))
    PRW = sb("PRW", (128, NPROW))
    LNR = sb("LNR", (128, 2048))
    PCL = sb("PCL", (128, NPCOL))
    XF = sb("XF", (128, 8 * 64))
    ARENA_COLS = 20400 - 3856
    arena_t = nc.alloc_sbuf_tensor("ARENA", [128, ARENA_COLS], F32)
    aoff = [0]

    def ar(ncols, dt=F32, shape=None):
        ncols = (ncols + 7) // 8 * 8
        assert aoff[0] + ncols <= ARENA_COLS, ("arena overflow", aoff[0], ncols)
        ap = arena_t[:, aoff[0]:aoff[0] + ncols]
        aoff[0] += ncols
        if dt == BF16:
            ap = ap.bitcast(BF16)
        return Tk(ap)

    OPB_COLS = 3856
    opb_t = nc.alloc_sbuf_tensor("OPB", [128, OPB_COLS], F32)
    boff = [0]

    def arb(ncols):
        ncols = (ncols + 7) // 8 * 8
        assert boff[0] + ncols <= OPB_COLS, ("opb overflow", boff[0], ncols)
        ap = opb_t[:, boff[0]:boff[0] + ncols]
        boff[0] += ncols
        return Tk(ap)

    PJ = Tk(nc.alloc_psum_tensor("PJ", [128, 1024], F32), ex=True)
    PDUM = Tk(nc.alloc_psum_tensor("PDUM", [128, 512], F32), ex=True)
    PX = Tk(nc.alloc_psum_tensor("PX", [128, 512], F32), ex=True)
    PD = Tk(nc.alloc_psum_tensor("PD", [128, 512], F32), ex=True)
    PA = Tk(nc.alloc_psum_tensor("PA", [128, 512], F32), ex=True)
    PO = Tk(nc.alloc_psum_tensor("PO", [128, 512], F32), ex=True)
    PS = Tk(nc.alloc_psum_tensor("PS", [128, 512], F32), ex=True)

    pe, act, dve, pool, sp = K.pe, K.act, K.dve, K.pool, K.sp
    stq = {'sp': sp, 'pool': pool, 'act': act}[STQ]

    def TT(eng, out, a, b, op, w, r):
        K.op(eng, lambda e: e.tensor_tensor(out=out, in0=a, in1=b, op=op), w=w, r=r)

    def TS(eng, out, a, s1, op0, w, r, s2=None, op1=None):
        if op1 is None:
            K.op(eng, lambda e: e.tensor_scalar(out=out, in0=a, scalar1=s1, scalar2=None, op0=op0), w=w, r=r)
        else:
            K.op(eng, lambda e: e.tensor_scalar(out=out, in0=a, scalar1=s1, scalar2=s2, op0=op0, op1=op1), w=w, r=r)

    def STT(out, a, s, b, op0, op1, w, r):
        K.op(dve, lambda e: e.scalar_tensor_tensor(out=out, in0=a, scalar=s, in1=b, op0=op0, op1=op1), w=w, r=r)

    def ACT(out, a, func, w, r, bias=None, scale=None):
        kw = {}
        if bias is not None:
            kw["bias"] = bias
        if scale is not None:
            kw["scale"] = scale
        K.op(act, lambda e: e.activation(out=out, in_=a, func=func, **kw), w=w, r=r)

    def SIG(out, a, w, r):
        ACT(out, a, AF.Exp, w=w, r=r, scale=-1.0)
        ACT(out, out, AF.Ln, w=w, r=w, bias=1.0)
        ACT(out, out, AF.Exp, w=w, r=w, scale=-1.0)

    def RSQ(out, a, w, r):
        ACT(out, a, AF.Ln, w=w, r=r)
        ACT(out, out, AF.Exp, w=w, r=w, scale=-0.5)

    def CPA(out, a, w, r):
        K.op(act, lambda e: e.copy(out=out, in_=a), w=w, r=r)

    def CPV(out, a, w, r, eng=None):
        K.op(eng or dve, lambda e: e.tensor_copy(out=out, in_=a), w=w, r=r)

    dstate = {"on": False, "n": 0}

    def _dummy():
        if DUMMY and dstate["on"]:
            dstate["n"] += 1
            if dstate["n"] % DUMMY == 0:
                K.op(pe, lambda e: e.matmul(PDUM[0:128, 0:min(512, NT)], lhsT=AT_all[:, 0, 0:128], rhs=AT_all[:, 1, 0:min(512, NT)],
                                            start=True, stop=True), w=[PDUM])

    def MM(out, lhsT, rhs, st, sp_, w, r):
        K.op(pe, lambda e: e.matmul(out, lhsT=lhsT, rhs=rhs, start=st, stop=sp_), w=w, r=r)
        if sp_:
            _dummy()

    def TPo(out, a, idn, w, r):
        K.op(pe, lambda e: e.transpose(out=out, in_=a, identity=idn), w=w, r=r)
        _dummy()

    def RECIP(out, a, w, r):
        K.op(dve, lambda e: e.reciprocal(out=out, in_=a), w=w, r=r)

    def MEMSET(eng, out, v, w):
        K.op(eng, lambda e: e.memset(out, v), w=w)

    def C(name, rows=64, sample=False):
        o, n = CO[name + ("S" if sample else "")]
        return CST[0:rows, o:o + n]

    ident = lambda n: CST[0:n, 0:n]

    def b3(ap2, n):
        return ap2.unsqueeze(2).broadcast_to([ap2.shape[0], ap2.shape[1], n])

    def bm(ap2, n):
        return ap2.unsqueeze(1).broadcast_to([ap2.shape[0], n, ap2.shape[1]])

    def v3(ap2, a):
        return ap2.rearrange("p (a b) -> p a b", a=a)

    K.dma(sp, CST[:, :], cst_d[:, :], w=[CST])

    STATs = [sb("STAT0", (128, 16)), sb("STAT1", (128, 16))]
    MVs = [sb("MV0", (128, 8)), sb("MV1", (128, 8))]
    XTs = [Tk(arena_t[:, 0:1024]), Tk(arena_t[:, 1024:2048])]

    def ln_tile(Xtk, Xap, P, OUTtk, OUTap, par=0):
        STAT, MV = STATs[par], MVs[par]
        for hf in range(2):
            K.op(dve, lambda e: e.bn_stats(out=STAT[0:P, hf * 6:hf * 6 + 6], in_=Xap[:, hf * 512:(hf + 1) * 512]),
                 w=[STAT], r=[Xtk])
        K.op(dve, lambda e: e.bn_aggr(out=MV[0:P, 0:2], in_=STAT[0:P, 0:12]), w=[MV], r=[STAT])
        TS(dve, MV[0:P, 2:3], MV[0:P, 1:2], EPS, ALU.add, w=[MV], r=[MV])
        RSQ(MV[0:P, 3:4], MV[0:P, 2:3], w=[MV], r=[MV])
        TS(dve, OUTap, Xap, MV[0:P, 0:1], ALU.subtract, w=[OUTtk], r=[Xtk, MV], s2=MV[0:P, 3:4], op1=ALU.mult)
        TT(pool, OUTap, OUTap, LNR[0:P, 0:1024], ALU.mult, w=[OUTtk], r=[OUTtk, LNR])
        TT(pool, OUTap, OUTap, LNR[0:P, 1024:2048], ALU.add, w=[OUTtk], r=[OUTtk, LNR])

    def to_AT(SRCtk, SRCap, i, P):
        cs = [ATc[2 * i]] + ([ATc[2 * i + 1]] if P == 128 else [])
        for hf in range(2):
            PB = (PX, PD, PA, PO)[(2 * i + hf) % 4]
            for b in range(4):
                TPo(PB[0:128, b * 128:b * 128 + P], SRCap[0:P, (hf * 4 + b) * 128:(hf * 4 + b + 1) * 128],
                    ident(P), w=[PB], r=[SRCtk, CST])
            CPA(AT_all[:, hf * 4:hf * 4 + 4, 128 * i:128 * i + P], v3(PB[0:128, 0:512], 4)[:, :, 0:P], w=cs, r=[PB])
            if i == 0:
                CPV(v3(XF[:, 0:512], 8)[:, hf * 4:hf * 4 + 4, :], v3(PB[0:128, 0:512], 4)[:, :, 0:64], w=[XF], r=[PB])

    def tile_P(i):
        return 128 if i < NTP else NSK

    K.dma(sp, LNR[:, :], lnrow_d[0], w=[LNR])
    for i in range(NTILES):
        P = tile_P(i)
        K.dma(sp, Rt[i][0:P, :], xin[128 * i:128 * i + P, :], w=[Rt[i]])
        XT = XTs[i % 2]
        ln_tile(Rt[i], Rt[i][0:P, :], P, XT, XT[0:P, :], i % 2)
        to_AT(XT, XT, i, P)
        K.op(act, lambda e: e.mul(out=Rt[i][0:P, :], in_=XT[0:P, :], mul=ALPHA), w=[Rt[i]], r=[XT])

    aoff[0] = 0
    Wm = ar(8 * 1032 // 2, BF16)
    Wo = ar(2 * 1024 // 2, BF16)
    U_main = ar(1032)
    RAW = [ar(6 * NS * 7), ar(6 * 67)]
    RAWS = RAW[0]
    CV = ar(768)
    SM = ar(64)
    SM2 = ar(64)
    DS = ar(64)
    LL = ar(8)
    LM = ar(256)
    LLS = ar(4 * NS)
    QT = arb(256)
    QPT = arb(256)
    KT = arb(256)
    WT = arb(256)
    QN = ar(256)
    KN = ar(256)
    KP = arb(256)
    RH = arb(512)
    UW = ar(512)
    DT = ar(256)
    ATT = arb(256)
    BA = [arb(256), arb(256)]
    BTA = [arb(256), arb(256)]
    RR = [arb(256), arb(256)]
    TMP = ar(512)
    TMP2 = ar(256)
    Ob = ar(264)
    VA = arb(264)
    Yb = ar(256)
    YT = ar(64, BF16)
    SPs = ar(264)
    SSs = [ar(264), ar(264)]
    SNV = ar(64)
    QM = ar(512)
    WM_ = QM
    WST = [ar(1032), ar(1032)]
    YT32 = ar(128)
    VM = ar(264)
    GI = ar(72)
    GF = ar(72)
    LF = ar(72)
    BC = ar(72)
    AA = ar(72)
    MGX = ar(72)
    EXS = ar(72)
    EXK = ar(72)
    EXM = ar(72)
    NMG = ar(72)
    AM = ar(256)
    DG = ar(64)
    MINI = ar(16)
    MFIN = ar(16)
    NEGF = ar(8)
    ONESROW = ar(72)
    LBR = ar(256)
    NEA = ar(8)
    mixer_arena_end = aoff[0]

    Wm3 = Wm[:, 0:8 * 1032].rearrange("p (k c) -> p k c", k=8)
    Wo3 = Wo[:, 0:2048].rearrange("p (k c) -> p k c", k=2)

    def prow(name, P=64):
        o, n = PROW[name]
        return PRW[0:P, o:o + n]

    tpc = [0]

    def tp4(srcs, DSTtk, rd, RO=lambda a: a):
        tpc[0] += 1
        PB = (PX, PJ)[tpc[0] % 2]
        for h in range(4):
            TPo(PB[0:64, h * 64:(h + 1) * 64], srcs[h], ident(64), w=[PB], r=rd + [CST])
        CPA(RO(DSTtk[0:64, 0:256]), PB[0:64, 0:256], w=[DSTtk], r=[PB])

    def mixer_pass(l, m):
        c0, c1 = MIX_RANGE[m]
        ncol = c1 - c0
        dvw = 66 if m == 3 else 64
        nconv = {0: 6, 1: 4}.get(m, 0)
        convoff = {0: 0, 1: 256}.get(m, 0)
        XF3 = v3(XF[:, 0:512], 8)
        ngrp0 = (ncol + 511) // 512
        zl = C("zeros")
        if nconv and FP32_MIX:
            MM(PX[0:128, 0:nconv * 64], zl[:, 0:128], CST[0:64, 0:nconv * 64], True, False, w=[PX], r=[CST])
        for k in range(8):
            ws = WST[k % 2]
            if not NOSTG:
                K.dma(stq, ws[:, 0:ncol], w_in[l, k * 128:(k + 1) * 128, c0:c1], w=[ws])
            for g in range(ngrp0 if FP32_MIX else 0):
                wd = min(512, ncol - g * 512)
                if g == 2:
                    MM(PS[0:64, 384:384 + wd], XF3[:, k, :], ws[:, g * 512:g * 512 + wd], k == 0, k == 7, w=[PS], r=[XF, ws])
                else:
                    MM(PJ[0:64, g * 512:g * 512 + wd], XF3[:, k, :], ws[:, g * 512:g * 512 + wd], k == 0, k == 7, w=[PJ], r=[XF, ws])
            for b in range(nconv if FP32_MIX else 0):
                MM(PX[0:128, b * 64:(b + 1) * 64], ws[:, convoff + b * 128:convoff + (b + 1) * 128], XF3[:, k, :],
                   False, (k == 7 and b == nconv - 1), w=[PX], r=[XF, ws])
            if STAGE:
                K.op(pool, lambda e: e.tensor_copy(out=Wm3[:, k, 0:ncol], in_=ws[:, 0:ncol]), w=[Wm], r=[ws])
            else:
                K.dma(pool, Wm3[:, k, 0:ncol], w_in[l, k * 128:(k + 1) * 128, c0:c1], w=[Wm])
        for b in range(2):
            if not NOSTG:
                K.dma(stq, WST[b][:, 0:1024], w_out[l, 256 * m + 128 * b:256 * m + 128 * (b + 1), :], w=[WST[b]])
            if STAGE:
                K.op(pool, lambda e: e.tensor_copy(out=Wo3[:, b, :], in_=WST[b][:, 0:1024]), w=[Wo], r=[WST[b]])
            else:
                K.dma(pool, Wo3[:, b, :], w_out[l, 256 * m + 128 * b:256 * m + 128 * (b + 1), :], w=[Wo])
        MEMSET(dve, SPs[0:64, 0:264], 0.0, w=[SPs])
        if m == 3:
            K.dma(sp, MINI[0:4, 0:NS], s_m_d[l], w=[MINI])
            MEMSET(dve, MGX[0:4, 0:1], 0.0, w=[MGX])
            MEMSET(dve, BC[0:4, 0:1], 0.0, w=[BC])
            MEMSET(dve, ONESROW[0:4, 0:72], 1.0, w=[ONESROW])
            TS(dve, NEGF[0:4, 0:1], PCL[0:4, 51:52], -1.0, ALU.mult, w=[NEGF], r=[PCL])

        def chunk_body(c):
            sample = (c == NCH - 1)
            tok0 = 64 * c
            i_tile, pb = c // 2, 64 * (c % 2)
            Mk = lambda name, rows=64: C(name, rows, sample)
            lp = LOWP and (0 < c < NCH - 1)
            RO = (lambda a: a.bitcast(F32R)) if lp else (lambda a: a)
            ROW = (lambda a: a.bitcast(F32R)) if LOWP else (lambda a: a)
            if m == 0:
                bgr, cgr = [(768, 256), (1024, 8)], [(0, 512), (512, 256)]
                ber, cer = [(768, 1032)], [(0, 768)]
            elif m == 1:
                bgr, cgr = [(0, 256), (768, 4)], [(256, 256), (512, 256)]
                ber, cer = [(0, 256), (768, 772)], [(256, 768)]
            elif m == 2:
                bgr, cgr, ber, cer = [(0, 512), (512, 512)], [], [(0, 1024)], []
            else:
                bgr, cgr, ber, cer = [(0, 512), (512, 512), (1024, 8)], [], [(0, 1032)], []
            paired_even = (not sample) and c >= 2 and c % 2 == 0 and c + 1 <= NCH - 2
            paired_odd = (not sample) and c >= 3 and c % 2 == 1
            need_raw = bool(nconv) and (sample or c == NCH - 2 or (paired_even and c + 1 == NCH - 2))
            grps = bgr + (cgr if need_raw else [])
            evr = ber + (cer if need_raw else [])
            U = WST[1] if ((not sample) and c % 2 == 1) else U_main
            do_proj = not ((c == 0 and FP32_MIX) or paired_odd)
            Pm = 128 if paired_even else 64
            if do_proj:
                rtk = [ATc[c], Wm] + ([ATc[c + 1]] if paired_even else [])
                for (g0, wd) in grps:
                    for k in range(8):
                        if g0 == 1024:
                            MM(PS[0:Pm, 384:384 + wd], AT_all[:, k, tok0:tok0 + Pm], Wm3[:, k, g0:g0 + wd],
                               k == 0, k == 7, w=[PS], r=rtk)
                        else:
                            MM(PJ[0:Pm, g0:g0 + wd], AT_all[:, k, tok0:tok0 + Pm], Wm3[:, k, g0:g0 + wd],
                               k == 0, k == 7, w=[PJ], r=rtk)
            if nconv and not (c == 0 and FP32_MIX):
                for b in range(nconv):
                    for k in range(8):
                        MM(PX[0:128, b * 64:(b + 1) * 64], Wm3[:, k, convoff + b * 128:convoff + (b + 1) * 128],
                           AT_all[:, k, tok0:tok0 + 64], k == 0, k == 7, w=[PX], r=[ATc[c], Wm])
            yield 'P1'
            if c == 0 and FP32_MIX:
                CPA(U[0:64, 0:min(ncol, 1024)], PJ[0:64, 0:min(ncol, 1024)], w=[U], r=[PJ])
                if ncol > 1024:
                    CPA(U[0:64, 1024:ncol], PS[0:64, 384:384 + ncol - 1024], w=[U], r=[PS])
            elif paired_odd:
                for (e0, e1) in evr:
                    K.dma(sp, WST[1][0:64, e0:e1], U_main[64:128, e0:e1], w=[WST[1]], r=[U_main])
            else:
                for (e0, e1) in evr:
                    e1j = min(e1, 1024)
                    CPA(U[0:Pm, e0:e1j], PJ[0:Pm, e0:e1j], w=[U], r=[PJ])
                    if e1 > 1024:
                        CPA(U[0:Pm, 1024:e1], PS[0:Pm, 384:384 + e1 - 1024], w=[U], r=[PS])

            yield 'step'
            if nconv:
                nch_ = nconv * 128
                cst_o = (p_conv_o if not sample else o_conv_o)[m]
                if sample:
                    for j in range(3):
                        K.dma(sp, cst_o[l].rearrange("(s j) c -> j s c", j=3)[j],
                              U[1 + j:64:4, convoff:convoff + nch_], r=[U])
                elif c == NCH - 2:
                    K.dma(sp, cst_o[l], U[61:64, convoff:convoff + nch_], r=[U])
                wof = 0 if m == 0 else 30
                if not sample:
                    Rw = RAW[c % 2]
                    Rp = RAW[(c + 1) % 2]
                    R3 = v3(Rw[:, 0:6 * 67], 6)
                    P3 = v3(Rp[:, 0:6 * 67], 6)
                    CPA(R3[:, 0:nconv, 3:67], v3(PX[0:128, 0:nconv * 64], nconv), w=[Rw], r=[PX])
                    if c == 0:
                        MEMSET(dve, R3[:, 0:nconv, 0:3], 0.0, w=[Rw])
                    else:
                        CPV(R3[:, 0:nconv, 0:3], P3[:, 0:nconv, 64:67], w=[Rw], r=[Rp])
                    srcs = lambda b, j: R3[:, b, j:j + 64]
                    outv = lambda b: CV[:, b * 64:(b + 1) * 64]
                    rtk = Rw
                else:
                    R4 = RAWS[:, 0:6 * NS * 7].rearrange("p (b s j) -> p b s j", b=6, s=NS)
                    K.dma(sp, CV[0:NS * 3, 0:nch_], s_conv_d[m][l], w=[CV])
                    cb = CV
                    cbap = CV[0:NS * 3, 0:nch_]
                    for b in range(nconv):
                        TPo(PD[0:128, b * 48:(b + 1) * 48], cbap[:, b * 128:(b + 1) * 128],
                            ident(48),
                            w=[PD], r=[cb, CST])
                    for b in range(nconv):
                        CPA(R4[:, b, :, 0:3], PD[0:128, b * 48:(b + 1) * 48].rearrange("p (s j) -> p s j", j=3),
                            w=[RAWS], r=[PD])
                        CPA(R4[:, b, :, 3:7], PX[0:128, b * 64:(b + 1) * 64].rearrange("p (s j) -> p s j", j=4),
                            w=[RAWS], r=[PX])
                    srcs = lambda b, j: R4[:, b, :, j:j + 4]
                    outv = lambda b: CV[:, b * 64:(b + 1) * 64].rearrange("p (s j) -> p s j", j=4)
                    rtk = RAWS
                for b in range(nconv):
                    wc = lambda j: PCL[:, wof + b * 4 + j:wof + b * 4 + j + 1]
                    bc_ = PCL[:, wof + nconv * 4 + b:wof + nconv * 4 + b + 1]
                    TS(dve, outv(b), srcs(b, 0), wc(0), ALU.mult, w=[CV], r=[rtk, PCL], s2=bc_, op1=ALU.add)
                    for j in range(1, 4):
                        STT(outv(b), srcs(b, j), wc(j), outv(b), ALU.mult, ALU.add, w=[CV], r=[rtk, PCL, CV])
                        if j == 2:
                            yield 'step'
                    yield 'step'
                SIG(CV[:, 384:384 + nconv * 64], CV[:, 0:nconv * 64], w=[CV], r=[CV])
                yield 'step'
                TT(pool, CV[:, 0:nconv * 64], CV[:, 0:nconv * 64], CV[:, 384:384 + nconv * 64], ALU.mult, w=[CV], r=[CV])
                yield 'step'
                for b in range(nconv):
                    TPo(PJ[0:64, b * 128:(b + 1) * 128], CV[:, b * 64:(b + 1) * 64], ident(128), w=[PJ], r=[CV, CST])
                yield 'step'
                CPA(U[0:64, convoff:convoff + nch_], PJ[0:64, 0:nch_], w=[U], r=[PJ])

            yield 'P2'
            Utk = U
            if m in (0, 1):
                ell = LL[0:64, 0:4]
                if m == 0:
                    q_ap, k_ap, v_ap = U[0:64, 0:256], U[0:64, 256:512], U[0:64, 512:768]
                    z_ap, b_ap, a_ap = U[0:64, 768:1024], U[0:64, 1024:1028], U[0:64, 1028:1032]
                    dtb, nea = prow("dt_bias_a"), NEA[0:64, 0:4]
                else:
                    z_ap, x_ap = U[0:64, 0:256], U[0:64, 256:512]
                    Bm, Cm, a_ap = U[0:64, 512:640], U[0:64, 640:768], U[0:64, 768:772]
                    dtb, nea = prow("dt_bias_b"), NEA[0:64, 4:8]
                    for g in range(2):
                        TPo(PX[0:64, g * 64:(g + 1) * 64], Cm[:, g * 64:(g + 1) * 64], ident(64), w=[PX], r=[U, CST])
                        TPo(PX[0:64, 128 + g * 64:128 + (g + 1) * 64], Bm[:, g * 64:(g + 1) * 64], ident(64), w=[PX], r=[U, CST])
                    CPA(ROW(QT[0:64, 0:128]), PX[0:64, 0:128], w=[QT], r=[PX])
                    CPA(ROW(KT[0:64, 0:128]), PX[0:64, 128:256], w=[KT], r=[PX])
                TT(dve, SM2[0:64, 20:24], a_ap, dtb, ALU.add, w=[SM2], r=[U, PRW])
                ACT(SM2[0:64, 20:24], SM2[0:64, 20:24], AF.Exp, w=[SM2], r=[SM2])
                ACT(SM2[0:64, 24:28], SM2[0:64, 20:24], AF.Ln, w=[SM2], r=[SM2], bias=1.0)
                TT(dve, ell, SM2[0:64, 24:28], nea, ALU.mult, w=[LL], r=[SM2, NEA])
                TT(dve, v3(LM[0:64, 0:256], 4), bm(Mk("gt"), 4), b3(ell, 64), ALU.mult, w=[LM], r=[LL, CST])
                MM(PS[0:64, 0:4], Mk("incl"), ell, True, True, w=[PS], r=[CST, LL])
                MM(PS[0:64, 4:8], Mk("gt"), ell, True, True, w=[PS], r=[CST, LL])
                if not sample:
                    MM(PS[0:64, 8:12], C("ones"), ell, True, True, w=[PS], r=[CST, LL])
                    nds = 4
                else:
                    TT(dve, v3(LLS[0:64, 0:4 * NS], 4), b3(ell, NS), bm(C("seqoh"), 4), ALU.mult, w=[LLS], r=[LL, CST])
                    MM(PS[0:64, 8:8 + 4 * NS], C("ones"), LLS[0:64, 0:4 * NS], True, True, w=[PS], r=[CST, LLS])
                    nds = 4 * NS
                MM(PD[0:64, 0:256], ident(64), Mk("neg4"), True, False, w=[PD], r=[CST])
                for h in range(4):
                    MM(PD[0:64, h * 64:(h + 1) * 64], LM[0:64, h * 64:(h + 1) * 64], Mk("incl"), False, h == 3,
                       w=[PD], r=[LM, CST])
                ACT(SM[0:64, 0:8], PS[0:64, 0:8], AF.Exp, w=[SM], r=[PS])
                ACT(DS[0:64, 0:nds], PS[0:64, 8:8 + nds], AF.Exp, w=[DS], r=[PS])
                ACT(DT[0:64, 0:256], PD[0:64, 0:256], AF.Exp, w=[DT], r=[PD])
                eg, ek = SM[0:64, 0:4], SM[0:64, 4:8]
            if m == 0:
                TT(dve, TMP[0:64, 0:512], U[0:64, 0:512], U[0:64, 0:512], ALU.mult, w=[TMP], r=[U])
                K.op(dve, lambda e: e.tensor_reduce(out=SM2[0:64, 0:8], in_=v3(TMP[0:64, 0:512], 8), axis=AX.X, op=ALU.add),
                     w=[SM2], r=[TMP])
                TS(dve, SM2[0:64, 0:8], SM2[0:64, 0:8], EPS, ALU.add, w=[SM2], r=[SM2])
                RSQ(SM2[0:64, 8:16], SM2[0:64, 0:8], w=[SM2], r=[SM2])
                TT(dve, v3(QN[0:64, 0:256], 4), v3(q_ap, 4), b3(SM2[0:64, 8:12], 64), ALU.mult, w=[QN], r=[U, SM2])
                TS(dve, QN[0:64, 0:256], QN[0:64, 0:256], 0.125, ALU.mult, w=[QN], r=[QN])
                TT(dve, v3(KN[0:64, 0:256], 4), v3(k_ap, 4), b3(SM2[0:64, 12:16], 64), ALU.mult, w=[KN], r=[U, SM2])
                SIG(SM2[0:64, 16:20], b_ap, w=[SM2], r=[U])
                beta = SM2[0:64, 16:20]
                tp4([KN[0:64, h * 64:(h + 1) * 64] for h in range(4)], KT, [KN], ROW)
                TT(dve, v3(TMP2[0:64, 0:256], 4), v3(DT[0:64, 0:256], 4), bm(Mk("strict"), 4), ALU.mult, w=[TMP2], r=[DT, CST])
                TT(dve, v3(TMP2[0:64, 0:256], 4), v3(TMP2[0:64, 0:256], 4), b3(beta, 64), ALU.mult, w=[TMP2], r=[TMP2, SM2])
                TS(dve, TMP2[0:64, 0:256], TMP2[0:64, 0:256], -1.0, ALU.mult, w=[TMP2], r=[TMP2])
                RH3 = ROW(v3(RH[0:64, 0:512], 4))
                side = [
                    lambda: tp4([QN[0:64, h * 64:(h + 1) * 64] for h in range(4)], QT, [QN], ROW),
                    lambda: (TT(pool, v3(TMP[0:64, 0:256], 4), v3(QN[0:64, 0:256], 4), b3(eg, 64), ALU.mult, w=[TMP], r=[QN, SM]),
                             tp4([TMP[0:64, h * 64:(h + 1) * 64] for h in range(4)], QPT, [TMP], ROW)),
                    lambda: TT(dve, ROW(v3(KP[0:64, 0:256], 4)), v3(KN[0:64, 0:256], 4), b3(ek, 64), ALU.mult, w=[KP], r=[KN, SM]),
                    lambda: (CPV(RH3[:, :, 0:64], v3(v_ap, 4), w=[RH], r=[U]),
                             TT(dve, RH3[:, :, 64:128], v3(KN[0:64, 0:256], 4), b3(eg, 64), ALU.mult, w=[RH], r=[KN, SM])),
                ]
                for h in range(4):
                    MM(PA[0:64, h * 64:(h + 1) * 64], RO(KT[0:64, h * 64:(h + 1) * 64]), RO(KT[0:64, h * 64:(h + 1) * 64]),
                       True, True, w=[PA], r=[KT])
                TT(dve, ROW(BA[0][0:64, 0:256]), PA[0:64, 0:256], TMP2[0:64, 0:256], ALU.mult, w=[BA[0]], r=[PA, TMP2])
                for h in range(4):
                    TPo(PA[0:64, h * 64:(h + 1) * 64], BA[0][0:64, h * 64:(h + 1) * 64], ident(64), w=[PA], r=[BA[0], CST])
                CPA(ROW(BTA[0][0:64, 0:256]), PA[0:64, 0:256], w=[BTA[0]], r=[PA])
                TT(dve, ROW(v3(RR[0][0:64, 0:256], 4)), v3(BA[0][0:64, 0:256], 4), bm(ident(64), 4), ALU.add, w=[RR[0]], r=[BA[0], CST])
                cur = 0
                for lev in range(5):
                    nx = 1 - cur
                    for h in range(4):
                        sl = slice(h * 64, (h + 1) * 64)
                        MM(PA[0:64, sl], RO(BA[cur][0:64, sl]), RO(BTA[cur][0:64, sl]), True, True, w=[PA], r=[BA[cur], BTA[cur]])
                    CPA(ROW(BTA[nx][0:64, 0:256]), PA[0:64, 0:256], w=[BTA[nx]], r=[PA])
                    if lev < 4:
                        for h in range(4):
                            sl = slice(h * 64, (h + 1) * 64)
                            MM(PD[0:64, sl], RO(BTA[cur][0:64, sl]), RO(BA[cur][0:64, sl]), True, True, w=[PD], r=[BA[cur], BTA[cur]])
                        CPV(ROW(BA[nx][0:64, 0:256]), PD[0:64, 0:256], w=[BA[nx]], r=[PD])
                    for h in range(4):
                        sl = slice(h * 64, (h + 1) * 64)
                        MM(PO[0:64, sl], RO(BTA[nx][0:64, sl]), RO(RR[cur][0:64, sl]), True, True, w=[PO], r=[BTA[nx], RR[cur]])
                    TT(dve, ROW(RR[nx][0:64, 0:256]), PO[0:64, 0:256], RR[cur][0:64, 0:256], ALU.add, w=[RR[nx]], r=[PO, RR[cur]])
                    cur = nx
                    if side:
                        side.pop(0)()
                while side:
                    side.pop(0)()
                T2T = RR[cur]
                for h in range(4):
                    MM(PO[0:64, h * 128:(h + 1) * 128], RO(T2T[0:64, h * 64:(h + 1) * 64]), RO(RH[0:64, h * 128:(h + 1) * 128]),
                       True, True, w=[PO], r=[T2T, RH])
                TT(dve, v3(UW[0:64, 0:512], 4), v3(PO[0:64, 0:512], 4), b3(beta, 128), ALU.mult, w=[UW], r=[PO, SM2])
                UW3 = v3(UW[0:64, 0:512], 4)
                tp4([UW3[:, h, 64:128] for h in range(4)], WT, [UW], ROW)
                qt = lambda h: QT[0:64, h * 64:(h + 1) * 64]
                kt = lambda h: KT[0:64, h * 64:(h + 1) * 64]
                DTm = DT[0:64, 0:256]
                DTtk = DT
            elif m == 1:
                dt_ap = SM2[0:64, 24:28]
                VA3 = ROW(v3(VA[0:64, 0:256], 4))
                TT(dve, VA3, v3(x_ap, 4), b3(dt_ap, 64), ALU.mult, w=[VA], r=[U, SM2])
                Cg = Cm.rearrange("p (g d) -> p g d", g=2).unsqueeze(2).broadcast_to([64, 2, 2, 64])
                Bg = Bm.rearrange("p (g d) -> p g d", g=2).unsqueeze(2).broadcast_to([64, 2, 2, 64])
                eg4 = eg.rearrange("p (g e) -> p g e", g=2).unsqueeze(3).broadcast_to([64, 2, 2, 64])
                ek4 = ek.rearrange("p (g e) -> p g e", g=2).unsqueeze(3).broadcast_to([64, 2, 2, 64])
                TT(dve, TMP[0:64, 0:256].rearrange("p (g e d) -> p g e d", g=2, e=2), Cg, eg4, ALU.mult, w=[TMP], r=[U, SM])
                tp4([TMP[0:64, h * 64:(h + 1) * 64] for h in range(4)], QPT, [TMP], ROW)
                TT(dve, ROW(KP[0:64, 0:256].rearrange("p (g e d) -> p g e d", g=2, e=2)), Bg, ek4, ALU.mult, w=[KP], r=[U, SM])
                qt = lambda h: QT[0:64, (h // 2) * 64:(h // 2 + 1) * 64]
                kt = lambda h: KT[0:64, (h // 2) * 64:(h // 2 + 1) * 64]
                DTm = DT[0:64, 0:256]
                DTtk = DT
            elif m == 2:
                q_ap, fx_ap, v_ap, g_ap = U[0:64, 0:256], U[0:64, 256:512], U[0:64, 512:768], U[0:64, 768:1024]
                lb = LBR[0:64, 0:256]
                SIG(TMP[0:64, 0:256], fx_ap, w=[TMP], r=[U])
                TS(dve, TMP[0:64, 256:512], lb, -1.0, ALU.mult, w=[TMP], r=[LBR], s2=1.0, op1=ALU.add)
                TT(dve, TMP[0:64, 0:256], TMP[0:64, 0:256], TMP[0:64, 256:512], ALU.mult, w=[TMP], r=[TMP])
                TT(dve, TMP[0:64, 0:256], TMP[0:64, 0:256], lb, ALU.add, w=[TMP], r=[TMP, LBR])
                ACT(TMP2[0:64, 0:256], TMP[0:64, 0:256], AF.Ln, w=[TMP2], r=[TMP])
                TS(dve, KN[0:64, 0:256], TMP[0:64, 0:256], -1.0, ALU.mult, w=[KN], r=[TMP], s2=1.0, op1=ALU.add)
                ellc = TMP2[0:64, 0:256]
                MM(PA[0:64, 0:256], (Mk("incl") if sample else C("rel")), ellc, True, True, w=[PA], r=[CST, TMP2])
                MM(PD[0:64, 0:256], Mk("incl"), ellc, True, True, w=[PD], r=[CST, TMP2])
                MM(PO[0:64, 0:256], Mk("gt"), ellc, True, True, w=[PO], r=[CST, TMP2])
                if not sample:
                    for h in range(4):
                        MM(PS[0:64, h:h + 1], TMP2[0:64, h * 64:(h + 1) * 64], C("ones")[:, 0:1], True, True, w=[PS], r=[TMP2, CST])
                    nds = 4
                else:
                    for h in range(4):
                        MM(PS[0:64, h * NS:(h + 1) * NS], TMP2[0:64, h * 64:(h + 1) * 64], C("seqoh"), True, True,
                           w=[PS], r=[TMP2, CST])
                    nds = 4 * NS
                ACT(DS[0:64, 0:nds], PS[0:64, 0:nds], AF.Exp, w=[DS], r=[PS])
                ACT(QN[0:64, 0:256], PA[0:64, 0:256], AF.Exp, w=[QN], r=[PA])
                ACT(LM[0:64, 0:256], PA[0:64, 0:256], AF.Exp, w=[LM], r=[PA], scale=-1.0)
                ACT(Yb[0:64, 0:256], PD[0:64, 0:256], AF.Exp, w=[Yb], r=[PD])
                ACT(TMP[0:64, 0:256], PO[0:64, 0:256], AF.Exp, w=[TMP], r=[PO])
                TT(dve, QN[0:64, 0:256], QN[0:64, 0:256], q_ap, ALU.mult, w=[QN], r=[QN, U])
                TT(dve, LM[0:64, 0:256], LM[0:64, 0:256], KN[0:64, 0:256], ALU.mult, w=[LM], r=[LM, KN])
                TT(dve, Yb[0:64, 0:256], Yb[0:64, 0:256], q_ap, ALU.mult, w=[Yb], r=[Yb, U])
                TT(dve, ROW(KP[0:64, 0:256]), TMP[0:64, 0:256], KN[0:64, 0:256], ALU.mult, w=[KP], r=[TMP, KN])
                tp4([QN[0:64, h * 64:(h + 1) * 64] for h in range(4)], QT, [QN], ROW)
                tp4([LM[0:64, h * 64:(h + 1) * 64] for h in range(4)], KT, [LM], ROW)
                tp4([Yb[0:64, h * 64:(h + 1) * 64] for h in range(4)], QPT, [Yb], ROW)
                CPA(ROW(VA[0:64, 0:256]), v_ap, w=[VA], r=[U])
                qt = lambda h: QT[0:64, h * 64:(h + 1) * 64]
                kt = lambda h: KT[0:64, h * 64:(h + 1) * 64]
                CPV(v3(DT[0:64, 0:256], 4), bm(Mk("incl"), 4), w=[DT], r=[CST])
                DTm = DT[0:64, 0:256]
                DTtk = DT
            else:
                q_ap, k_ap, v_ap, og_ap = U[0:64, 0:256], U[0:64, 256:512], U[0:64, 512:768], U[0:64, 768:1024]
                TPo(PS[0:4, 0:64], U[0:64, 1024:1028], ident(64), w=[PS], r=[U, CST])
                TPo(PS[0:4, 64:128], U[0:64, 1028:1032], ident(64), w=[PS], r=[U, CST])
                TS(pool, KN[0:64, 0:256], k_ap, 0.125, ALU.mult, w=[KN], r=[U])
                tp4([U[0:64, h * 64:(h + 1) * 64] for h in range(4)], QT, [U], ROW)
                tp4([KN[0:64, h * 64:(h + 1) * 64] for h in range(4)], KT, [KN], ROW)
                VA3 = ROW(VA[0:64, 0:264].rearrange("p (h d) -> p h d", h=4))
                CPV(VA3[:, :, 0:64], v3(v_ap, 4), w=[VA], r=[U])
                CPV(VA3[:, :, 64:65], C("ones")[:, 0:1].unsqueeze(1).broadcast_to([64, 4, 1]), w=[VA], r=[CST])
                CPV(VA3[:, :, 65:66], C("zeros")[:, 0:1].unsqueeze(1).broadcast_to([64, 4, 1]), w=[VA], r=[CST])
                TS(dve, GI[0:4, 0:64], PS[0:4, 0:64], PCL[0:4, 50:51], ALU.add, w=[GI], r=[PS, PCL])
                ACT(GF[0:4, 0:64], PS[0:4, 64:128], AF.Exp, w=[GF], r=[PS, NEGF], bias=NEGF[0:4, 0:1], scale=-1.0)
                ACT(GF[0:4, 0:64], GF[0:4, 0:64], AF.Ln, w=[GF], r=[GF], bias=1.0)
                TS(dve, LF[0:4, 0:64], GF[0:4, 0:64], -1.0, ALU.mult, w=[LF], r=[GF])
                if not sample:
                    K.op(dve, lambda e: e.tensor_tensor_scan(out=BC[0:4, 1:65], data0=ONESROW[0:4, 0:64], data1=LF[0:4, 0:64],
                                                             initial=BC[0:4, 0:1], op0=ALU.mult, op1=ALU.add),
                         w=[BC], r=[BC, LF, ONESROW])
                    TT(dve, AA[0:4, 0:64], GI[0:4, 0:64], BC[0:4, 1:65], ALU.subtract, w=[AA], r=[GI, BC])
                    K.op(dve, lambda e: e.tensor_tensor_scan(out=MGX[0:4, 1:65], data0=AA[0:4, 0:64], data1=AA[0:4, 0:64],
                                                             initial=MGX[0:4, 0:1], op0=ALU.max, op1=ALU.max),
                         w=[MGX], r=[MGX, AA])
                    mg = MGX[0:4, 1:65]
                    mprev = MGX[0:4, 0:1].broadcast_to([4, 64])
                    mend = MGX[0:4, 64:65].broadcast_to([4, 64])
                    TT(dve, EXS[0:4, 0:64], mprev, mg, ALU.subtract, w=[EXS], r=[MGX])
                    TT(dve, EXK[0:4, 0:64], AA[0:4, 0:64], mend, ALU.subtract, w=[EXK], r=[AA, MGX])
                    TT(dve, EXM[0:4, 0:64], BC[0:4, 1:65], mg, ALU.add, w=[EXM], r=[BC, MGX])
                    TS(dve, NMG[0:4, 0:64], mg, -1.0, ALU.mult, w=[NMG], r=[MGX])
                    TT(dve, DG[0:4, 8:9], MGX[0:4, 0:1], MGX[0:4, 64:65], ALU.subtract, w=[DG], r=[MGX])
                    TS(dve, DG[0:4, 0:4], C("I4", 4), DG[0:4, 8:9], ALU.mult, w=[DG], r=[DG, CST])
                    ndg = 4
                    if c == NCH - 2:
                        CPV(MFIN[0:4, 0:1], EXM[0:4, 63:64], w=[MFIN], r=[EXM])
                        K.dma(sp, p_m_o[l], MFIN[0:4, 0:1], r=[MFIN])
                else:
                    L3 = LF[0:4, 0:64].rearrange("p (s j) -> p s j", j=4)
                    B3 = BC[0:4, 0:64].rearrange("p (s j) -> p s j", j=4)
                    A3 = AA[0:4, 0:64].rearrange("p (s j) -> p s j", j=4)
                    G3 = GI[0:4, 0:64].rearrange("p (s j) -> p s j", j=4)
                    M3 = MGX[0:4, 0:64].rearrange("p (s j) -> p s j", j=4)
                    CPV(B3[:, :, 0:1], L3[:, :, 0:1], w=[BC], r=[LF])
                    for j in range(1, 4):
                        TT(dve, B3[:, :, j:j + 1], B3[:, :, j - 1:j], L3[:, :, j:j + 1], ALU.add, w=[BC], r=[BC, LF])
                    TT(dve, AA[0:4, 0:64], GI[0:4, 0:64], BC[0:4, 0:64], ALU.subtract, w=[AA], r=[GI, BC])
                    TT(dve, M3[:, :, 0:1], A3[:, :, 0:1], MINI[0:4, 0:NS].unsqueeze(2), ALU.max, w=[MGX], r=[AA, MINI])
                    for j in range(1, 4):
                        TT(dve, M3[:, :, j:j + 1], M3[:, :, j - 1:j], A3[:, :, j:j + 1], ALU.max, w=[MGX], r=[MGX, AA])
                    mg = MGX[0:4, 0:64]
                    mprev3 = MINI[0:4, 0:NS].unsqueeze(2).broadcast_to([4, NS, 4])
                    mend3 = M3[:, :, 3:4].broadcast_to([4, NS, 4])
                    TT(dve, EXS[0:4, 0:64].rearrange("p (s j) -> p s j", j=4), mprev3, M3, ALU.subtract, w=[EXS], r=[MGX, MINI])
                    TT(dve, EXK[0:4, 0:64].rearrange("p (s j) -> p s j", j=4), A3, mend3, ALU.subtract, w=[EXK], r=[AA, MGX])
                    TT(dve, EXM[0:4, 0:64], BC[0:4, 0:64], mg, ALU.add, w=[EXM], r=[BC, MGX])
                    TS(dve, NMG[0:4, 0:64], mg, -1.0, ALU.mult, w=[NMG], r=[MGX])
                    TT(dve, DG[0:4, 0:NS], MINI[0:4, 0:NS], M3[:, :, 3], ALU.subtract, w=[DG], r=[MINI, MGX])
                    TT(dve, AM[0:4, 0:4 * NS].rearrange("p (h s) -> p h s", h=4), bm(DG[0:4, 0:NS], 4),
                       b3(C("I4", 4), NS), ALU.mult, w=[AM], r=[DG, CST])
                    ndg = 4 * NS
                    CPV(MFIN[0:4, 0:NS], EXM[0:4, 0:64].rearrange("p (s j) -> p s j", j=4)[:, :, 3], w=[MFIN], r=[EXM])
                    K.dma(sp, o_m_o[l], MFIN[0:4, 0:NS], r=[MFIN])
                MM(PS[0:64, 128:132], EXS[0:4, 0:64], C("I4", 4), True, True, w=[PS], r=[EXS, CST])
                MM(PS[0:64, 132:136], EXK[0:4, 0:64], C("I4", 4), True, True, w=[PS], r=[EXK, CST])
                MM(PS[0:64, 136:140], EXM[0:4, 0:64], C("I4", 4), True, True, w=[PS], r=[EXM, CST])
                ACT(SM[0:64, 0:8], PS[0:64, 128:136], AF.Exp, w=[SM], r=[PS])
                ACT(SM[0:64, 8:12], PS[0:64, 136:140], AF.Exp, w=[SM], r=[PS], scale=-1.0)
                sinit, kw_, emm = SM[0:64, 0:4], SM[0:64, 4:8], SM[0:64, 8:12]
                if not sample:
                    MM(PS[0:64, 160:164], C("ones", 4), DG[0:4, 0:4], True, True, w=[PS], r=[CST, DG])
                else:
                    MM(PS[0:64, 160:160 + ndg], C("ones", 4), AM[0:4, 0:ndg], True, True, w=[PS], r=[CST, AM])
                ACT(DS[0:64, 0:ndg], PS[0:64, 160:160 + ndg], AF.Exp, w=[DS], r=[PS])
                for h in range(4):
                    TS(dve, TMP2[0:4, h * 64:(h + 1) * 64], AA[0:4, 0:64], C("I4", 4)[:, h:h + 1], ALU.mult, w=[TMP2], r=[AA, CST])
                MM(PD[0:64, 0:256], ident(64), Mk("neg4"), True, False, w=[PD], r=[CST])
                for h in range(4):
                    sl = slice(h * 64, (h + 1) * 64)
                    MM(PD[0:64, sl], TMP2[0:4, sl], C("ones", 4), False, False, w=[PD], r=[TMP2, CST])
                    MM(PD[0:64, sl], C("oh4", 4)[:, sl], NMG[0:4, 0:64], False, h == 3,
                       w=[PD], r=[NMG, CST])
                ACT(DT[0:64, 0:256], PD[0:64, 0:256], AF.Exp, w=[DT], r=[PD])
                TT(dve, v3(TMP[0:64, 0:256], 4), v3(q_ap, 4), b3(sinit, 64), ALU.mult, w=[TMP], r=[U, SM])
                tp4([TMP[0:64, h * 64:(h + 1) * 64] for h in range(4)], QPT, [TMP], ROW)
                TT(dve, ROW(v3(KP[0:64, 0:256], 4)), v3(KN[0:64, 0:256], 4), b3(kw_, 64), ALU.mult, w=[KP], r=[KN, SM])
                qt = lambda h: QT[0:64, h * 64:(h + 1) * 64]
                kt = lambda h: KT[0:64, h * 64:(h + 1) * 64]
                DTm = DT[0:64, 0:256]
                DTtk = DT
                if not sample:
                    CPV(MGX[0:4, 0:1], MGX[0:4, 64:65], w=[MGX], r=[MGX])
                    CPV(BC[0:4, 0:1], BC[0:4, 64:65], w=[BC], r=[BC])

            for h in range(4):
                MM(PA[0:64, h * 64:(h + 1) * 64], RO(kt(h)), RO(qt(h)), True, True, w=[PA], r=[KT, QT])
            TT(dve, ROW(ATT[0:64, 0:256]), PA[0:64, 0:256], DTm, ALU.mult, w=[ATT], r=[PA, DTtk])

            nseq = NS if sample else 1
            hw = 4 * dvw
            ssm_in = s_ssm_d[m]
            ssm_out = (o_ssm_o if sample else p_ssm_o)[m]

            def load_state(s):
                St = SSs[s % 2]
                S3 = St[0:64, 0:hw].rearrange("p (h d) -> p h d", h=4)
                K.dma(sp, S3[:, :, 0:64], ssm_in[l, s].rearrange("h k v -> k h v"), w=[St])
                if m == 3:
                    K.dma(sp, SNV[0:4, 0:64], s_nvec_d[l, 4 * s:4 * s + 4, :], w=[SNV])
                    TPo(PX[0:64, 256:260], SNV[0:4, 0:64], ident(4), w=[PX], r=[SNV, CST])
                    CPA(S3[:, :, 64], PX[0:64, 256:260], w=[St], r=[PX])
                    CPV(S3[:, :, 65:66], C("zeros")[:, 0:1].unsqueeze(1).broadcast_to([64, 4, 1]), w=[St], r=[CST])
                return St

            def store_state(s, St):
                S3 = St[0:64, 0:hw].rearrange("p (h d) -> p h d", h=4)
                if sample:
                    K.dma(sp, ssm_out[l, s].rearrange("h k v -> k h v"), S3[:, :, 0:64], r=[St])
                else:
                    K.dma(sp, ssm_out[l].rearrange("h k v -> k h v"), S3[:, :, 0:64], r=[St])
                if m == 3:
                    CPV(TMP2[0:64, 0:4], S3[:, :, 64], w=[TMP2], r=[St])
                    TPo(PX[0:4, 264:328], TMP2[0:64, 0:4], ident(64), w=[PX], r=[TMP2, CST])
                    CPA(SNV[0:4, 0:64], PX[0:4, 264:328], w=[SNV], r=[PX])
                    if sample:
                        K.dma(sp, o_nvec_o[l, 4 * s:4 * s + 4, :], SNV[0:4, 0:64], r=[SNV])
                    else:
                        K.dma(sp, p_nvec_o[l], SNV[0:4, 0:64], r=[SNV])

            def valh(h):
                if m == 0:
                    return RO(VA[0:64, h * 64:(h + 1) * 64])
                if m == 1:
                    return RO(VA[0:64, h * 64:(h + 1) * 64])
                if m == 2:
                    return RO(VA[0:64, h * 64:(h + 1) * 64])
                return RO(VA[0:64, h * 66:(h + 1) * 66])
            valtk = VA

            if not sample:
                St = SPs
                S3 = St[0:64, 0:hw].rearrange("p (h d) -> p h d", h=4)
                if m == 0:
                    for h in range(4):
                        MM(PO[0:64, h * 64:(h + 1) * 64], WT[0:64, h * 64:(h + 1) * 64], S3[:, h, 0:64], True, True,
                           w=[PO], r=[WT, St])
                    TT(dve, ROW(v3(VA[0:64, 0:256], 4)), UW3[:, :, 0:64], v3(PO[0:64, 0:256], 4), ALU.subtract, w=[VA], r=[UW, PO])
                for h in range(4):
                    MM(PO[0:64, h * dvw:(h + 1) * dvw], RO(ATT[0:64, h * 64:(h + 1) * 64]), valh(h), True, False,
                       w=[PO], r=[ATT, valtk])
                    MM(PO[0:64, h * dvw:(h + 1) * dvw], QPT[0:64, h * 64:(h + 1) * 64], S3[:, h, :], False, True,
                       w=[PO], r=[QPT, St])
                CPA(Ob[0:64, 0:hw], PO[0:64, 0:hw], w=[Ob], r=[PO])
                for h in range(4):
                    MM(PS[0:64, h * dvw:(h + 1) * dvw], RO(KP[0:64, h * 64:(h + 1) * 64]), valh(h), True, True,
                       w=[PS], r=[KP, valtk])
                TT(dve, S3, S3, b3(DS[0:64, 0:4], dvw), ALU.mult, w=[St], r=[St, DS])
                TT(dve, St[0:64, 0:hw], St[0:64, 0:hw], PS[0:64, 0:hw], ALU.add, w=[St], r=[St, PS])
                if c == NCH - 2:
                    store_state(0, St)
            else:
                colm = C("colmask")
                seqoh = C("seqoh")
                zeros = C("zeros")

                def build_masked(SRC, DST, sg):
                    cm4 = colm[:, sg * 128:(sg + 1) * 128].rearrange("p (s r) -> p s r", s=2)
                    TT(dve, DST[0:64, 0:512].rearrange("p (h s r) -> p h s r", h=4, s=2),
                       v3(SRC[0:64, 0:256], 4).unsqueeze(2).broadcast_to([64, 4, 2, 64]),
                       cm4.unsqueeze(1).broadcast_to([64, 4, 2, 64]), ALU.mult, w=[DST], r=[SRC, CST])

                def upd_state(s, St, S3):
                    vsrc = VA[0:64, 0:hw]
                    TS(dve, VM[0:64, 0:hw], vsrc, seqoh[:, s:s + 1], ALU.mult, w=[VM], r=[valtk, CST])
                    for h in range(4):
                        MM(PS[0:64, h * dvw:(h + 1) * dvw], KP[0:64, h * 64:(h + 1) * 64],
                           VM[0:64, h * dvw:(h + 1) * dvw], True, True, w=[PS], r=[KP, VM])
                    dss = DS[0:64, 0:4 * NS].rearrange("p (h s) -> p h s", h=4)[:, :, s]
                    TT(dve, S3, S3, b3(dss, dvw), ALU.mult, w=[St], r=[St, DS])
                    TT(dve, St[0:64, 0:hw], St[0:64, 0:hw], PS[0:64, 0:hw], ALU.add, w=[St], r=[St, PS])
                    store_state(s, St)

                obase = 0
                if m == 0:
                    MM(PO[0:64, 0:256], ident(64), zeros[:, 0:256], True, False, w=[PO], r=[CST])
                    for sg in range(NS // 2):
                        build_masked(WT, WM_, sg)
                        for s4 in range(2):
                            s = sg * 2 + s4
                            St = load_state(s)
                            S3 = St[0:64, 0:hw].rearrange("p (h d) -> p h d", h=4)
                            for h in range(4):
                                MM(PO[0:64, h * 64:(h + 1) * 64],
                                   WM_[0:64, (h * 2 + s4) * 64:(h * 2 + s4 + 1) * 64], S3[:, h, 0:64],
                                   False, (s == NS - 1 and h == 3), w=[PO], r=[WM_, St])
                    TT(dve, ROW(v3(VA[0:64, 0:256], 4)), UW3[:, :, 0:64], v3(PO[0:64, 0:256], 4), ALU.subtract, w=[VA], r=[UW, PO])
                    obase = 256
                MM(PO[0:64, obase:obase + hw], ident(64), zeros[:, 0:hw], True, False, w=[PO], r=[CST])
                for h in range(4):
                    MM(PO[0:64, obase + h * dvw:obase + (h + 1) * dvw], ATT[0:64, h * 64:(h + 1) * 64], valh(h), False, False,
                       w=[PO], r=[ATT, valtk])
                for sg in range(NS // 2):
                    build_masked(QPT, QM, sg)
                    for s4 in range(2):
                        s = sg * 2 + s4
                        St = load_state(s)
                        S3 = St[0:64, 0:hw].rearrange("p (h d) -> p h d", h=4)
                        for h in range(4):
                            MM(PO[0:64, obase + h * dvw:obase + (h + 1) * dvw],
                               QM[0:64, (h * 2 + s4) * 64:(h * 2 + s4 + 1) * 64], S3[:, h, :],
                               False, (s == NS - 1 and h == 3), w=[PO], r=[QM, St])
                        upd_state(s, St, S3)
                CPA(Ob[0:64, 0:hw], PO[0:64, obase:obase + hw], w=[Ob], r=[PO])

            yield 'RM'
            O3 = Ob[0:64, 0:hw].rearrange("p (h d) -> p h d", h=4)
            gname = ("norm_a_g", "norm_b_g", "norm_c_g", "norm_d_g")[m]
            gsrc = {0: None, 1: None, 2: None, 3: None}
            if m in (0, 1):
                SIG(TMP2[0:64, 0:256], z_ap, w=[TMP2], r=[U])
                TT(pool, TMP2[0:64, 0:256], TMP2[0:64, 0:256], z_ap, ALU.mult, w=[TMP2], r=[TMP2, U])
            elif m == 2:
                SIG(TMP2[0:64, 0:256], g_ap, w=[TMP2], r=[U])
            else:
                SIG(TMP2[0:64, 0:256], og_ap, w=[TMP2], r=[U])
            if m != 1:
                TT(pool, TMP2[0:64, 0:256], TMP2[0:64, 0:256], prow(gname), ALU.mult, w=[TMP2], r=[TMP2, PRW])
            yield 'step'
            if m == 3:
                ACT(SM2[0:64, 32:36], O3[:, :, 64], AF.Abs, w=[SM2], r=[Ob])
                TT(dve, SM2[0:64, 32:36], SM2[0:64, 32:36], emm, ALU.max, w=[SM2], r=[SM2, SM])
                yield 'step'
                RECIP(SM2[0:64, 36:40], SM2[0:64, 32:36], w=[SM2], r=[SM2])
                TT(dve, v3(TMP[0:64, 0:256], 4), O3[:, :, 0:64], b3(SM2[0:64, 36:40], 64), ALU.mult, w=[TMP], r=[Ob, SM2])
                yield 'step'
                osrc, otk = TMP[0:64, 0:256], TMP
            elif m == 1:
                TT(dve, v3(TMP[0:64, 0:256], 4), v3(x_ap, 4), b3(prow("d_skip_b"), 64), ALU.mult, w=[TMP], r=[U, PRW])
                TT(dve, TMP[0:64, 0:256], TMP[0:64, 0:256], Ob[0:64, 0:256], ALU.add, w=[TMP], r=[TMP, Ob])
                yield 'step'
                TT(dve, TMP[0:64, 0:256], TMP[0:64, 0:256], TMP2[0:64, 0:256], ALU.mult, w=[TMP], r=[TMP, TMP2])
                osrc, otk = TMP[0:64, 0:256], TMP
            else:
                osrc, otk = Ob[0:64, 0:256], Ob
            ng = 2 if m == 1 else 4
            gw = 256 // ng
            TT(dve, TMP[0:64, 256:512], osrc, osrc, ALU.mult, w=[TMP], r=[otk, TMP])
            yield 'step'
            K.op(dve, lambda e: e.tensor_reduce(out=SM2[0:64, 40:40 + ng], in_=v3(TMP[0:64, 256:512], ng), axis=AX.X, op=ALU.add),
                 w=[SM2], r=[TMP])
            TS(dve, SM2[0:64, 40:40 + ng], SM2[0:64, 40:40 + ng], 1.0 / gw, ALU.mult, w=[SM2], r=[SM2], s2=EPS, op1=ALU.add)
            yield 'step'
            RSQ(SM2[0:64, 44:44 + ng], SM2[0:64, 40:40 + ng], w=[SM2], r=[SM2])
            yield 'step'
            TT(dve, v3(Yb[0:64, 0:256], ng), v3(osrc, ng), b3(SM2[0:64, 44:44 + ng], gw), ALU.mult, w=[Yb], r=[otk, SM2])
            if m == 1:
                TT(dve, Yb[0:64, 0:256], Yb[0:64, 0:256], prow(gname), ALU.mult, w=[Yb], r=[Yb, PRW])
            else:
                TT(dve, Yb[0:64, 0:256], Yb[0:64, 0:256], TMP2[0:64, 0:256], ALU.mult, w=[Yb], r=[Yb, TMP2])
            yield 'step'
            for b in range(2):
                TPo(PD[0:128, b * 64:(b + 1) * 64], Yb[0:64, b * 128:(b + 1) * 128], ident(64), w=[PD], r=[Yb, CST])
            PW = (PA, PO)
            yield 'step'
            if c == 0 and FP32_MIX:
                CPA(YT32[:, 0:128], PD[0:128, 0:128], w=[YT32], r=[PD])
                for hf in range(2):
                    for b in range(2):
                        MM(PW[hf][pb:pb + 64, 0:512], YT32[:, b * 64:(b + 1) * 64], WST[b][:, hf * 512:(hf + 1) * 512],
                           b == 0, b == 1, w=[PW[hf]], r=[YT32, WST[b]])
            else:
                CPA(YT[:, 0:128], PD[0:128, 0:128], w=[YT], r=[PD])
                yield 'step'
                for hf in range(2):
                    for b in range(2):
                        MM(PW[hf][pb:pb + 64, 0:512], YT[:, b * 64:(b + 1) * 64], Wo3[:, b, hf * 512:(hf + 1) * 512],
                           b == 0, b == 1, w=[PW[hf]], r=[YT, Wo])
            for hf in range(2):
                TT(dve, Rt[i_tile][pb:pb + 64, hf * 512:(hf + 1) * 512], Rt[i_tile][pb:pb + 64, hf * 512:(hf + 1) * 512],
                   PW[hf][pb:pb + 64, 0:512], ALU.add, w=[Rt[i_tile]], r=[Rt[i_tile], PW[hf]])

        gens = [chunk_body(c) for c in range(NCH)]

        def run_until(g, marker):
            for v in g:
                if v == marker:
                    return

        def step(g, end):
            try:
                v = next(g)
            except StopIteration:
                return False
            return v != end

        if PIPE:
            run_until(gens[0], 'P2')
            for c in range(NCH):
                run_until(gens[c], 'RM')
                if c + 1 < NCH:
                    run_until(gens[c + 1], 'P1')
                    if PIPE2 and c + 1 != 1 and c + 1 != NCH - 1:
                        la = lb = True
                        while la or lb:
                            if la:
                                la = step(gens[c], None)
                            if lb:
                                lb = step(gens[c + 1], 'P2')
                    else:
                        run_until(gens[c], None)
                        run_until(gens[c + 1], 'P2')
                else:
                    run_until(gens[c], None)
        else:
            for c in range(NCH):
                run_until(gens[c], None)

    aoff[0] = 1024
    Wu = [ar(8 * 512 // 2, BF16)] * 2
    Wd = [ar(4 * 1024 // 2, BF16)] * 2
    RL = ar(512)
    ACB = ar(4 * 512 // 2, BF16)
    WU32 = ar(4096)
    WD32 = ar(4096)
    A32 = ar(256)
    RL0 = ar(64)
    assert aoff[0] <= ARENA_COLS

    def ffn(l):
        PUP = [PX, PD, PA, PO]
        dsel = [0]
        ngr = (T + 511) // 512
        groups = [(g * 512, min(512, T - g * 512)) for g in range(ngr)] + [(T, NSK)]
        XF3 = v3(XF[:, 0:512], 8)
        for fc in range(8):
            wu, wd = Wu[fc % 2], Wd[fc % 2]
            wu3 = wu[:, 0:4096].rearrange("p (k c) -> p k c", k=8)
            wd3 = wd[:, 0:4096].rearrange("p (k c) -> p k c", k=4)
            wu32 = v3(WU32[:, 0:4096], 8)
            wd32 = v3(WD32[:, 0:4096], 4)
            for k in range(8):
                if not NOSTG:
                    K.dma(stq, wu32[:, k, :], w_up[l, k * 128:(k + 1) * 128, fc * 512:(fc + 1) * 512], w=[WU32])
            for j in range(4):
                if not NOSTG:
                    K.dma(stq, wd32[:, j, :], w_down[l, fc * 512 + j * 128:fc * 512 + (j + 1) * 128, :], w=[WD32])
            if STAGE:
                K.op(pool, lambda e: e.tensor_copy(out=wu[:, 0:4096], in_=WU32[:, 0:4096]), w=[wu], r=[WU32])
                K.op(act, lambda e: e.copy(out=wd[:, 0:4096], in_=WD32[:, 0:4096]), w=[wd], r=[WD32])
            else:
                for k in range(8):
                    K.dma(pool, wu3[:, k, :], w_up[l, k * 128:(k + 1) * 128, fc * 512:(fc + 1) * 512], w=[wu])
                for j in range(4):
                    K.dma(pool, wd3[:, j, :], w_down[l, fc * 512 + j * 128:fc * 512 + (j + 1) * 128, :], w=[wd])
            a32 = v3(A32[:, 0:256], 4)
            for j in range(4 if FP32_FFN else 0):
                for k in range(8):
                    MM(PS[0:128, j * 64:(j + 1) * 64], wu32[:, k, j * 128:(j + 1) * 128], XF3[:, k, :], k == 0, k == 7,
                       w=[PS], r=[WU32, XF])
                ACT(RL0[:, 0:64], PS[0:128, j * 64:(j + 1) * 64], AF.Relu, w=[RL0], r=[PS])
                TT(dve, a32[:, j, :], RL0[:, 0:64], RL0[:, 0:64], ALU.mult, w=[A32], r=[RL0])
            for hf in range(2 if FP32_FFN else 0):
                for j in range(4):
                    MM(PJ[0:64, hf * 512:(hf + 1) * 512], a32[:, j, :], wd32[:, j, hf * 512:(hf + 1) * 512], j == 0, j == 3,
                       w=[PJ], r=[A32, WD32])
            if FP32_FFN:
                TT(dve, Rt[0][0:64, :], Rt[0][0:64, :], PJ[0:64, 0:1024], ALU.add, w=[Rt[0]], r=[Rt[0], PJ])
            ac3 = ACB[:, 0:2048].rearrange("p (j n) -> p j n", j=4)
            for (g0, n) in groups:
                cks = [ATc[cc] for cc in range(g0 // 64, (g0 + n) // 64)]
                for j in range(4):
                    for k in range(8):
                        MM(PUP[j][:, 0:n], wu3[:, k, j * 128:(j + 1) * 128], AT_all[:, k, g0:g0 + n], k == 0, k == 7,
                           w=[PUP[j]], r=[wu] + cks)
                    ACT(RL[:, 0:n], PUP[j][:, 0:n], AF.Relu, w=[RL], r=[PUP[j]])
                    TT(pool, ac3[:, j, 0:n], RL[:, 0:n], RL[:, 0:n], ALU.mult, w=[ACB], r=[RL])
                for t0 in range(0, n, 128):
                    P = min(128, n - t0)
                    i = (g0 + t0) // 128
                    r0 = 64 if (i == 0 and FP32_FFN) else 0
                    dsel[0] += 1
                    for hf in range(2):
                        pw = (PJ, PJ) if dsel[0] % 2 == 0 else (PDUM, PS)
                        po = pw[hf][r0:P, hf * 512:(hf + 1) * 512] if pw[hf] is PJ else pw[hf][r0:P, 0:512]
                        for j in range(4):
                            MM(po, ac3[:, j, t0 + r0:t0 + P], wd3[:, j, hf * 512:(hf + 1) * 512],
                               j == 0, j == 3, w=[pw[hf]], r=[ACB, wd])
                        TT(dve, Rt[i][r0:P, hf * 512:(hf + 1) * 512], Rt[i][r0:P, hf * 512:(hf + 1) * 512], po, ALU.add,
                           w=[Rt[i]], r=[Rt[i], pw[hf]])

    for l in range(L):
        K.barrier()
        K.dma(sp, PRW[:, :], prow_d[l], w=[PRW])
        K.dma(sp, PCL[:, :], pcol_d[l], w=[PCL])
        ACT(NEA[0:64, 0:4], prow("a_log_a"), AF.Exp, w=[NEA], r=[PRW])
        ACT(NEA[0:64, 4:8], prow("a_log_b"), AF.Exp, w=[NEA], r=[PRW])
        TS(dve, NEA[0:64, 0:8], NEA[0:64, 0:8], -1.0, ALU.mult, w=[NEA], r=[NEA])
        l0, l1 = prow("lbl0"), prow("lbl1")
        TT(dve, TMP[0:64, 0:256], l0, l1, ALU.max, w=[TMP], r=[PRW])
        TT(dve, TMP[0:64, 256:512], l0, TMP[0:64, 0:256], ALU.subtract, w=[TMP], r=[PRW, TMP])
        TT(dve, TMP2[0:64, 0:256], l1, TMP[0:64, 0:256], ALU.subtract, w=[TMP2], r=[PRW, TMP])
        ACT(TMP[0:64, 256:512], TMP[0:64, 256:512], AF.Exp, w=[TMP], r=[TMP])
        ACT(TMP2[0:64, 0:256], TMP2[0:64, 0:256], AF.Exp, w=[TMP2], r=[TMP2])
        TT(dve, TMP[0:64, 0:256], TMP[0:64, 256:512], TMP2[0:64, 0:256], ALU.add, w=[TMP], r=[TMP, TMP2])
        RECIP(TMP[0:64, 0:256], TMP[0:64, 0:256], w=[TMP], r=[TMP])
        TT(dve, TMP[0:64, 256:512], TMP[0:64, 256:512], TMP[0:64, 0:256], ALU.mult, w=[TMP], r=[TMP])
        TT(dve, TMP2[0:64, 0:256], TMP2[0:64, 0:256], TMP[0:64, 0:256], ALU.mult, w=[TMP2], r=[TMP2, TMP])
        if l == 0:
            TT(dve, LBR[0:64, 0:256], TMP[0:64, 256:512], TMP[0:64, 256:512], ALU.subtract, w=[LBR], r=[TMP])
        else:
            TT(dve, LBR[0:64, 0:256], TMP[0:64, 256:512], TMP2[0:64, 0:256], ALU.add, w=[LBR], r=[TMP, TMP2])
            TT(dve, LBR[0:64, 0:256], LBR[0:64, 0:256], TMP[0:64, 256:512], ALU.subtract, w=[LBR], r=[LBR, TMP])
        dstate["on"] = True
        for m in [int(ch) for ch in '0123']:
            mixer_pass(l, m)
        dstate["on"] = False
        K.barrier()
        K.dma(sp, LNR[:, :], lnrow_d[1 + 2 * l], w=[LNR])
        for i in range(NTILES):
            P = tile_P(i)
            XT = XTs[i % 2]
            ln_tile(Rt[i], Rt[i][0:P, :], P, XT, XT[0:P, :], i % 2)
            to_AT(XT, XT, i, P)
            K.op(act, lambda e: e.mul(out=Rt[i][0:P, :], in_=XT[0:P, :], mul=ALPHA), w=[Rt[i]], r=[XT])
        K.barrier()
        ffn(l)
        K.barrier()
        K.dma(sp, LNR[:, :], lnrow_d[2 + 2 * l], w=[LNR])
        for i in range(NTILES):
            P = tile_P(i)
            XT = XTs[i % 2]
            ln_tile(Rt[i], Rt[i][0:P, :], P, XT, XT[0:P, :], i % 2)
            if l == L - 1:
                K.dma(sp, y_d[128 * i:128 * i + P, :], XT[0:P, :], r=[XT])
            else:
                to_AT(XT, XT, i, P)
                K.op(act, lambda e: e.mul(out=Rt[i][0:P, :], in_=XT[0:P, :], mul=ALPHA), w=[Rt[i]], r=[XT])
    K.finish()
    return nc, cst_np


def _rep(row, n=128):
    return np.ascontiguousarray(np.broadcast_to(np.asarray(row, np.float32)[None, :], (n, row.shape[-1])))


def make_in_maps(inp, T, NS, L, ncores, cst_np):
    f = lambda a: np.ascontiguousarray(np.asarray(a, np.float32))
    prow = np.zeros((L, 128, NPROW), np.float32)
    pcol = np.zeros((L, 128, NPCOL), np.float32)
    lnrow = np.zeros((2 * L + 1, 128, 2048), np.float32)
    lnrow[0] = _rep(np.concatenate([f(inp["emb_ln_g"]), f(inp["emb_ln_b"])]))
    lbl = f(inp["lb_logits_c"])
    for l in range(L):
        lnrow[1 + 2 * l] = _rep(np.concatenate([f(inp["ln1_g"])[l], f(inp["ln1_b"])[l]]))
        lnrow[2 + 2 * l] = _rep(np.concatenate([f(inp["ln2_g"])[l], f(inp["ln2_b"])[l]]))
        for name in ("norm_a_g", "norm_b_g", "norm_c_g", "norm_d_g", "a_log_a", "dt_bias_a", "a_log_b",
                     "dt_bias_b", "d_skip_b"):
            o, n = PROW[name]
            prow[l, :, o:o + n] = f(inp[name])[l][None, :]
        prow[l, :, PROW["lbl0"][0]:PROW["lbl0"][0] + 256] = lbl[0][None, :]
        prow[l, :, PROW["lbl1"][0]:PROW["lbl1"][0] + 256] = lbl[1][None, :]
        caw = f(inp["conv_a_w"])[l]
        pcol[l, :, 0:24] = caw.reshape(4, 6, 128).transpose(2, 1, 0).reshape(128, 24)
        pcol[l, :, 24:30] = f(inp["conv_a_b"])[l].reshape(6, 128).T
        cbw = f(inp["conv_b_w"])[l]
        pcol[l, :, 30:46] = cbw.reshape(4, 4, 128).transpose(2, 1, 0).reshape(128, 16)
        pcol[l, :, 46:50] = f(inp["conv_b_b"])[l].reshape(4, 128).T
        pcol[l, 0:4, 50] = f(inp["i_bias_d"])[l]
        pcol[l, 0:4, 51] = f(inp["f_bias_d"])[l]
    maps = []
    for c in range(ncores):
        sl = slice(NS * c, NS * (c + 1))
        m = {
            "xin": np.concatenate([f(inp["x_prompt"])[c], f(inp["x_sample"])[sl].reshape(NS * 4, 1024)], axis=0),
            "w_in": f(inp["w_in"]), "w_out": f(inp["w_out"]), "w_up": f(inp["w_up"]), "w_down": f(inp["w_down"]),
            "prow": prow, "lnrow": lnrow, "pcol": pcol, "cst": cst_np,
            "sa_conv": f(f(inp["state_a_conv"])[:, sl].reshape(L, NS * 3, 768)),
            "sb_conv": f(f(inp["state_b_conv"])[:, sl].reshape(L, NS * 3, 512)),
            "sa_ssm": f(f(inp["state_a_ssm"])[:, sl]), "sb_ssm": f(f(inp["state_b_ssm"])[:, sl]),
            "sc_ssm": f(f(inp["state_c_ssm"])[:, sl]), "sd_cmem": f(f(inp["state_d_cmem"])[:, sl]),
            "sd_nvec": f(f(inp["state_d_nvec"])[:, sl].reshape(L, NS * 4, 64)),
            "sd_m": f(f(inp["state_d_mstab"])[:, sl].transpose(0, 2, 1)),
        }
        maps.append(m)
    return maps


def gather(results, T, NS, L, ncores):
    R = results
    cat = lambda name: np.stack([np.asarray(R[c][name]) for c in range(ncores)], axis=1)
    y = [np.asarray(R[c]["y"]) for c in range(ncores)]
    y_prompt = np.stack([yy[:T] for yy in y], axis=0)
    y_sample = np.concatenate([yy[T:].reshape(NS, 4, 1024) for yy in y], axis=0)
    pa_conv = cat("pa_conv")
    pb_conv = cat("pb_conv")
    pss = [cat(n) for n in ("pa_ssm", "pb_ssm", "pc_ssm", "pd_cmem")]
    pd_nvec = cat("pd_nvec")
    pd_m = cat("pd_m")[..., 0]
    catS = lambda name, shp: np.concatenate([np.asarray(R[c][name]).reshape((L, NS) + shp) for c in range(ncores)], axis=1)
    oa_conv = catS("oa_conv", (3, 768))
    ob_conv = catS("ob_conv", (3, 512))
    oss = [catS(n, (4, 64, 64)) for n in ("oa_ssm", "ob_ssm", "oc_ssm", "od_cmem")]
    od_nvec = catS("od_nvec", (4, 64))
    od_m = np.concatenate([np.asarray(R[c]["od_m"]).transpose(0, 2, 1) for c in range(ncores)], axis=1)
    outs = (y_prompt, y_sample, pa_conv, pss[0], pb_conv, pss[1], pss[2], pss[3], pd_nvec, pd_m,
            oa_conv, oss[0], ob_conv, oss[1], oss[2], oss[3], od_nvec, od_m)
    return tuple(np.ascontiguousarray(o.astype(np.float32)) for o in outs)


def kernel(**inputs):
    T, NS, L, ncores = 2048, 16, 2, 8
    nc = bass.Bass("TRN2", target_bir_lowering=False)
    nc, cst_np = build(nc, T, NS, L)
    maps = make_in_maps(inputs, T, NS, L, ncores, cst_np)
    res = run_bass_kernel_spmd(nc, maps, core_ids=list(range(ncores)))
    return gather(res.results, T, NS, L, ncores)
```
